# Optimizing a Trainium2 kernel written in Bass

```python
import math
import jax
import jax.numpy as jnp
from jax import lax
import numpy as np

D_MODEL = 2048
BATCH = 16
SEQ = 2048
DEPTH = 2

NSA_HEADS = 8
NSA_KV_GROUPS = 2
NSA_HPG = NSA_HEADS // NSA_KV_GROUPS
NSA_HEAD_DIM = 128
NSA_WIDTH = NSA_HEADS * NSA_HEAD_DIM
NSA_KV_WIDTH = NSA_KV_GROUPS * NSA_HEAD_DIM
CMP_BLOCK = 32
CMP_STRIDE = 16
CMP_HIDDEN = 256
SEL_BLOCK = 64
SEL_TOP_N = 16
SEL_FORCE = 1000.0
WINDOW = 512
NSA_QBLOCK = 32

GDN_HEADS = 8
GDN_HEAD_DIM = 128
GDN_WIDTH = GDN_HEADS * GDN_HEAD_DIM
CONV_WIDTH = 4
GDN_CHUNK = 64

REL_BUCKETS = 32
REL_MAX_DIST = 128

D_FF = 4 * D_MODEL
NORM_EPS = 1e-6

IN_SPLIT_SIZES = (NSA_WIDTH, 6 * NSA_KV_WIDTH, 3 * NSA_HEADS, 3 * GDN_WIDTH, GDN_WIDTH, GDN_HEADS, GDN_HEADS, 2 * D_MODEL)
IN_COLS = NSA_WIDTH + 6 * NSA_KV_WIDTH + 3 * NSA_HEADS + 3 * GDN_WIDTH + GDN_WIDTH + GDN_HEADS + GDN_HEADS + 2 * D_MODEL

kernel_name = 'hybrid_nsa_gdn_block'


def rms_norm(x, w):
    xf = x.astype(jnp.float32)
    y = xf * lax.rsqrt(jnp.mean(xf * xf, axis=-1, keepdims=True) + NORM_EPS)
    return (y * w.astype(jnp.float32)).astype(x.dtype)


def masked_softmax(s, valid):
    s = jnp.where(valid, s.astype(jnp.float32), -jnp.inf)
    m = jnp.max(s, axis=-1, keepdims=True)
    m = jnp.where(jnp.isfinite(m), m, 0.0)
    e = jnp.exp(s - m)
    return e / jnp.maximum(jnp.sum(e, axis=-1, keepdims=True), 1e-30)


def t5_bucket(dist):
    n = jnp.maximum(dist, 0)
    exact = REL_BUCKETS // 2
    nf = jnp.maximum(n, 1).astype(jnp.float32)
    large = exact + (jnp.log(nf / exact) / math.log(REL_MAX_DIST / exact) * (REL_BUCKETS - exact)).astype(jnp.int32)
    large = jnp.minimum(large, REL_BUCKETS - 1)
    return jnp.where(n < exact, n, large)


def rel_bias_grid(rel_table, dist):
    tq, tk = dist.shape
    b = rel_table[t5_bucket(dist)]
    return b.reshape(tq, tk, NSA_KV_GROUPS, NSA_HPG).transpose(2, 3, 0, 1)


def compress_tokens(kv, pe, w1, w2):
    b, s, g, dh = kv.shape
    n_sub = CMP_BLOCK // CMP_STRIDE
    n_cmp = s // CMP_STRIDE - n_sub + 1
    r = kv.reshape(b, s // CMP_STRIDE, CMP_STRIDE, g, dh)
    blocks = jnp.concatenate([r[:, j:j + n_cmp] for j in range(n_sub)], axis=2)
    blocks = blocks + pe[None, None, :, None, :]
    flat = blocks.transpose(0, 1, 3, 2, 4).reshape(b, n_cmp, g, CMP_BLOCK * dh)
    return jax.nn.gelu(flat @ w1) @ w2


def nsa_attention(q, kv, gate_logits, rel_table, pe_k, pe_v, w1_k, w2_k, w1_v, w2_v):
    b, s, _ = q.shape
    g_, j_, dh = NSA_KV_GROUPS, NSA_HPG, NSA_HEAD_DIM
    scale = dh ** -0.5
    q = q.reshape(b, s, g_, j_, dh)
    k_c, v_c, k_s, v_s, k_w, v_w = [t.reshape(b, s, g_, dh) for t in jnp.split(kv, 6, axis=-1)]
    kc = compress_tokens(k_c, pe_k, w1_k, w2_k)
    vc = compress_tokens(v_c, pe_v, w1_v, w2_v)
    n_cmp = kc.shape[1]
    n_sel = s // SEL_BLOCK
    top_n = min(SEL_TOP_N, n_sel)
    ks = k_s.reshape(b, n_sel, SEL_BLOCK, g_, dh).transpose(0, 3, 1, 2, 4)
    vs = v_s.reshape(b, n_sel, SEL_BLOCK, g_, dh).transpose(0, 3, 1, 2, 4)
    pad = ((0, 0), (WINDOW, 0), (0, 0), (0, 0))
    kw = jnp.pad(k_w, pad)
    vw = jnp.pad(v_w, pad)
    gates = jax.nn.sigmoid(gate_logits).reshape(b, s, g_, j_, 3)

    cmp_end = jnp.arange(n_cmp) * CMP_STRIDE + (CMP_BLOCK - 1)
    c_start = np.arange(n_cmp) * CMP_STRIDE
    s_start = np.arange(n_sel) * SEL_BLOCK
    cover = jnp.asarray(((c_start[:, None] <= s_start[None, :] + SEL_BLOCK - 1)
                         & (c_start[:, None] + CMP_BLOCK - 1 >= s_start[None, :])).astype(np.float32))
    sel_ids = jnp.arange(n_sel)
    rel_flat = rel_table.reshape(REL_BUCKETS, g_, j_).transpose(1, 0, 2).reshape(g_ * REL_BUCKETS, j_)
    group_off = jnp.arange(g_)[None, :, None, None, None] * REL_BUCKETS
    gather_blocks = jax.vmap(jax.vmap(lambda blk, ix: blk[ix]))

    def query_block(i):
        q0 = i * NSA_QBLOCK
        t = q0 + jnp.arange(NSA_QBLOCK)
        qb = lax.dynamic_slice_in_dim(q, q0, NSA_QBLOCK, axis=1)
        gb = lax.dynamic_slice_in_dim(gates, q0, NSA_QBLOCK, axis=1)

        dist_c = t[:, None] - cmp_end[None, :]
        s_c = jnp.einsum('btgjd,bngd->bgjtn', qb, kc) * scale + rel_bias_grid(rel_table, dist_c)
        p_c = masked_softmax(s_c, dist_c >= 0)
        o_c = jnp.einsum('bgjtn,bngd->btgjd', p_c.astype(vc.dtype), vc)

        imp = jnp.einsum('bgjtn,nm->btgm', p_c, cover)
        cur = t // SEL_BLOCK
        forced = (sel_ids[None, :] == 0) | (sel_ids[None, :] == cur[:, None]) | (sel_ids[None, :] == cur[:, None] - 1)
        causal_blk = sel_ids[None, :] <= cur[:, None]
        score = jnp.where(causal_blk[:, None, :], imp + jnp.where(forced, SEL_FORCE, 0.0)[:, None, :], -jnp.inf)
        top_val, top_idx = lax.top_k(score, top_n)
        top_idx = top_idx.transpose(0, 2, 1, 3)
        k_sel = gather_blocks(ks, top_idx)
        v_sel = gather_blocks(vs, top_idx)
        key_pos = top_idx[..., None] * SEL_BLOCK + jnp.arange(SEL_BLOCK)
        dist_s = t[None, None, :, None, None] - key_pos
        valid_s = jnp.isfinite(top_val).transpose(0, 2, 1, 3)[..., None] & (dist_s >= 0)
        bias_s = rel_flat[group_off + t5_bucket(dist_s)]
        s_s = jnp.einsum('btgjd,bgtnrd->bgjtnr', qb, k_sel) * scale + jnp.moveaxis(bias_s, -1, 2)
        n_keys = top_n * SEL_BLOCK
        p_s = masked_softmax(s_s.reshape(b, g_, j_, NSA_QBLOCK, n_keys),
                             valid_s.reshape(b, g_, 1, NSA_QBLOCK, n_keys))
        o_s = jnp.einsum('bgjtnr,bgtnrd->btgjd', p_s.reshape(s_s.shape).astype(v_sel.dtype), v_sel)

        kwb = lax.dynamic_slice_in_dim(kw, q0, WINDOW + NSA_QBLOCK, axis=1)
        vwb = lax.dynamic_slice_in_dim(vw, q0, WINDOW + NSA_QBLOCK, axis=1)
        kpos = q0 - WINDOW + jnp.arange(WINDOW + NSA_QBLOCK)
        dist_w = t[:, None] - kpos[None, :]
        valid_w = (dist_w >= 0) & (dist_w < WINDOW) & (kpos[None, :] >= 0)
        s_w = jnp.einsum('btgjd,bsgd->bgjts', qb, kwb) * scale + rel_bias_grid(rel_table, dist_w)
        p_w = masked_softmax(s_w, valid_w)
        o_w = jnp.einsum('bgjts,bsgd->btgjd', p_w.astype(vwb.dtype), vwb)

        return gb[..., 0:1] * o_c + gb[..., 1:2] * o_s + gb[..., 2:3] * o_w

    out = lax.map(query_block, jnp.arange(s // NSA_QBLOCK))
    return out.transpose(1, 0, 2, 3, 4, 5).reshape(b, s, NSA_WIDTH)


def causal_depthwise_conv(x, w):
    return lax.conv_general_dilated(x, w[:, None, :], window_strides=(1,), padding=[(CONV_WIDTH - 1, 0)],
                                    dimension_numbers=('NWC', 'WIO', 'NWC'), feature_group_count=x.shape[-1])


def l2_normalize(x):
    return x * lax.rsqrt(jnp.sum(x * x, axis=-1, keepdims=True) + NORM_EPS)


def gated_delta_rule(q, k, v, g, beta):
    b, s, h, dk = q.shape
    dv = v.shape[-1]
    c = GDN_CHUNK
    n = s // c
    q = l2_normalize(q) * dk ** -0.5
    k = l2_normalize(k)

    def chunks(t):
        return jnp.moveaxis(t.reshape(b, n, c, h, *t.shape[3:]), 3, 1)

    q, k, v, g, beta = (chunks(t) for t in (q, k, v, g, beta))
    g = jnp.cumsum(g, axis=-1)
    kb = k * beta[..., None]
    vb = v * beta[..., None]
    tri = jnp.tril(jnp.ones((c, c), dtype=bool))
    strict = jnp.tril(jnp.ones((c, c), dtype=bool), -1)
    diff = g[..., :, None] - g[..., None, :]
    decay = jnp.where(tri, jnp.exp(jnp.where(tri, diff, 0.0)), 0.0)
    lower = jnp.where(strict, jnp.einsum('bhncd,bhnsd->bhncs', kb, k) * decay, 0.0)
    eye = jnp.eye(c, dtype=q.dtype)
    tinv = lax.linalg.triangular_solve(eye + lower, jnp.broadcast_to(eye, lower.shape),
                                       left_side=True, lower=True, unit_diagonal=True)
    u = tinv @ vb
    w = tinv @ (kb * jnp.exp(g)[..., None])
    qk = jnp.einsum('bhncd,bhnsd->bhncs', q, k) * decay

    def step(state, inp):
        qn, kn, un, wn, gn, qkn = inp
        v_new = un - wn @ state
        o = (qn * jnp.exp(gn)[..., None]) @ state + qkn @ v_new
        g_last = gn[..., -1:]
        state = state * jnp.exp(g_last)[..., None] + jnp.einsum('bhck,bhcv->bhkv', kn * jnp.exp(g_last - gn)[..., None], v_new)
        return state, o

    xs = tuple(jnp.moveaxis(t, 2, 0) for t in (q, k, u, w, g, qk))
    state0 = jnp.zeros((b, h, dk, dv), q.dtype)
    _, o = lax.scan(step, state0, xs)
    return o.transpose(1, 0, 3, 2, 4).reshape(b, s, h, dv)


def gated_deltanet(qkv, z, a, bl, conv_w, a_log, dt_bias, norm_w):
    b, s, _ = qkv.shape
    dtype = qkv.dtype
    qkv = jax.nn.silu(causal_depthwise_conv(qkv, conv_w)).astype(jnp.float32)
    q, k, v = jnp.split(qkv, 3, axis=-1)
    shape = (b, s, GDN_HEADS, GDN_HEAD_DIM)
    beta = jax.nn.sigmoid(bl.astype(jnp.float32))
    g = -jnp.exp(a_log.astype(jnp.float32)) * jax.nn.softplus(a.astype(jnp.float32) + dt_bias.astype(jnp.float32))
    o = gated_delta_rule(q.reshape(shape), k.reshape(shape), v.reshape(shape), g, beta)
    o = o * lax.rsqrt(jnp.mean(o * o, axis=-1, keepdims=True) + NORM_EPS) * norm_w.astype(jnp.float32)
    o = o * jax.nn.silu(z.astype(jnp.float32)).reshape(shape)
    return o.reshape(b, s, GDN_WIDTH).astype(dtype)


def hybrid_layer(x, rel_table, ln1_w, w_in, pe_k, pe_v, w1_k, w2_k, w1_v, w2_v, conv_w, a_log, dt_bias,
                 gdn_norm_w, w_pa, w_pb, w_o, ln2_w, w_up, w_down):
    h = rms_norm(x, ln1_w)
    proj = h @ w_in
    offs = [int(o) for o in np.cumsum(IN_SPLIT_SIZES)[:-1]]
    nsa_q, nsa_kv, nsa_gate, gdn_qkv, gdn_z, gdn_a, gdn_b, merge = jnp.split(proj, offs, axis=-1)
    o_a = nsa_attention(nsa_q, nsa_kv, nsa_gate, rel_table, pe_k, pe_v, w1_k, w2_k, w1_v, w2_v)
    o_b = gated_deltanet(gdn_qkv, gdn_z, gdn_a, gdn_b, conv_w, a_log, dt_bias, gdn_norm_w)
    g_a, g_b = jnp.split(jax.nn.sigmoid(merge), 2, axis=-1)
    x = x + (g_a * (o_a @ w_pa) + g_b * (o_b @ w_pb)) @ w_o
    h = rms_norm(x, ln2_w)
    return x + jnp.square(jax.nn.relu(h @ w_up)) @ w_down


def setup_inputs(seed: int = 0) -> dict:
    key = jax.random.key(seed)
    k = jax.random.split(key, 21)
    f32 = jnp.float32
    L = DEPTH

    def normal(kk, shape, scale):
        return jax.random.normal(kk, shape, f32) * scale

    cmp_in = CMP_BLOCK * NSA_HEAD_DIM
    dt = jnp.exp(jax.random.uniform(k[12], (L, GDN_HEADS), f32, math.log(1e-3), math.log(1e-1)))
    return {
        'x': normal(k[0], (BATCH, SEQ, D_MODEL), 1.0),
        'rel_table': normal(k[1], (REL_BUCKETS, NSA_HEADS), 0.5),
        'ln1_w': 1.0 + normal(k[2], (L, D_MODEL), 0.02),
        'w_in': normal(k[3], (L, D_MODEL, IN_COLS), D_MODEL ** -0.5),
        'cmp_pe_k': normal(k[4], (L, CMP_BLOCK, NSA_HEAD_DIM), 0.1),
        'cmp_pe_v': normal(k[5], (L, CMP_BLOCK, NSA_HEAD_DIM), 0.1),
        'cmp_w1_k': normal(k[6], (L, cmp_in, CMP_HIDDEN), cmp_in ** -0.5),
        'cmp_w2_k': normal(k[7], (L, CMP_HIDDEN, NSA_HEAD_DIM), CMP_HIDDEN ** -0.5),
        'cmp_w1_v': normal(k[8], (L, cmp_in, CMP_HIDDEN), cmp_in ** -0.5),
        'cmp_w2_v': normal(k[9], (L, CMP_HIDDEN, NSA_HEAD_DIM), CMP_HIDDEN ** -0.5),
        'conv_w': normal(k[10], (L, CONV_WIDTH, 3 * GDN_WIDTH), CONV_WIDTH ** -0.5),
        'a_log': jnp.log(jax.random.uniform(k[11], (L, GDN_HEADS), f32, 1.0, 16.0)),
        'dt_bias': dt + jnp.log(-jnp.expm1(-dt)),
        'gdn_norm_w': 1.0 + normal(k[13], (L, GDN_HEAD_DIM), 0.02),
        'w_pa': normal(k[14], (L, NSA_WIDTH, D_MODEL), NSA_WIDTH ** -0.5),
        'w_pb': normal(k[15], (L, GDN_WIDTH, D_MODEL), GDN_WIDTH ** -0.5),
        'w_o': normal(k[16], (L, D_MODEL, D_MODEL), D_MODEL ** -0.5),
        'ln2_w': 1.0 + normal(k[17], (L, D_MODEL), 0.02),
        'w_up': normal(k[18], (L, D_MODEL, D_FF), D_MODEL ** -0.5),
        'w_down': normal(k[19], (L, D_FF, D_MODEL), D_FF ** -0.5),
        'ln_f_w': 1.0 + normal(k[20], (D_MODEL,), 0.02),
    }


def reference(x, rel_table, ln1_w, w_in, cmp_pe_k, cmp_pe_v, cmp_w1_k, cmp_w2_k, cmp_w1_v, cmp_w2_v, conv_w,
              a_log, dt_bias, gdn_norm_w, w_pa, w_pb, w_o, ln2_w, w_up, w_down, ln_f_w):
    for l in range(DEPTH):
        x = hybrid_layer(x, rel_table, ln1_w[l], w_in[l], cmp_pe_k[l], cmp_pe_v[l], cmp_w1_k[l], cmp_w2_k[l],
                         cmp_w1_v[l], cmp_w2_v[l], conv_w[l], a_log[l], dt_bias[l], gdn_norm_w[l],
                         w_pa[l], w_pb[l], w_o[l], ln2_w[l], w_up[l], w_down[l])
    return rms_norm(x, ln_f_w)
```

```python
import math
import numpy as np
from contextlib import ExitStack
import concourse.bass as bass
import concourse.mybir as mybir
from concourse.bass_utils import run_bass_kernel_spmd

F32 = mybir.dt.float32
BF16 = mybir.dt.bfloat16
I32 = mybir.dt.int32
ALU = mybir.AluOpType
AF = mybir.ActivationFunctionType
AX = mybir.AxisListType

NCORES = 8
BPC = 2
S = 2048
D = 2048
DEPTH = 2
DFF = 8192
IN_COLS = 10792
EPS = 1e-6
NEG = -30000.0

C_Q = 0
C_KV = 1024
C_GATE = 2560
C_GQKV = 2584
C_Z = 5656
C_A = 6680
C_B = 6688
C_MERGE = 6696


class Buf:
    __slots__ = ("name", "w", "r")

    def __init__(self, name):
        self.name = name
        self.w = None
        self.r = {}


class KB:
    SEM_LIMIT = 30000

    def __init__(self, nc, es, n_dma_slots=8):
        self.nc = nc
        self.es = es
        self.eng = {"pe": nc.tensor, "act": nc.scalar, "dve": nc.vector, "pool": nc.gpsimd, "sp": nc.sync}
        self.sems = {}
        self.cur = {}
        self.epoch = {}
        for e in self.eng:
            self.epoch[e] = 0
            self._new_sem(e)
        self.seen = {e: {} for e in self.eng}
        self.dma_slots = {}
        for q in ("sp", "act", "pool"):
            sl = []
            for i in range(n_dma_slots):
                key = f"d_{q}_{i}"
                self.sems[key] = es.enter_context(nc.semaphore(key))
                sl.append([key, 0])
            self.dma_slots[q] = [sl, 0]
        self.n_ins = 0
        self.n_wait = 0
        self.uid = 0
        self.last_tok = {}

    def _new_sem(self, e):
        key = f"s_{e}_{self.epoch[e]}"
        self.sems[key] = self.es.enter_context(self.nc.semaphore(key))
        self.cur[e] = [key, 0]
        self.epoch[e] += 1

    def _need(self, e, toks):
        seen = self.seen[e]
        best = {}
        for t in toks:
            if t is None:
                continue
            k, v = t
            if seen.get(k, 0) >= v:
                continue
            if best.get(k, 0) < v:
                best[k] = v
        for k, v in best.items():
            self.eng[e].wait_ge(self.sems[k], v)
            seen[k] = v
            self.n_wait += 1

    def _deps(self, reads, writes, skip_waw_key=None):
        toks = []
        for b in reads:
            toks.append(b.w)
        for b in writes:
            if b.w is not None and not (skip_waw_key is not None and b.w[0] == skip_waw_key):
                toks.append(b.w)
            for k, v in b.r.items():
                toks.append((k, v))
        return toks

    def _record(self, tok, reads, writes):
        k, v = tok
        for b in reads:
            if b.r.get(k, 0) < v:
                b.r[k] = v
        for b in writes:
            b.w = tok
            b.r = {}

    def op(self, e, fn, reads=(), writes=(), sig=True):
        cur = self.cur[e]
        if cur[1] >= self.SEM_LIMIT:
            self._new_sem(e)
            cur = self.cur[e]
        skip = cur[0] if e == "pe" else None
        self._need(e, self._deps(reads, writes, skip_waw_key=skip))
        ins = fn(self.eng[e])
        self.n_ins += 1
        tok = (cur[0], cur[1] + 1)
        if sig:
            ins.then_inc(self.sems[cur[0]], 1)
            cur[1] += 1
            self.last_tok[e] = tok
        self._record(tok, reads, writes)
        return ins

    def dma(self, q, out, in_, reads=(), writes=(), **kw):
        sl, idx = self.dma_slots[q]
        slot = sl[idx % len(sl)]
        self.dma_slots[q][1] = idx + 1
        key, uses = slot
        toks = self._deps(reads, writes)
        if uses > 0:
            toks.append((key, 16 * uses))
        self._need(q, toks)
        ins = self.eng[q].dma_start(out=out, in_=in_, **kw)
        ins.then_inc(self.sems[key], 16)
        slot[1] = uses + 1
        self.n_ins += 1
        tok = (key, 16 * (uses + 1))
        self._record(tok, reads, writes)
        return ins

    def barrier(self):
        toks = []
        for e in self.eng:
            if e in self.last_tok:
                toks.append(self.last_tok[e])
        for q in self.dma_slots:
            for key, uses in self.dma_slots[q][0]:
                if uses > 0:
                    toks.append((key, 16 * uses))
        for e in self.eng:
            self._need(e, toks)

    def sb(self, es, name, shape, dtype):
        self.uid += 1
        t = es.enter_context(self.nc.sbuf_tensor(f"{name}_{self.uid}", list(shape), dtype))
        return t, Buf(name)

    def ps(self, es, name, shape, dtype=F32):
        self.uid += 1
        t = es.enter_context(self.nc.psum_tensor(f"{name}_{self.uid}", list(shape), dtype))
        return t, Buf(name)


class Rot:
    def __init__(self, items):
        self.items = items
        self.i = 0

    def next(self):
        it = self.items[self.i % len(self.items)]
        self.i += 1
        return it


class Prog:
    def __init__(self, nc, dbg=()):
        self.nc = nc
        self.dbg = set(dbg)
        self.ext_out = {}

    def dram(self, name, shape, dtype):
        kind = "ExternalOutput" if name in self.dbg else "Internal"
        if ("in:" + name) in self.dbg:
            kind = "ExternalInput"
        t = self.nc.dram_tensor(name, list(shape), dtype, kind=kind)
        if name in self.dbg:
            self.ext_out[name] = t
        return t.ap()

    def setup(self, kb, es):
        nc = self.nc
        self.ident_f, self.ident_f_b = kb.sb(es, "identf", [128, 128], F32)
        self.ident_b, self.ident_b_b = kb.sb(es, "identb", [128, 128], BF16)
        kb.op("pool", lambda e: e.memset(self.ident_f[:], 0.0), writes=[self.ident_f_b])
        kb.op("pool", lambda e: e.affine_select(out=self.ident_f[:], in_=self.ident_f[:], pattern=[[-1, 128]],
                                                compare_op=ALU.not_equal, fill=1.0, base=0, channel_multiplier=1),
              reads=[self.ident_f_b], writes=[self.ident_f_b])
        kb.op("dve", lambda e: e.tensor_copy(out=self.ident_b[:], in_=self.ident_f[:]),
              reads=[self.ident_f_b], writes=[self.ident_b_b])
        self.eps_t, self.eps_b = kb.sb(es, "eps", [128, 1], F32)
        kb.op("pool", lambda e: e.memset(self.eps_t[:], EPS), writes=[self.eps_b])
        self.one_t, self.one_b = kb.sb(es, "onec", [128, 1], F32)
        kb.op("pool", lambda e: e.memset(self.one_t[:], 1.0), writes=[self.one_b])

    def norm_res(self, kb, es, lnw_ap, npts=4):
        R = {}
        R["lnw"] = kb.sb(es, "lnw", [128, D], F32)
        kb.dma("sp", R["lnw"][0][:], bass.AP(tensor=lnw_ap.tensor, offset=lnw_ap.offset, ap=[[0, 128], [1, D]]),
               writes=[R["lnw"][1]])
        R["xts"] = Rot([kb.sb(es, f"xt{i}", [128, D], F32) for i in range(2)])
        R["hbs"] = Rot([kb.sb(es, f"hb{i}", [128, D], BF16) for i in range(2)])
        R["junk"] = kb.sb(es, "junk", [128, D], BF16)
        R["sts"] = Rot([kb.sb(es, f"st{i}", [128, 4], F32) for i in range(2)])
        R["pts"] = Rot([kb.ps(es, f"pt{i}", [128, 4, 128], BF16) for i in range(npts)])
        return R

    def norm_T(self, kb, x_ap, lnw_ap, hT, hT_b, ntok, tok0=0, res=None):
        nc = self.nc
        with ExitStack() as es:
            R = res if res is not None else self.norm_res(kb, es, lnw_ap)
            lnw, lnw_b = R["lnw"]
            xts, hbs, sts, pts = R["xts"], R["hbs"], R["sts"], R["pts"]
            junk, junk_b = R["junk"]
            for tt in range(ntok // 128):
                xt, xt_b = xts.next()
                hb, hb_b = hbs.next()
                st, st_b = sts.next()
                r0 = tok0 + tt * 128
                kb.dma("sp", xt[:], x_ap[r0:r0 + 128, :], writes=[xt_b])
                kb.op("act", lambda e: e.activation(out=junk[:], in_=xt[:], func=AF.Square, accum_out=st[:, 0:1]),
                      reads=[xt_b], writes=[junk_b, st_b])
                kb.op("act", lambda e: e.activation(out=st[:, 1:2], in_=st[:, 0:1], func=AF.Sqrt, scale=1.0 / D, bias=self.eps_t[:, 0:1]),
                      reads=[st_b, self.eps_b], writes=[st_b])
                kb.op("dve", lambda e: e.reciprocal(out=st[:, 2:3], in_=st[:, 1:2]), reads=[st_b], writes=[st_b])
                kb.op("dve", lambda e: e.scalar_tensor_tensor(out=hb[:], in0=xt[:], scalar=st[:, 2:3], in1=lnw[:],
                                                              op0=ALU.mult, op1=ALU.mult),
                      reads=[xt_b, st_b, lnw_b], writes=[hb_b])
                for g in range(4):
                    pt, pt_b = pts.next()
                    for j in range(4):
                        c = g * 4 + j
                        kb.op("pe", lambda e, c=c, j=j: e.transpose(out=pt[:, j, :], in_=hb[:, c * 128:(c + 1) * 128],
                                                                     identity=self.ident_b[:]),
                              reads=[hb_b, self.ident_b_b], writes=[pt_b], sig=(j == 3))
                    eng = "act" if g % 2 == 0 else "dve"
                    dst = hT[:, g * 4:(g + 1) * 4, tt * 128:(tt + 1) * 128]
                    if eng == "act":
                        kb.op("act", lambda e: e.copy(out=dst, in_=pt[:]), reads=[pt_b], writes=[hT_b])
                    else:
                        kb.op("dve", lambda e: e.tensor_copy(out=dst, in_=pt[:]), reads=[pt_b], writes=[hT_b])
            if res is None:
                kb.barrier()

    def dense(self, kb, inT, inT_b, KC, ntok, W_ap, jobs, wq="pool"):
        PW = 512
        with ExitStack() as es:
            panels = Rot([kb.sb(es, f"wp{i}", [128, KC, PW], BF16) for i in range(2)])
            pss = Rot([kb.ps(es, f"dps{i}", [128, 512], F32) for i in range(4)])
            self.dense_core(kb, inT, inT_b, KC, ntok, W_ap, jobs, panels, pss, wq)
            kb.barrier()

    def dense_core(self, kb, inT, inT_b, KC, ntok, W_ap, jobs, panels, pss, wq="pool"):
        PW = 512
        if True:
            Wv = W_ap.rearrange("(kc p) n -> p kc n", p=128)
            for (c0, ncols, form, epi) in jobs:
                for p0 in range(0, ncols, PW):
                    pw = min(PW, ncols - p0)
                    wp, wp_b = panels.next()
                    kb.dma(wq, wp[:, 0:KC, 0:pw], Wv[:, :, c0 + p0:c0 + p0 + pw], writes=[wp_b])
                    if form == "b":
                        for cb in range(0, pw, 128):
                            cw = min(128, pw - cb)
                            for t0 in range(0, ntok, 512):
                                tw = min(512, ntok - t0)
                                ps, ps_b = pss.next()
                                for kc in range(KC):
                                    kb.op("pe", lambda e, kc=kc: e.matmul(ps[0:cw, 0:tw], lhsT=wp[:, kc, cb:cb + cw],
                                                                         rhs=inT[:, kc, t0:t0 + tw],
                                                                         start=(kc == 0), stop=(kc == KC - 1)),
                                          reads=[wp_b, inT_b], writes=[ps_b], sig=(kc == KC - 1))
                                epi(ps, ps_b, p0 + cb, cw, t0, tw)
                    else:
                        for t0 in range(0, ntok, 128):
                            ps, ps_b = pss.next()
                            for kc in range(KC):
                                kb.op("pe", lambda e, kc=kc: e.matmul(ps[:, 0:pw], lhsT=inT[:, kc, t0:t0 + 128],
                                                                     rhs=wp[:, kc, 0:pw],
                                                                     start=(kc == 0), stop=(kc == KC - 1)),
                                      reads=[wp_b, inT_b], writes=[ps_b], sig=(kc == KC - 1))
                            epi(ps, ps_b, p0, pw, t0, 128)

    def make_stage(self, kb, es, n=4):
        self.stg_f = Rot([kb.sb(es, f"stgf{i}", [128, 512], F32) for i in range(n)])
        self.stg_h = Rot([kb.sb(es, f"stgh{i}", [128, 512], BF16) for i in range(n)])
        self.evac_i = 0

    def evac(self, kb, ps, ps_b, rows, cols, dst_ap, dtype=F32, func=None, eng=None):
        st, st_b = (self.stg_f if dtype == F32 else self.stg_h).next()
        if eng is None:
            eng = "act" if (func is not None or self.evac_i % 2 == 0) else "dve"
        self.evac_i += 1
        if eng == "act":
            f = func if func is not None else AF.Copy
            kb.op("act", lambda e: e.activation(out=st[0:rows, 0:cols], in_=ps[0:rows, 0:cols], func=f),
                  reads=[ps_b], writes=[st_b])
        else:
            kb.op("dve", lambda e: e.tensor_copy(out=st[0:rows, 0:cols], in_=ps[0:rows, 0:cols]),
                  reads=[ps_b], writes=[st_b])
        kb.dma("sp", dst_ap, st[0:rows, 0:cols], reads=[st_b])

    def alloc_proj_scratch(self):
        self.qT_d = self.dram("qT_d", [8, 128, S], BF16)
        self.kvT_d = self.dram("kvT_d", [4, 2, 128, S], BF16)
        self.vtok_d = self.dram("vtok_d", [2, S, 256], BF16)
        self.gate_d = self.dram("gate_d", [S, 24], F32)
        self.gqkvT_d = self.dram("gqkvT_d", [24, 128, S], F32)
        self.z_d = self.dram("z_d", [S, 1024], F32)
        self.ab_d = self.dram("ab_d", [S, 16], F32)
        self.mergeT_d = self.dram("mergeT_d", [32, 128, S], F32)

    def proj_phase(self, kb, hT, hT_b, w_in_l):
        with ExitStack() as es:
            self.make_stage(kb, es)

            def epi_q(ps, ps_b, j0, cw, t0, tw):
                self.evac(kb, ps, ps_b, cw, tw, self.qT_d[j0 // 128, :, t0:t0 + tw], dtype=BF16)

            def epi_kvT(kind):
                def f(ps, ps_b, j0, cw, t0, tw):
                    self.evac(kb, ps, ps_b, cw, tw, self.kvT_d[kind, j0 // 128, :, t0:t0 + tw], dtype=BF16)
                return f

            def epi_vtok(kind):
                def f(ps, ps_b, j0, pw, t0, tw):
                    self.evac(kb, ps, ps_b, 128, pw, self.vtok_d[kind, t0:t0 + 128, j0:j0 + pw], dtype=BF16)
                return f

            def epi_gate(ps, ps_b, j0, pw, t0, tw):
                self.evac(kb, ps, ps_b, 128, pw, self.gate_d[t0:t0 + 128, :], func=AF.Sigmoid)

            def epi_gqkv(ps, ps_b, j0, cw, t0, tw):
                self.evac(kb, ps, ps_b, cw, tw, self.gqkvT_d[j0 // 128, :, t0:t0 + tw])

            def epi_z(ps, ps_b, j0, pw, t0, tw):
                self.evac(kb, ps, ps_b, 128, pw, self.z_d[t0:t0 + 128, j0:j0 + pw], func=AF.Silu)

            def epi_ab(ps, ps_b, j0, pw, t0, tw):
                self.evac(kb, ps, ps_b, 128, pw, self.ab_d[t0:t0 + 128, :])

            def epi_merge(ps, ps_b, j0, cw, t0, tw):
                self.evac(kb, ps, ps_b, cw, tw, self.mergeT_d[j0 // 128, :, t0:t0 + tw], func=AF.Sigmoid)

            jobs = [
                (C_Q, 1024, "b", epi_q),
                (C_KV + 0, 256, "b", epi_kvT(0)),
                (C_KV + 256, 256, "b", epi_kvT(1)),
                (C_KV + 512, 256, "b", epi_kvT(2)),
                (C_KV + 768, 256, "a", epi_vtok(0)),
                (C_KV + 1024, 256, "b", epi_kvT(3)),
                (C_KV + 1280, 256, "a", epi_vtok(1)),
                (C_GATE, 24, "a", epi_gate),
                (C_GQKV, 3072, "b", epi_gqkv),
                (C_Z, 1024, "a", epi_z),
                (C_A, 16, "a", epi_ab),
                (C_MERGE, 4096, "b", epi_merge),
            ]
            self.dense(kb, hT, hT_b, 16, S, w_in_l, jobs)


WNAMES = [("rel_table", [32, 8]), ("ln1_w", [DEPTH, D]), ("w_in", [DEPTH, D, IN_COLS]),
          ("cmp_pe_k", [DEPTH, 32, 128]), ("cmp_pe_v", [DEPTH, 32, 128]),
          ("cmp_w1_k", [DEPTH, 4096, 256]), ("cmp_w2_k", [DEPTH, 256, 128]),
          ("cmp_w1_v", [DEPTH, 4096, 256]), ("cmp_w2_v", [DEPTH, 256, 128]),
          ("conv_w", [DEPTH, 4, 3072]), ("a_log", [DEPTH, 8]), ("dt_bias", [DEPTH, 8]),
          ("gdn_norm_w", [DEPTH, 128]), ("w_pa", [DEPTH, 1024, D]), ("w_pb", [DEPTH, 1024, D]),
          ("w_o", [DEPTH, D, D]), ("ln2_w", [DEPTH, D]), ("w_up", [DEPTH, D, DFF]),
          ("w_down", [DEPTH, DFF, D]), ("ln_f_w", [D])]


class LazyW:
    def __init__(self, nc, depth):
        self.nc = nc
        self.depth = depth
        self.aps = {}
        self.shapes = dict(WNAMES)

    def __getitem__(self, n):
        if n not in self.aps:
            shp = list(self.shapes[n])
            if len(shp) > 1 and shp[0] == DEPTH and n != "rel_table":
                shp[0] = self.depth
            self.aps[n] = self.nc.dram_tensor(n, shp, F32, kind="ExternalInput").ap()
        return self.aps[n]


def build(dbg=(), stop=None, nseq=BPC, depth=DEPTH):
    nc = bass.Bass("TRN2", target_bir_lowering=False)
    P = Prog(nc, dbg)
    x = nc.dram_tensor("x", [BPC, S, D], F32, kind="ExternalInput").ap()
    W = LazyW(nc, depth)
    P.W = W
    out = nc.dram_tensor("out", [BPC, S, D], F32, kind="ExternalOutput").ap()
    P.alloc_proj_scratch()
    P.oaT_d = P.dram("oaT_d", [8, 128, S], BF16)
    P.obT_d = P.dram("obT_d", [8, 128, S], BF16)
    P.uT_d = P.dram("uT_d", [16, 128, S], BF16)
    X1 = P.dram("X1_d", [BPC, S, D], F32)
    X2 = P.dram("X2_d", [BPC, S, D], F32)
    P.dbg_nsa = None
    P.dbg_gdn = None
    P.dbg_gdn_n = 0
    P.gdn_heads = 8
    if "gdn_dbg" in P.dbg:
        P.dbg_gdn = {nm: nc.dram_tensor("dbg_" + nm, [128, 128], F32, kind="ExternalOutput").ap()
                     for nm in ["q_tok", "k_tok", "v_tok", "e1", "e2", "TT", "u", "negw", "zt", "xm", "qkt", "yy", "kd", "Sn", "gcol"]}
        P.gdn_heads = GDN_TEST_HEADS
    if "nsa_dbg" in P.dbg:
        P.dbg_nsa = {"kcT": nc.dram_tensor("dbg_kcT", [128, 128], BF16, kind="ExternalOutput").ap(),
                     "vc": nc.dram_tensor("dbg_vc", [128, 164], F32, kind="ExternalOutput").ap(),
                     "imp": nc.dram_tensor("dbg_imp", [128, 16, 32], F32, kind="ExternalOutput").ap(),
                     "selbT": nc.dram_tensor("dbg_selbT", [32, S], BF16, kind="ExternalOutput").ap()}
    with ExitStack() as es:
        kb = KB(nc, es)
        P.setup(kb, es)
        nsa_setup(P, kb, es, W["rel_table"])
        gdn_setup(P, kb, es)
        kb.barrier()
        if stop == "nsa_only":
            nsa_phase(P, kb, 0)
        if stop == "gdn_only":
            gdn_phase(P, kb, 0)
        if stop == "post_only":
            merge_phase(P, kb, 0)
            wo_phase(P, kb, 0, x[0], X1[0])
            mlp_phase(P, kb, 0, X1[0], X2[0])
            final_norm(P, kb, X2[0], W["ln_f_w"], out[0])
        for l in range(depth if stop not in ("setup", "nsa_only", "gdn_only", "post_only") else 0):
            for s in range(nseq):
                x_in = x[s] if l == 0 else X2[s]
                with ExitStack() as es_h:
                    hT, hT_b = kb.sb(es_h, "hT", [128, 16, S], BF16)
                    P.norm_T(kb, x_in, W["ln1_w"][l], hT, hT_b, S)
                    if stop == "norm":
                        hT_d = P.dram("hT_d", [128, 16, S], BF16)
                        kb.dma("sp", hT_d, hT[:], reads=[hT_b])
                        break
                    P.proj_phase(kb, hT, hT_b, W["w_in"][l])
                    kb.barrier()
                if stop == "proj":
                    break
                nsa_phase(P, kb, l)
                if stop == "nsa":
                    break
                gdn_phase(P, kb, l)
                merge_phase(P, kb, l)
                wo_phase(P, kb, l, x_in, X1[s])
                mlp_phase(P, kb, l, X1[s], X2[s])
            if stop is not None:
                break
        if stop is None:
            for s in range(nseq):
                final_norm(P, kb, X2[s], W["ln_f_w"], out[s])
        kb.barrier()
        print("instructions", kb.n_ins, "waits", kb.n_wait)
    return nc, P


_CACHE = {}


def kernel(**inputs):
    if "nc" not in _CACHE:
        _CACHE["nc"] = build()
    nc, P = _CACHE["nc"]
    x = np.ascontiguousarray(inputs["x"], dtype=np.float32)
    wts = {n: np.ascontiguousarray(inputs[n], dtype=np.float32) for n in P.W.aps}
    in_maps = []
    for c in range(NCORES):
        m = dict(wts)
        m["x"] = np.ascontiguousarray(x[c * BPC:(c + 1) * BPC])
        in_maps.append(m)
    res = run_bass_kernel_spmd(nc, in_maps, core_ids=list(range(NCORES)))
    return np.concatenate([np.asarray(r["out"], dtype=np.float32) for r in res.results], axis=0)


SQ = math.sqrt(128.0)
SCALE = 1.0 / SQ
OFFW, WW = 384, 1408
OFFS, WS = 384, 1024
LGW = 127 + WW
LGS = 127 + WS
LGC = 4080
LG = 4096


def _bucket_ranges():
    d = np.arange(0, 4200)
    nf = np.maximum(d, 1).astype(np.float32)
    large = 16 + (np.log(nf / np.float32(16)) / np.float32(math.log(128 / 16)) * np.float32(16)).astype(np.int32)
    large = np.minimum(large, 31)
    bk = np.where(d < 16, d, large)
    out = []
    for b in range(32):
        idx = np.nonzero(bk == b)[0]
        out.append((b, int(idx[0]), int(idx[-1]) + 1))
    return out


def nsa_setup(P, kb, es, rel_table):
    nc = P.nc
    P.Mw_d = P.dram("Mw_d", [8, 128, WW], BF16)
    P.Ms_d = P.dram("Ms_d", [8, 128, WS], BF16)
    P.Mc_d = P.dram("Mc_d", [8, 128, S], BF16)
    G_d = P.dram("G_d", [3, 8, LG], BF16)
    P.t31, P.t31_b = kb.sb(es, "t31", [128, 8], F32)
    kb.dma("sp", P.t31[:], bass.AP(tensor=rel_table.tensor, offset=rel_table.offset + 31 * 8, ap=[[0, 128], [1, 8]]),
           writes=[P.t31_b])
    P.Jb, P.Jb_b = kb.sb(es, "Jb", [128, 128], BF16)
    P.I30k, P.I30k_b = kb.sb(es, "I30k", [128, 128], BF16)
    P.expall, P.expall_b = kb.sb(es, "expall", [128, S], BF16)
    P.FB, P.FB_b = kb.sb(es, "FB", [128, 16, 32], F32)
    P.cover, P.cover_b = kb.sb(es, "cover", [128, 32], F32)
    with ExitStack() as es2:
        tmpf, tmpf_b = kb.sb(es2, "tmpf", [128, 128], F32)
        kb.op("pool", lambda e: e.memset(tmpf[:], 0.0), writes=[tmpf_b])
        kb.op("pool", lambda e: e.affine_select(out=tmpf[:], in_=tmpf[:], pattern=[[1, 128]], compare_op=ALU.not_equal,
                                                fill=1.0, base=-127, channel_multiplier=1), reads=[tmpf_b], writes=[tmpf_b])
        kb.op("dve", lambda e: e.tensor_copy(out=P.Jb[:], in_=tmpf[:]), reads=[tmpf_b], writes=[P.Jb_b])
        kb.op("dve", lambda e: e.tensor_scalar(out=P.I30k[:], in0=P.ident_f[:], scalar1=30000.0, scalar2=None, op0=ALU.mult),
              reads=[P.ident_f_b], writes=[P.I30k_b])
        ex, ex_b = kb.sb(es2, "ex", [32, S], F32)
        kb.op("pool", lambda e: e.memset(ex[:], 1.0), writes=[ex_b])
        kb.op("pool", lambda e: e.affine_select(out=ex[:], in_=ex[:], pattern=[[1, S]], compare_op=ALU.is_ge, fill=0.0,
                                                base=0, channel_multiplier=-64), reads=[ex_b], writes=[ex_b])
        kb.op("pool", lambda e: e.affine_select(out=ex[:], in_=ex[:], pattern=[[-1, S]], compare_op=ALU.is_ge, fill=0.0,
                                                base=63, channel_multiplier=64), reads=[ex_b], writes=[ex_b])
        kb.op("pool", lambda e: e.memset(P.expall[:], 0.0), writes=[P.expall_b])
        kb.op("dve", lambda e: e.tensor_copy(out=P.expall[0:32, :], in_=ex[:]), reads=[ex_b, P.expall_b], writes=[P.expall_b])
        kb.op("pool", lambda e: e.memset(P.cover[:], 1.0), writes=[P.cover_b])
        kb.op("pool", lambda e: e.affine_select(out=P.cover[:], in_=P.cover[:], pattern=[[64, 32]], compare_op=ALU.is_ge,
                                                fill=0.0, base=63, channel_multiplier=-16), reads=[P.cover_b], writes=[P.cover_b])
        kb.op("pool", lambda e: e.affine_select(out=P.cover[:], in_=P.cover[:], pattern=[[-64, 32]], compare_op=ALU.is_ge,
                                                fill=0.0, base=31, channel_multiplier=16), reads=[P.cover_b], writes=[P.cover_b])
        kb.op("pool", lambda e: e.memset(P.FB[:], 0.0), writes=[P.FB_b])
        for st in range(16):
            for half in range(2):
                c = 2 * st + half
                ps_ = slice(64 * half, 64 * half + 64)
                if c + 1 < 32:
                    kb.op("pool", lambda e, ps_=ps_, c=c, st=st: e.memset(P.FB[ps_, st, c + 1:32], -1e30),
                          reads=[P.FB_b], writes=[P.FB_b])
                for m in sorted(set([0, c, max(c - 1, 0)])):
                    kb.op("pool", lambda e, ps_=ps_, m=m, st=st: e.memset(P.FB[ps_, st, m:m + 1], 1000.0),
                          reads=[P.FB_b], writes=[P.FB_b])
        tabT, tabT_b = kb.sb(es2, "tabT", [8, 32], F32)
        with nc.allow_non_contiguous_dma(reason="tiny table transpose"):
            kb.dma("sp", tabT[:], rel_table.rearrange("b h -> h b"), writes=[tabT_b])
        zer, zer_b = kb.sb(es2, "zer", [8, LG], F32)
        kb.op("pool", lambda e: e.memset(zer[:], 0.0), writes=[zer_b])
        rngs = _bucket_ranges()
        for kind, (off, dmax, L) in enumerate([(127 + OFFW, 512, LGW), (127 + OFFS, None, LGS), (2063, None, LGC)]):
            G, G_b = kb.sb(es2, f"G{kind}", [8, LG], F32)
            Gh, Gh_b = kb.sb(es2, f"Gh{kind}", [8, LG], BF16)
            kb.op("pool", lambda e, G=G: e.memset(G[:], NEG), writes=[G_b])
            for (b, lo, hi) in rngs:
                if b == 31:
                    hi = 10 ** 6
                if dmax is not None:
                    hi = min(hi, dmax)
                a0 = lo + off
                a1 = min(hi + off, L)
                if a1 <= a0:
                    continue
                kb.op("dve", lambda e, G=G, a0=a0, a1=a1, b=b: e.tensor_scalar(
                    out=G[:, a0:a1], in0=zer[:, a0:a1], scalar1=tabT[:, b:b + 1], scalar2=SQ, op0=ALU.add, op1=ALU.mult),
                    reads=[zer_b, tabT_b, G_b], writes=[G_b])
            kb.op("dve", lambda e, G=G, Gh=Gh: e.tensor_copy(out=Gh[:], in_=G[:]), reads=[G_b], writes=[Gh_b])
            kb.dma("sp", G_d[kind], Gh[:], reads=[Gh_b])
        kb.barrier()
        mps = Rot([kb.ps(es2, f"mps{i}", [128, 512], F32) for i in range(2)])
        mrev = Rot([kb.sb(es2, f"mrev{i}", [128, S], BF16) for i in range(2)])
        mout = Rot([kb.sb(es2, f"mout{i}", [128, S], BF16) for i in range(2)])
        for kind, (W_, pstep, dst) in enumerate([(WW, 1, P.Mw_d), (WS, 1, P.Ms_d), (S, 16, P.Mc_d)]):
            for h in range(8):
                mr, mr_b = mrev.next()
                mo, mo_b = mout.next()
                g_ap = G_d[kind, h]
                kb.dma("sp", mr[:, 0:W_], bass.AP(tensor=g_ap.tensor, offset=g_ap.offset, ap=[[pstep, 128], [1, W_]]),
                       writes=[mr_b])
                for c0 in range(0, W_, 512):
                    cw = min(512, W_ - c0)
                    ps, ps_b = mps.next()
                    kb.op("pe", lambda e, ps=ps, mr=mr, c0=c0, cw=cw: e.matmul(ps[:, 0:cw], lhsT=P.Jb[:], rhs=mr[:, c0:c0 + cw],
                                                                              start=True, stop=True),
                          reads=[P.Jb_b, mr_b], writes=[ps_b])
                    kb.op("act", lambda e, ps=ps, mo=mo, c0=c0, cw=cw: e.copy(out=mo[:, c0:c0 + cw], in_=ps[:, 0:cw]),
                          reads=[ps_b], writes=[mo_b])
                kb.dma("sp", dst[h], mo[:, 0:W_], reads=[mo_b])
        kb.barrier()


def nsa_phase(P, kb, l):
    nc = P.nc
    W = P.W
    with ExitStack() as es:
        qT, qT_b = kb.sb(es, "qT", [128, 8, S], BF16)
        for h in range(8):
            kb.dma("sp", qT[:, h, :], P.qT_d[h], writes=[qT_b])
        gates, gates_b = kb.sb(es, "gates", [128, 16, 24], F32)
        kb.dma("sp", gates[:], P.gate_d.rearrange("(st p) c -> p st c", p=128), writes=[gates_b])
        sc_ps = Rot([kb.ps(es, f"scps{i}", [128, 512], F32) for i in range(2)])
        o_ps = [kb.ps(es, f"ops{i}", [128, 512], F32) for i in range(4)]
        m_ps = Rot([kb.ps(es, f"mps{i}", [128, 512], F32) for i in range(2)])
        Ebf = Rot([kb.sb(es, f"Ebf{i}", [128, 512], BF16) for i in range(3)])
        Ef = Rot([kb.sb(es, f"Ef{i}", [128, 512], F32) for i in range(2)])
        acc, acc_b = kb.sb(es, "acc", [128, 4, 512], F32)
        small = Rot([kb.sb(es, f"sm{i}", [128, 8], F32) for i in range(8)])
        imp, imp_b = kb.sb(es, "imp", [128, 4, 32], F32)
        score, score_b = kb.sb(es, "score", [128, 4, 32], F32)
        wk, wk_b = kb.sb(es, "wk", [128, 4, 32], F32)
        m8, m8_b = kb.sb(es, "m8", [128, 4, 16], F32)
        selm, selm_b = kb.sb(es, "selm", [128, 4, 128], BF16)
        selbT, selbT_b = kb.sb(es, "selbT", [128, 512], BF16)
        ostg = Rot([kb.sb(es, f"ostg{i}", [128, 512], BF16) for i in range(2)])
        ksT, ksT_b = kb.sb(es, "ksT", [128, S], BF16)
        kwT, kwT_b = kb.sb(es, "kwT", [128, S], BF16)
        kcT_in, kcT_in_b = kb.sb(es, "kcTin", [128, S], BF16)
        vs_aug, vs_aug_b = kb.sb(es, "vsaug", [128, 16, 130], BF16)
        vw_aug, vw_aug_b = kb.sb(es, "vwaug", [128, 16, 130], BF16)
        w1, w1_b = kb.sb(es, "w1", [128, 32, 256], BF16)
        w2, w2_b = kb.sb(es, "w2", [128, 2, 128], BF16)
        pe_t, pe_b = kb.sb(es, "peT", [128, 32], F32)
        pe_raw, pe_raw_b = kb.sb(es, "peraw", [128, 128], F32)
        X, X_b = kb.sb(es, "X", [128, 32, 128], BF16)
        hid = [kb.sb(es, f"hid{i}", [128, 128], F32) for i in range(3)]
        gT = [kb.sb(es, f"gT{i}", [128, 128], BF16) for i in range(2)]
        kcT, kcT_b = kb.sb(es, "kcT", [128, 128], BF16)
        vc_aug, vc_aug_b = kb.sb(es, "vcaug", [128, 164], F32)
        Mc = [kb.sb(es, f"Mc{i}", [128, S], BF16) for i in range(4)]
        Ms = [kb.sb(es, f"Ms{i}", [128, WS], BF16) for i in range(4)]
        Mw = [kb.sb(es, f"Mw{i}", [128, WW], BF16) for i in range(4)]
        kb.op("pool", lambda e: e.memset(selm[:], 0.0), writes=[selm_b])
        kb.op("pool", lambda e: e.memset(pe_raw[:], 0.0), writes=[pe_raw_b])
        kb.op("pool", lambda e: e.memset(vs_aug[:, :, 128:130], 1.0), writes=[vs_aug_b])
        kb.op("pool", lambda e: e.memset(vw_aug[:, :, 128:130], 1.0), writes=[vw_aug_b])

        for g in range(2):
            kb.dma("sp", kcT_in[:], P.kvT_d[0, g], writes=[kcT_in_b])
            kb.dma("sp", ksT[:], P.kvT_d[2, g], writes=[ksT_b])
            kb.dma("sp", kwT[:], P.kvT_d[3, g], writes=[kwT_b])
            kb.dma("sp", vs_aug[:, :, 0:128], P.vtok_d[0, :, g * 128:(g + 1) * 128].rearrange("(kt p) d -> p kt d", p=128),
                   writes=[vs_aug_b])
            kb.dma("sp", vw_aug[:, :, 0:128], P.vtok_d[1, :, g * 128:(g + 1) * 128].rearrange("(kt p) d -> p kt d", p=128),
                   writes=[vw_aug_b])
            for j in range(4):
                h = g * 4 + j
                kb.dma("sp", Mc[j][0][:], P.Mc_d[h], writes=[Mc[j][1]])
                kb.dma("sp", Ms[j][0][:], P.Ms_d[h], writes=[Ms[j][1]])
                kb.dma("sp", Mw[j][0][:], P.Mw_d[h], writes=[Mw[j][1]])
            for kv in range(2):
                if kv == 1:
                    kb.dma("sp", kcT_in[:], P.kvT_d[1, g], writes=[kcT_in_b])
                w1n = "cmp_w1_k" if kv == 0 else "cmp_w1_v"
                w2n = "cmp_w2_k" if kv == 0 else "cmp_w2_v"
                pen = "cmp_pe_k" if kv == 0 else "cmp_pe_v"
                kb.dma("pool", w1[:], W[w1n][l].rearrange("(p d) h -> d p h", d=128), writes=[w1_b])
                kb.dma("pool", w2[:], W[w2n][l].rearrange("(c p) d -> p c d", p=128), writes=[w2_b])
                kb.dma("sp", pe_raw[0:32, :], W[pen][l], writes=[pe_raw_b])
                pps, pps_b = m_ps.next()
                kb.op("pe", lambda e, pps=pps: e.transpose(out=pps[:, 0:128], in_=pe_raw[:], identity=P.ident_f[:]),
                      reads=[pe_raw_b, P.ident_f_b], writes=[pps_b])
                kb.op("act", lambda e, pps=pps: e.copy(out=pe_t[:], in_=pps[:, 0:32]), reads=[pps_b], writes=[pe_b])
                for p in range(32):
                    src = kcT_in[:, p:p + 16 * 126 + 1:16]
                    kb.op("dve", lambda e, p=p, src=src: e.tensor_scalar(out=X[:, p, 0:127], in0=src, scalar1=pe_t[:, p:p + 1],
                                                                        scalar2=None, op0=ALU.add),
                          reads=[kcT_in_b, pe_b], writes=[X_b])
                for c in range(2):
                    ps, ps_b = m_ps.next()
                    for p in range(32):
                        kb.op("pe", lambda e, p=p, c=c, ps=ps: e.matmul(ps[:, 0:127], lhsT=w1[:, p, c * 128:(c + 1) * 128],
                                                                       rhs=X[:, p, 0:127], start=(p == 0), stop=(p == 31)),
                              reads=[w1_b, X_b], writes=[ps_b], sig=(p == 31))
                    (x_, x_b), (t_, t_b), (u_, u_b) = hid
                    kb.op("act", lambda e, ps=ps: e.copy(out=x_[:, 0:127], in_=ps[:, 0:127]), reads=[ps_b], writes=[x_b])
                    kb.op("dve", lambda e: e.tensor_tensor(out=t_[:, 0:127], in0=x_[:, 0:127], in1=x_[:, 0:127], op=ALU.mult),
                          reads=[x_b], writes=[t_b])
                    kb.op("dve", lambda e: e.tensor_scalar(out=t_[:, 0:127], in0=t_[:, 0:127], scalar1=0.044715, scalar2=1.0,
                                                           op0=ALU.mult, op1=ALU.add), reads=[t_b], writes=[t_b])
                    kb.op("dve", lambda e: e.tensor_tensor(out=t_[:, 0:127], in0=t_[:, 0:127], in1=x_[:, 0:127], op=ALU.mult),
                          reads=[t_b, x_b], writes=[t_b])
                    kb.op("act", lambda e: e.activation(out=u_[:, 0:127], in_=t_[:, 0:127], func=AF.Tanh,
                                                        scale=0.7978845608028654), reads=[t_b], writes=[u_b])
                    kb.op("dve", lambda e: e.tensor_scalar(out=u_[:, 0:127], in0=u_[:, 0:127], scalar1=1.0, scalar2=0.5,
                                                           op0=ALU.add, op1=ALU.mult), reads=[u_b], writes=[u_b])
                    kb.op("dve", lambda e, c=c: e.tensor_tensor(out=gT[c][0][:, 0:127], in0=u_[:, 0:127], in1=x_[:, 0:127],
                                                                op=ALU.mult), reads=[u_b, x_b], writes=[gT[c][1]])
                ps, ps_b = m_ps.next()
                if kv == 0:
                    for c in range(2):
                        kb.op("pe", lambda e, c=c, ps=ps: e.matmul(ps[:, 0:127], lhsT=w2[:, c, :], rhs=gT[c][0][:, 0:127],
                                                                  start=(c == 0), stop=(c == 1)),
                              reads=[w2_b, gT[c][1]], writes=[ps_b], sig=(c == 1))
                    kb.op("pool", lambda e: e.memset(kcT[:], 0.0), writes=[kcT_b])
                    kb.op("act", lambda e, ps=ps: e.copy(out=kcT[:, 0:127], in_=ps[:, 0:127]), reads=[ps_b, kcT_b], writes=[kcT_b])
                else:
                    for c in range(2):
                        kb.op("pe", lambda e, c=c, ps=ps: e.matmul(ps[0:127, 0:128], lhsT=gT[c][0][:, 0:127], rhs=w2[:, c, :],
                                                                  start=(c == 0), stop=(c == 1)),
                              reads=[w2_b, gT[c][1]], writes=[ps_b], sig=(c == 1))
                    kb.op("pool", lambda e: e.memset(vc_aug[:], 0.0), writes=[vc_aug_b])
                    kb.op("pool", lambda e: e.memset(vc_aug[:, 128:129], 1.0), reads=[vc_aug_b], writes=[vc_aug_b])
                    kb.op("act", lambda e, ps=ps: e.copy(out=vc_aug[0:127, 0:128], in_=ps[0:127, 0:128]),
                          reads=[ps_b, vc_aug_b], writes=[vc_aug_b])
                    kb.op("dve", lambda e: e.tensor_copy(out=vc_aug[:, 129:161], in_=P.cover[:]),
                          reads=[P.cover_b, vc_aug_b], writes=[vc_aug_b])
            if P.dbg_nsa is not None and g == 0:
                kb.dma("sp", P.dbg_nsa["kcT"], kcT[:], reads=[kcT_b])
                kb.dma("sp", P.dbg_nsa["vc"], vc_aug[:], reads=[vc_aug_b])

            for qt in range(4):
                t0 = qt * 512
                def cmp_scores(j):
                    h = g * 4 + j
                    ps, ps_b = sc_ps.next()
                    kb.op("pe", lambda e: e.matmul(ps[:], lhsT=kcT[:], rhs=qT[:, h, t0:t0 + 512], start=True, stop=False),
                          reads=[kcT_b, qT_b], writes=[ps_b], sig=False)
                    kb.op("pe", lambda e: e.matmul(ps[:], lhsT=P.ident_b[:], rhs=Mc[j][0][:, t0:t0 + 512], start=False, stop=True),
                          reads=[P.ident_b_b, Mc[j][1]], writes=[ps_b])
                    ef, ef_b = Ef.next()
                    kb.op("act", lambda e: e.activation(out=ef[:], in_=ps[:], func=AF.Exp, scale=SCALE),
                          reads=[ps_b], writes=[ef_b])
                    return ef, ef_b

                def cmp_pv(j, ef, ef_b):
                    h = g * 4 + j
                    for sub in range(4):
                        op_, op_b = o_ps[sub]
                        st = qt * 4 + sub
                        kb.op("pe", lambda e, op_=op_, sub=sub: e.matmul(op_[:, 0:161], lhsT=ef[:, sub * 128:(sub + 1) * 128],
                                                                        rhs=vc_aug[:, 0:161], start=True, stop=True),
                              reads=[ef_b, vc_aug_b], writes=[op_b])
                        sm, sm_b = small.next()
                        kb.op("dve", lambda e, sm=sm, op_=op_: e.tensor_scalar(out=sm[:, 0:1], in0=op_[:, 128:129], scalar1=1e-30,
                                                                              scalar2=None, op0=ALU.max), reads=[op_b], writes=[sm_b])
                        kb.op("dve", lambda e, sm=sm: e.reciprocal(out=sm[:, 1:2], in_=sm[:, 0:1]), reads=[sm_b], writes=[sm_b])
                        kb.op("dve", lambda e, sm=sm, st=st: e.tensor_tensor(out=sm[:, 2:3], in0=sm[:, 1:2],
                                                                            in1=gates[:, st, h * 3:h * 3 + 1], op=ALU.mult),
                              reads=[sm_b, gates_b], writes=[sm_b])
                        kb.op("dve", lambda e, sm=sm, op_=op_, sub=sub: e.tensor_scalar(
                            out=acc[:, sub, j * 128:(j + 1) * 128], in0=op_[:, 0:128], scalar1=sm[:, 2:3], scalar2=None, op0=ALU.mult),
                            reads=[op_b, sm_b, acc_b], writes=[acc_b])
                        if j == 0:
                            kb.op("dve", lambda e, sm=sm, op_=op_, sub=sub: e.tensor_scalar(
                                out=imp[:, sub, :], in0=op_[:, 129:161], scalar1=sm[:, 1:2], scalar2=None, op0=ALU.mult),
                                reads=[op_b, sm_b, imp_b], writes=[imp_b])
                        else:
                            kb.op("dve", lambda e, sm=sm, op_=op_, sub=sub: e.scalar_tensor_tensor(
                                out=imp[:, sub, :], in0=op_[:, 129:161], scalar=sm[:, 1:2], in1=imp[:, sub, :],
                                op0=ALU.mult, op1=ALU.add), reads=[op_b, sm_b, imp_b], writes=[imp_b])

                cpend = None
                for j in range(4):
                    ef, ef_b = cmp_scores(j)
                    if cpend is not None:
                        cmp_pv(*cpend)
                    cpend = (j, ef, ef_b)
                cmp_pv(*cpend)
                kb.op("dve", lambda e: e.tensor_tensor(out=score[:], in0=imp[:], in1=P.FB[:, qt * 4:qt * 4 + 4, :], op=ALU.add),
                      reads=[imp_b, P.FB_b], writes=[score_b])
                sp_, sp_b = m_ps.next()
                for sub in range(4):
                    kb.op("dve", lambda e, sub=sub: e.max(out=m8[:, sub, 0:8], in_=score[:, sub, :]), reads=[score_b, m8_b], writes=[m8_b])
                    kb.op("dve", lambda e, sub=sub: e.match_replace(out=wk[:, sub, :], in_to_replace=m8[:, sub, 0:8],
                                                                    in_values=score[:, sub, :], imm_value=-1e30),
                          reads=[score_b, m8_b, wk_b], writes=[wk_b])
                    kb.op("dve", lambda e, sub=sub: e.max(out=m8[:, sub, 8:16], in_=wk[:, sub, :]), reads=[wk_b, m8_b], writes=[m8_b])
                    kb.op("dve", lambda e, sub=sub: e.tensor_scalar(out=m8[:, sub, 15:16], in0=m8[:, sub, 15:16], scalar1=-1e29,
                                                                    scalar2=None, op0=ALU.max), reads=[m8_b], writes=[m8_b])
                    kb.op("dve", lambda e, sub=sub: e.tensor_scalar(out=selm[:, sub, 0:32], in0=score[:, sub, :], scalar1=m8[:, sub, 15:16],
                                                                    scalar2=1.0, op0=ALU.is_ge, op1=ALU.subtract),
                          reads=[score_b, m8_b, selm_b], writes=[selm_b])
                    kb.op("pe", lambda e, sub=sub: e.matmul(sp_[:, sub * 128:(sub + 1) * 128], lhsT=selm[:, sub, :], rhs=P.I30k[:],
                                                           start=True, stop=True),
                          reads=[selm_b, P.I30k_b], writes=[sp_b])
                kb.op("act", lambda e: e.copy(out=selbT[:], in_=sp_[:]), reads=[sp_b], writes=[selbT_b])
                if P.dbg_nsa is not None and g == 0:
                    kb.dma("sp", P.dbg_nsa["imp"][:, qt * 4:qt * 4 + 4, :], imp[:], reads=[imp_b])
                    kb.dma("sp", P.dbg_nsa["selbT"][:, t0:t0 + 512], selbT[0:32, :], reads=[selbT_b])

                def emit_scores(br, j, ki, kt):
                    h = g * 4 + j
                    dlt = t0 - kt * 128
                    ps, ps_b = sc_ps.next()
                    kT_, kT_b_ = (ksT, ksT_b) if br == 0 else (kwT, kwT_b)
                    const_bias = (br == 0 and dlt >= 256)
                    kb.op("pe", lambda e: e.matmul(ps[:], lhsT=kT_[:, kt * 128:(kt + 1) * 128], rhs=qT[:, h, t0:t0 + 512],
                                                   start=True, stop=False),
                          reads=[kT_b_, qT_b], writes=[ps_b], sig=False)
                    if br == 0:
                        kb.op("pe", lambda e: e.matmul(ps[:], lhsT=P.expall[:, kt * 128:(kt + 1) * 128], rhs=selbT[:],
                                                       start=False, stop=const_bias),
                              reads=[P.expall_b, selbT_b], writes=[ps_b], sig=const_bias)
                        if not const_bias:
                            kb.op("pe", lambda e: e.matmul(ps[:], lhsT=P.ident_b[:], rhs=Ms[j][0][:, dlt + OFFS:dlt + OFFS + 512],
                                                           start=False, stop=True),
                                  reads=[P.ident_b_b, Ms[j][1]], writes=[ps_b])
                    else:
                        kb.op("pe", lambda e: e.matmul(ps[:], lhsT=P.ident_b[:], rhs=Mw[j][0][:, dlt + OFFW:dlt + OFFW + 512],
                                                       start=False, stop=True),
                              reads=[P.ident_b_b, Mw[j][1]], writes=[ps_b])
                    eb, eb_b = Ebf.next()
                    if const_bias:
                        kb.op("act", lambda e: e.activation(out=eb[:], in_=ps[:], func=AF.Exp, scale=SCALE, bias=P.t31[:, h:h + 1]),
                              reads=[ps_b, P.t31_b], writes=[eb_b])
                    else:
                        kb.op("act", lambda e: e.activation(out=eb[:], in_=ps[:], func=AF.Exp, scale=SCALE),
                              reads=[ps_b], writes=[eb_b])
                    return eb, eb_b

                def emit_pv(br, j, ki, kt, nk, eb, eb_b):
                    h = g * 4 + j
                    v_, v_b_ = (vs_aug, vs_aug_b) if br == 0 else (vw_aug, vw_aug_b)
                    for sub in range(4):
                        op_, op_b = o_ps[sub]
                        kb.op("pe", lambda e, op_=op_, sub=sub: e.matmul(op_[:, 0:129], lhsT=eb[:, sub * 128:(sub + 1) * 128],
                                                                        rhs=v_[:, kt, 0:129], start=(ki == 0), stop=(ki == nk - 1)),
                              reads=[eb_b, v_b_], writes=[op_b], sig=(sub == 3))
                    if ki == nk - 1:
                        for sub in range(4):
                            op_, op_b = o_ps[sub]
                            st = qt * 4 + sub
                            sm, sm_b = small.next()
                            kb.op("dve", lambda e, sm=sm, op_=op_: e.reciprocal(out=sm[:, 1:2], in_=op_[:, 128:129]),
                                  reads=[op_b], writes=[sm_b])
                            kb.op("dve", lambda e, sm=sm, st=st: e.tensor_tensor(
                                out=sm[:, 2:3], in0=sm[:, 1:2], in1=gates[:, st, h * 3 + 1 + br:h * 3 + 2 + br], op=ALU.mult),
                                reads=[sm_b, gates_b], writes=[sm_b])
                            kb.op("dve", lambda e, sm=sm, op_=op_, sub=sub: e.scalar_tensor_tensor(
                                out=acc[:, sub, j * 128:(j + 1) * 128], in0=op_[:, 0:128], scalar=sm[:, 2:3],
                                in1=acc[:, sub, j * 128:(j + 1) * 128], op0=ALU.mult, op1=ALU.add),
                                reads=[op_b, sm_b, acc_b], writes=[acc_b])

                tiles = []
                for br in range(2):
                    for j in range(4):
                        if br == 0:
                            kts = list(range(0, (t0 + 511) // 128 + 1))
                        else:
                            kts = list(range(max(0, t0 // 128 - 4), t0 // 128 + 4))
                        for ki, kt in enumerate(kts):
                            tiles.append((br, j, ki, kt, len(kts)))
                pend = None
                for (br, j, ki, kt, nk) in tiles:
                    eb, eb_b = emit_scores(br, j, ki, kt)
                    if pend is not None:
                        emit_pv(*pend)
                    pend = (br, j, ki, kt, nk, eb, eb_b)
                emit_pv(*pend)
                for j in range(4):
                    h = g * 4 + j
                    tp, tp_b = m_ps.next()
                    for sub in range(4):
                        kb.op("pe", lambda e, tp=tp, sub=sub, j=j: e.transpose(out=tp[:, sub * 128:(sub + 1) * 128],
                                                                             in_=acc[:, sub, j * 128:(j + 1) * 128],
                                                                             identity=P.ident_f[:]),
                              reads=[acc_b, P.ident_f_b], writes=[tp_b], sig=(sub == 3))
                    og, og_b = ostg.next()
                    kb.op("act", lambda e, tp=tp, og=og: e.copy(out=og[:], in_=tp[:]), reads=[tp_b], writes=[og_b])
                    kb.dma("sp", P.oaT_d[h, :, t0:t0 + 512], og[:], reads=[og_b])
        kb.barrier()


def gdn_setup(P, kb, es):
    P.U_f, P.U_b = kb.sb(es, "U_f", [128, 128], F32)
    P.ones_f, P.ones_b = kb.sb(es, "ones_f", [128, 128], F32)
    P.mpos, P.mpos_b = kb.sb(es, "mpos", [128, 128], F32)
    P.mneg, P.mneg_b = kb.sb(es, "mneg", [128, 128], F32)
    kb.op("pool", lambda e: e.memset(P.ones_f[:], 1.0), writes=[P.ones_b])
    kb.op("pool", lambda e: e.memset(P.U_f[:], 1.0), writes=[P.U_b])
    kb.op("pool", lambda e: e.affine_select(out=P.U_f[:], in_=P.U_f[:], pattern=[[1, 128]], compare_op=ALU.is_ge, fill=0.0,
                                            base=0, channel_multiplier=-1), reads=[P.U_b], writes=[P.U_b])
    kb.op("pool", lambda e: e.memset(P.mpos[:], 0.0), writes=[P.mpos_b])
    kb.op("pool", lambda e: e.affine_select(out=P.mpos[:], in_=P.mpos[:], pattern=[[-1, 128]], compare_op=ALU.is_gt, fill=1e4,
                                            base=0, channel_multiplier=1), reads=[P.mpos_b], writes=[P.mpos_b])
    kb.op("pool", lambda e: e.memset(P.mneg[:], 0.0), writes=[P.mneg_b])
    kb.op("pool", lambda e: e.affine_select(out=P.mneg[:], in_=P.mneg[:], pattern=[[1, 128]], compare_op=ALU.is_ge, fill=-1e4,
                                            base=0, channel_multiplier=-1), reads=[P.mneg_b], writes=[P.mneg_b])


GDN_K = 3
GDN_TEST_HEADS = 1


def gdn_phase(P, kb, l):
    nc = P.nc
    W = P.W
    NCH = S // 128
    with ExitStack() as es:
        ab, ab_b = kb.sb(es, "ab", [128, NCH, 16], F32)
        kb.dma("sp", ab[:], P.ab_d.rearrange("(n p) c -> p n c", p=128), writes=[ab_b])
        dtb, dtb_b = kb.sb(es, "dtb", [128, 8], F32)
        nA, nA_b = kb.sb(es, "nA", [128, 8], F32)
        nw, nw_b = kb.sb(es, "nw", [128, 128], F32)
        mhalf, mhalf_b = kb.sb(es, "mhalf", [128, 1], F32)
        kb.op("pool", lambda e: e.memset(mhalf[:], -0.5), writes=[mhalf_b])
        kb.dma("sp", dtb[:], bass.AP(tensor=W["dt_bias"].tensor, offset=W["dt_bias"][l].offset, ap=[[0, 128], [1, 8]]), writes=[dtb_b])
        kb.dma("sp", nA[:], bass.AP(tensor=W["a_log"].tensor, offset=W["a_log"][l].offset, ap=[[0, 128], [1, 8]]), writes=[nA_b])
        kb.dma("sp", nw[:], bass.AP(tensor=W["gdn_norm_w"].tensor, offset=W["gdn_norm_w"][l].offset, ap=[[0, 128], [1, 128]]),
               writes=[nw_b])
        kb.op("act", lambda e: e.activation(out=nA[:], in_=nA[:], func=AF.Exp), reads=[nA_b], writes=[nA_b])
        kb.op("dve", lambda e: e.tensor_scalar(out=nA[:], in0=nA[:], scalar1=-1.0, scalar2=None, op0=ALU.mult), reads=[nA_b], writes=[nA_b])
        graw, graw_b = kb.sb(es, "graw", [128, NCH, 8], F32)
        beta, beta_b = kb.sb(es, "beta", [128, NCH, 8], F32)
        nbeta, nbeta_b = kb.sb(es, "nbeta", [128, NCH, 8], F32)
        gcol, gcol_b = kb.sb(es, "gcol", [128, NCH, 8], F32)
        eg, eg_b = kb.sb(es, "eg", [128, NCH, 8], F32)
        bg, bg_b = kb.sb(es, "bg", [128, NCH, 8], F32)
        for n in range(NCH):
            kb.op("dve", lambda e, n=n: e.tensor_tensor(out=graw[:, n, :], in0=ab[:, n, 0:8], in1=dtb[:], op=ALU.add),
                  reads=[ab_b, dtb_b, graw_b], writes=[graw_b])
        kb.op("act", lambda e: e.activation(out=graw[:], in_=graw[:], func=AF.Exp), reads=[graw_b], writes=[graw_b])
        kb.op("act", lambda e: e.activation(out=graw[:], in_=graw[:], func=AF.Ln, bias=P.one_t[:, 0:1]), reads=[graw_b, P.one_b], writes=[graw_b])
        for n in range(NCH):
            kb.op("dve", lambda e, n=n: e.tensor_tensor(out=graw[:, n, :], in0=graw[:, n, :], in1=nA[:], op=ALU.mult),
                  reads=[graw_b, nA_b], writes=[graw_b])
        kb.op("act", lambda e: e.activation(out=beta[:], in_=ab[:, :, 8:16], func=AF.Sigmoid), reads=[ab_b], writes=[beta_b])
        kb.op("dve", lambda e: e.tensor_scalar(out=nbeta[:], in0=beta[:], scalar1=-1.0, scalar2=None, op0=ALU.mult),
              reads=[beta_b], writes=[nbeta_b])
        bank = [kb.ps(es, f"gbank{i}", [128, 4, 128], F32) for i in range(6)]
        Q = Rot([(bank[i][0][:, 0, :], bank[i][1]) for i in range(6)])
        big = Rot([kb.ps(es, f"gbig{i}", [128, 512], F32) for i in range(2)])
        for n in range(NCH):
            pq, pq_b = Q.next()
            kb.op("pe", lambda e, n=n, pq=pq: e.matmul(pq[:, 0:8], lhsT=P.U_f[:], rhs=graw[:, n, :], start=True, stop=True),
                  reads=[P.U_b, graw_b], writes=[pq_b])
            kb.op("act", lambda e, n=n, pq=pq: e.copy(out=gcol[:, n, :], in_=pq[:, 0:8]), reads=[pq_b, gcol_b], writes=[gcol_b])
        kb.op("act", lambda e: e.activation(out=eg[:], in_=gcol[:], func=AF.Exp), reads=[gcol_b], writes=[eg_b])
        kb.op("dve", lambda e: e.tensor_tensor(out=bg[:], in0=eg[:], in1=beta[:], op=ALU.mult), reads=[eg_b, beta_b], writes=[bg_b])
        cw, cw_b = kb.sb(es, "cw", [128, 24, 4], F32)
        with ExitStack() as es_c:
            cwraw, cwraw_b = kb.sb(es_c, "cwraw", [128, 3072], F32)
            kb.op("pool", lambda e: e.memset(cwraw[:], 0.0), writes=[cwraw_b])
            kb.dma("sp", cwraw[0:4, :], W["conv_w"][l], reads=[cwraw_b], writes=[cwraw_b])
            for c in range(24):
                pq, pq_b = Q.next()
                kb.op("pe", lambda e, pq=pq, c=c: e.transpose(out=pq, in_=cwraw[:, c * 128:(c + 1) * 128], identity=P.ident_f[:]),
                      reads=[cwraw_b, P.ident_f_b], writes=[pq_b])
                kb.op("act", lambda e, pq=pq, c=c: e.copy(out=cw[:, c, :], in_=pq[:, 0:4]), reads=[pq_b, cw_b], writes=[cw_b])
            kb.barrier()
        xr = [kb.sb(es, f"xr{i}", [128, S], F32) for i in range(3)]
        sqt, sqt_b = kb.sb(es, "sqt", [128, S], F32)
        rn, rn_b = kb.sb(es, "rn", [128, 512], F32)

        def slot_bufs(k):
            B = {}
            B["xc"] = [kb.sb(es, f"xc{k}_{i}", [128, S], F32) for i in range(3)]
            B["tok"] = [kb.sb(es, f"tok{k}_{i}", [128, 128], F32) for i in range(3)]
            for nm, shp in [("vbk", [128, 256]), ("kd", [128, 128]), ("dg", [128, 128]), ("gd", [128, 128]), ("e1", [128, 128]),
                            ("e2", [128, 128]), ("qkt", [128, 128]), ("u", [128, 128]), ("nwk", [128, 128]), ("zt", [128, 128]),
                            ("xm", [128, 128]), ("zn", [128, 128]), ("yy", [128, 128]), ("junk", [128, 128])]:
                B[nm] = kb.sb(es, f"{nm}{k}", shp, F32)
            for nm in ["Pm", "PTm", "IPm", "Rm", "St", "ztok"]:
                B[nm] = Rot([kb.sb(es, f"{nm}{k}_{i}", [128, 128], F32) for i in range(2)])
            B["sm"] = Rot([kb.sb(es, f"gsm{k}_{i}", [128, 8], F32) for i in range(4)])
            B["og"] = Rot([kb.sb(es, f"gost{k}_{i}", [128, 512], BF16) for i in range(2)])
            B["zna"] = kb.sb(es, f"zna{k}", [128, NCH, 128], F32)
            return B

        def head_gen(h, B):
            xc = B["xc"]
            tok = B["tok"]
            vbk, vbk_b = B["vbk"]; kd, kd_b = B["kd"]; dg, dg_b = B["dg"]; gd, gd_b = B["gd"]
            e1, e1_b = B["e1"]; e2, e2_b = B["e2"]; qkt, qkt_b = B["qkt"]; u_, u_b = B["u"]; nwk, nwk_b = B["nwk"]
            zt, zt_b = B["zt"]; xm, xm_b = B["xm"]; zn, zn_b = B["zn"]; yy, yy_b = B["yy"]; junk, junk_b = B["junk"]
            Pm, PTm, IPm, Rm, St, ztok, sm, ogr = B["Pm"], B["PTm"], B["IPm"], B["Rm"], B["St"], B["ztok"], B["sm"], B["og"]
            for i in range(3):
                c = i * 8 + h
                kb.dma("sp", xr[i][0][:], P.gqkvT_d[c], writes=[xr[i][1]])
                x_, x_b = xr[i]
                y_, y_b = xc[i]
                kb.op("dve", lambda e, x_=x_, y_=y_, c=c: e.tensor_scalar(out=y_[:], in0=x_[:], scalar1=cw[:, c, 3:4], scalar2=None,
                                                                         op0=ALU.mult), reads=[x_b, cw_b], writes=[y_b])
                for sft in range(1, 4):
                    kb.op("dve", lambda e, x_=x_, y_=y_, c=c, sft=sft: e.scalar_tensor_tensor(
                        out=y_[:, sft:S], in0=x_[:, 0:S - sft], scalar=cw[:, c, 3 - sft:4 - sft], in1=y_[:, sft:S],
                        op0=ALU.mult, op1=ALU.add), reads=[x_b, cw_b, y_b], writes=[y_b])
                kb.op("act", lambda e, y_=y_: e.activation(out=y_[:], in_=y_[:], func=AF.Silu), reads=[y_b], writes=[y_b])
                if i < 2:
                    kb.op("act", lambda e, y_=y_: e.activation(out=sqt[:], in_=y_[:], func=AF.Square), reads=[y_b], writes=[sqt_b])
                    for t0 in range(0, S, 512):
                        bp, bp_b = big.next()
                        kb.op("pe", lambda e, bp=bp, t0=t0: e.matmul(bp[:], lhsT=P.ones_f[:], rhs=sqt[:, t0:t0 + 512], start=True, stop=True),
                              reads=[P.ones_b, sqt_b], writes=[bp_b])
                        kb.op("act", lambda e, bp=bp: e.activation(out=rn[:], in_=bp[:], func=AF.Sqrt, bias=P.eps_t[:, 0:1]),
                              reads=[bp_b, P.eps_b], writes=[rn_b])
                        kb.op("dve", lambda e: e.reciprocal(out=rn[:], in_=rn[:]), reads=[rn_b], writes=[rn_b])
                        if i == 0:
                            kb.op("dve", lambda e, y_=y_, t0=t0: e.scalar_tensor_tensor(
                                out=y_[:, t0:t0 + 512], in0=rn[:], scalar=SCALE, in1=y_[:, t0:t0 + 512], op0=ALU.mult, op1=ALU.mult),
                                reads=[rn_b, y_b], writes=[y_b])
                        else:
                            kb.op("dve", lambda e, y_=y_, t0=t0: e.tensor_tensor(out=y_[:, t0:t0 + 512], in0=rn[:], in1=y_[:, t0:t0 + 512],
                                                                               op=ALU.mult), reads=[rn_b, y_b], writes=[y_b])
            zna, zna_b = B["zna"]
            kb.dma("sp", zna[:], P.z_d[:, h * 128:(h + 1) * 128].rearrange("(n p) d -> p n d", p=128), writes=[zna_b])
            for n in range(NCH):
                kb.op("pool", lambda e, n=n: e.tensor_tensor(out=zna[:, n, :], in0=zna[:, n, :], in1=nw[:], op=ALU.mult),
                      reads=[zna_b, nw_b], writes=[zna_b])
            yield
            qTn, qTn_b = xc[0]
            kTn, kTn_b = xc[1]
            S_, S_b = St.next()
            kb.op("pool", lambda e, S_=S_: e.memset(S_[:], 0.0), writes=[S_b])
            og, og_b = None, None
            for n in range(NCH):
                cs = slice(n * 128, (n + 1) * 128)
                (q_tok, q_tok_b), (k_tok, k_tok_b), (v_tok, v_tok_b) = tok
                for i in range(3):
                    pq, pq_b = Q.next()
                    kb.op("pe", lambda e, pq=pq, i=i: e.transpose(out=pq, in_=xc[i][0][:, cs], identity=P.ident_f[:]),
                          reads=[xc[i][1], P.ident_f_b], writes=[pq_b])
                    if i < 2:
                        kb.op("act", lambda e, pq=pq, i=i: e.copy(out=tok[i][0][:], in_=pq), reads=[pq_b], writes=[tok[i][1]])
                    if i == 1:
                        kb.op("act", lambda e, pq=pq: e.activation(out=vbk[:, 128:256], in_=pq, func=AF.Copy, scale=bg[:, n, h:h + 1]),
                              reads=[pq_b, bg_b, vbk_b], writes=[vbk_b])
                    if i == 2:
                        kb.op("act", lambda e, pq=pq: e.activation(out=vbk[:, 0:128], in_=pq, func=AF.Copy, scale=beta[:, n, h:h + 1]),
                              reads=[pq_b, beta_b, vbk_b], writes=[vbk_b])
                yield
                kb.op("act", lambda e: e.activation(out=dg[:], in_=P.ident_f[:], func=AF.Copy, scale=eg[:, n, h:h + 1]),
                      reads=[P.ident_f_b, eg_b], writes=[dg_b])
                kb.op("act", lambda e: e.activation(out=gd[:], in_=P.ident_f[:], func=AF.Copy, scale=gcol[:, n, h:h + 1]),
                      reads=[P.ident_f_b, gcol_b], writes=[gd_b])
                pg, pg_b = Q.next()
                kb.op("pe", lambda e, pg=pg: e.matmul(pg, lhsT=P.ones_f[:], rhs=gd[:], start=True, stop=True),
                      reads=[P.ones_b, gd_b], writes=[pg_b])
                kb.op("dve", lambda e, pg=pg: e.scalar_tensor_tensor(out=e1[:], in0=pg, scalar=gcol[:, n, h:h + 1], in1=P.mpos[:],
                                                                    op0=ALU.subtract, op1=ALU.max),
                      reads=[pg_b, gcol_b, P.mpos_b], writes=[e1_b])
                kb.op("dve", lambda e, pg=pg: e.scalar_tensor_tensor(out=e2[:], in0=pg, scalar=gcol[:, n, h:h + 1], in1=P.mneg[:],
                                                                    op0=ALU.subtract, op1=ALU.min),
                      reads=[pg_b, gcol_b, P.mneg_b], writes=[e2_b])
                s1, s1_b = sm.next()
                kb.op("dve", lambda e, pg=pg, s1=s1: e.tensor_copy(out=s1[:, 0:1], in_=pg[:, 127:128]), reads=[pg_b], writes=[s1_b])
                kb.op("act", lambda e: e.activation(out=e1[:], in_=e1[:], func=AF.Exp, scale=-1.0), reads=[e1_b], writes=[e1_b])
                kb.op("act", lambda e: e.activation(out=e2[:], in_=e2[:], func=AF.Exp), reads=[e2_b], writes=[e2_b])
                kb.op("act", lambda e, s1=s1: e.activation(out=s1[:, 1:2], in_=gcol[:, n, h:h + 1], func=AF.Exp, scale=-1.0, bias=s1[:, 0:1]),
                      reads=[s1_b, gcol_b], writes=[s1_b])
                kb.op("act", lambda e, s1=s1: e.activation(out=s1[:, 2:3], in_=s1[:, 0:1], func=AF.Exp), reads=[s1_b], writes=[s1_b])
                kb.op("act", lambda e, s1=s1: e.activation(out=kd[:], in_=k_tok[:], func=AF.Copy, scale=s1[:, 1:2]),
                      reads=[k_tok_b, s1_b], writes=[kd_b])
                yield
                pk, pk_b = Q.next()
                kb.op("pe", lambda e, pk=pk: e.matmul(pk, lhsT=kTn[:, cs], rhs=kTn[:, cs], start=True, stop=True),
                      reads=[kTn_b], writes=[pk_b])
                P0, P0_b = Pm.next()
                kb.op("dve", lambda e, pk=pk, P0=P0: e.scalar_tensor_tensor(out=P0[:], in0=pk, scalar=nbeta[:, n, h:h + 1], in1=e1[:],
                                                                           op0=ALU.mult, op1=ALU.mult),
                      reads=[pk_b, nbeta_b, e1_b], writes=[P0_b])
                pk2, pk2_b = Q.next()
                kb.op("pe", lambda e, pk2=pk2: e.matmul(pk2, lhsT=kTn[:, cs], rhs=qTn[:, cs], start=True, stop=True),
                      reads=[kTn_b, qTn_b], writes=[pk2_b])
                kb.op("dve", lambda e, pk2=pk2: e.tensor_tensor(out=qkt[:], in0=pk2, in1=e2[:], op=ALU.mult),
                      reads=[pk2_b, e2_b], writes=[qkt_b])
                pt_, pt_b = Q.next()
                kb.op("pe", lambda e, pt_=pt_, P0=P0: e.transpose(out=pt_, in_=P0[:], identity=P.ident_f[:]),
                      reads=[P0_b, P.ident_f_b], writes=[pt_b])
                PT0, PT0_b = PTm.next()
                R0, R0_b = Rm.next()
                kb.op("dve", lambda e, pt_=pt_, PT0=PT0: e.tensor_copy(out=PT0[:], in_=pt_), reads=[pt_b], writes=[PT0_b])
                kb.op("dve", lambda e, pt_=pt_, R0=R0: e.tensor_tensor(out=R0[:], in0=pt_, in1=P.ident_f[:], op=ALU.add),
                      reads=[pt_b, P.ident_f_b], writes=[R0_b])
                yield
                Pc, Pc_b, PTc, PTc_b, Rc, Rc_b = P0, P0_b, PT0, PT0_b, R0, R0_b
                for lvl in range(1, 7):
                    pa_, pa_b = Q.next()
                    kb.op("pe", lambda e, pa_=pa_, Pc=Pc, PTc=PTc: e.matmul(pa_, lhsT=PTc[:], rhs=Pc[:], start=True, stop=True),
                          reads=[Pc_b, PTc_b], writes=[pa_b])
                    Pn, Pn_b = Pm.next()
                    kb.op("act", lambda e, pa_=pa_, Pn=Pn: e.copy(out=Pn[:], in_=pa_), reads=[pa_b], writes=[Pn_b])
                    if lvl < 6:
                        pb_, pb_b = Q.next()
                        kb.op("pe", lambda e, pb_=pb_, Pc=Pc, PTc=PTc: e.matmul(pb_, lhsT=Pc[:], rhs=PTc[:], start=True, stop=True),
                              reads=[Pc_b, PTc_b], writes=[pb_b])
                        PTn, PTn_b = PTm.next()
                        kb.op("dve", lambda e, pb_=pb_, PTn=PTn: e.tensor_copy(out=PTn[:], in_=pb_), reads=[pb_b], writes=[PTn_b])
                    yield
                    pr_, pr_b = Q.next()
                    kb.op("pe", lambda e, pr_=pr_, Pn=Pn, Rc=Rc: e.matmul(pr_, lhsT=Pn[:], rhs=Rc[:], start=True, stop=True),
                          reads=[Pn_b, Rc_b], writes=[pr_b])
                    Rn, Rn_b = Rm.next()
                    kb.op("dve", lambda e, pr_=pr_, Rn=Rn, Rc=Rc: e.tensor_tensor(out=Rn[:], in0=pr_, in1=Rc[:], op=ALU.add),
                          reads=[pr_b, Rc_b], writes=[Rn_b])
                    Rc, Rc_b = Rn, Rn_b
                    if lvl < 6:
                        Pc, Pc_b, PTc, PTc_b = Pn, Pn_b, PTn, PTn_b
                    yield
                TT, TT_b = Rc, Rc_b
                bp, bp_b = big.next()
                kb.op("pe", lambda e, bp=bp, TT=TT: e.matmul(bp[:, 0:256], lhsT=TT[:], rhs=vbk[:], start=True, stop=True),
                      reads=[TT_b, vbk_b], writes=[bp_b])
                kb.op("act", lambda e, bp=bp: e.copy(out=u_[:], in_=bp[:, 0:128]), reads=[bp_b], writes=[u_b])
                kb.op("act", lambda e, bp=bp: e.activation(out=nwk[:], in_=bp[:, 128:256], func=AF.Copy, scale=-1.0),
                      reads=[bp_b], writes=[nwk_b])
                yield
                pz, pz_b = Q.next()
                kb.op("pe", lambda e, pz=pz: e.matmul(pz, lhsT=q_tok[:], rhs=dg[:], start=True, stop=False),
                      reads=[q_tok_b, dg_b], writes=[pz_b])
                kb.op("pe", lambda e, pz=pz: e.matmul(pz, lhsT=nwk[:], rhs=qkt[:], start=False, stop=True),
                      reads=[nwk_b, qkt_b], writes=[pz_b])
                kb.op("act", lambda e, pz=pz: e.copy(out=zt[:], in_=pz), reads=[pz_b], writes=[zt_b])
                px, px_b = Q.next()
                kb.op("pe", lambda e, px=px: e.matmul(px, lhsT=nwk[:], rhs=kd[:], start=True, stop=True),
                      reads=[nwk_b, kd_b], writes=[px_b])
                kb.op("dve", lambda e, px=px, s1=s1: e.scalar_tensor_tensor(out=xm[:], in0=P.ident_f[:], scalar=s1[:, 2:3], in1=px,
                                                                           op0=ALU.mult, op1=ALU.add),
                      reads=[px_b, s1_b, P.ident_f_b], writes=[xm_b])
                yield
                po, po_b = Q.next()
                kb.op("pe", lambda e, po=po: e.matmul(po, lhsT=qkt[:], rhs=u_[:], start=True, stop=False),
                      reads=[qkt_b, u_b], writes=[po_b])
                kb.op("pe", lambda e, po=po, S_=S_: e.matmul(po, lhsT=zt[:], rhs=S_[:], start=False, stop=True),
                      reads=[zt_b, S_b], writes=[po_b])
                pS, pS_b = Q.next()
                kb.op("pe", lambda e, pS=pS: e.matmul(pS, lhsT=kd[:], rhs=u_[:], start=True, stop=False),
                      reads=[kd_b, u_b], writes=[pS_b])
                kb.op("pe", lambda e, pS=pS, S_=S_: e.matmul(pS, lhsT=xm[:], rhs=S_[:], start=False, stop=True),
                      reads=[xm_b, S_b], writes=[pS_b])
                Sn, Sn_b = St.next()
                kb.op("act", lambda e, pS=pS, Sn=Sn: e.copy(out=Sn[:], in_=pS), reads=[pS_b], writes=[Sn_b])
                S_, S_b = Sn, Sn_b
                s2, s2_b = sm.next()
                kb.op("act", lambda e, po=po, s2=s2: e.activation(out=junk[:], in_=po, func=AF.Square, accum_out=s2[:, 0:1]),
                      reads=[po_b], writes=[junk_b, s2_b])
                kb.op("act", lambda e, s2=s2: e.activation(out=s2[:, 1:2], in_=s2[:, 0:1], func=AF.Sqrt, scale=1.0 / 128, bias=P.eps_t[:, 0:1]),
                      reads=[s2_b, P.eps_b], writes=[s2_b])
                kb.op("dve", lambda e, s2=s2: e.reciprocal(out=s2[:, 2:3], in_=s2[:, 1:2]), reads=[s2_b], writes=[s2_b])
                kb.op("dve", lambda e, po=po, s2=s2: e.scalar_tensor_tensor(out=yy[:], in0=po, scalar=s2[:, 2:3], in1=zna[:, n, :],
                                                                           op0=ALU.mult, op1=ALU.mult),
                      reads=[po_b, s2_b, zna_b], writes=[yy_b])
                yield
                pT, pT_b = Q.next()
                kb.op("pe", lambda e, pT=pT: e.transpose(out=pT, in_=yy[:], identity=P.ident_f[:]),
                      reads=[yy_b, P.ident_f_b], writes=[pT_b])
                if n % 4 == 0:
                    og, og_b = ogr.next()
                kb.op("act", lambda e, pT=pT, og=og, n=n: e.copy(out=og[:, (n % 4) * 128:(n % 4 + 1) * 128], in_=pT),
                      reads=[pT_b, og_b], writes=[og_b])
                if n % 4 == 3:
                    kb.dma("sp", P.obT_d[h, :, (n - 3) * 128:(n + 1) * 128], og[:], reads=[og_b])
                yield

        slots = [slot_bufs(k) for k in range(GDN_K)]
        heads = list(range(P.gdn_heads))
        for r0 in range(0, len(heads), GDN_K):
            gens = [head_gen(h, slots[k]) for k, h in enumerate(heads[r0:r0 + GDN_K])]
            while gens:
                for g_ in list(gens):
                    try:
                        next(g_)
                    except StopIteration:
                        gens.remove(g_)
        kb.barrier()


def merge_phase(P, kb, l):
    W = P.W
    with ExitStack() as es:
        oaT, oaT_b = kb.sb(es, "oaT", [128, 8, S], BF16)
        obT, obT_b = kb.sb(es, "obT", [128, 8, S], BF16)
        for c in range(8):
            kb.dma("sp", oaT[:, c, :], P.oaT_d[c], writes=[oaT_b])
            kb.dma("sp", obT[:, c, :], P.obT_d[c], writes=[obT_b])
        pa_p = Rot([kb.sb(es, f"pap{i}", [128, 8, 512], BF16) for i in range(2)])
        pb_p = Rot([kb.sb(es, f"pbp{i}", [128, 8, 512], BF16) for i in range(2)])
        psA = Rot([kb.ps(es, f"psA{i}", [128, 512], F32) for i in range(3)])
        psB = Rot([kb.ps(es, f"psB{i}", [128, 512], F32) for i in range(3)])
        gA = Rot([kb.sb(es, f"gA{i}", [128, 512], F32) for i in range(2)])
        gB = Rot([kb.sb(es, f"gB{i}", [128, 512], F32) for i in range(2)])
        t1 = Rot([kb.sb(es, f"t1{i}", [128, 512], F32) for i in range(2)])
        t2 = Rot([kb.sb(es, f"t2{i}", [128, 512], F32) for i in range(2)])
        uo = Rot([kb.sb(es, f"uo{i}", [128, 512], BF16) for i in range(3)])
        Wa = W["w_pa"][l].rearrange("(kc p) n -> p kc n", p=128)
        Wb = W["w_pb"][l].rearrange("(kc p) n -> p kc n", p=128)
        for p0 in range(0, D, 512):
            wa, wa_b = pa_p.next()
            wb, wb_b = pb_p.next()
            kb.dma("pool", wa[:], Wa[:, :, p0:p0 + 512], writes=[wa_b])
            kb.dma("pool", wb[:], Wb[:, :, p0:p0 + 512], writes=[wb_b])
            for cb in range(0, 512, 128):
                fb = (p0 + cb) // 128
                for t0 in range(0, S, 512):
                    pA, pA_b = psA.next()
                    pB, pB_b = psB.next()
                    for kc in range(8):
                        kb.op("pe", lambda e, kc=kc, pA=pA: e.matmul(pA[:], lhsT=wa[:, kc, cb:cb + 128], rhs=oaT[:, kc, t0:t0 + 512],
                                                                    start=(kc == 0), stop=(kc == 7)),
                              reads=[wa_b, oaT_b], writes=[pA_b], sig=(kc == 7))
                    for kc in range(8):
                        kb.op("pe", lambda e, kc=kc, pB=pB: e.matmul(pB[:], lhsT=wb[:, kc, cb:cb + 128], rhs=obT[:, kc, t0:t0 + 512],
                                                                    start=(kc == 0), stop=(kc == 7)),
                              reads=[wb_b, obT_b], writes=[pB_b], sig=(kc == 7))
                    ga, ga_b = gA.next()
                    gb_, gb_b = gB.next()
                    kb.dma("act", ga[:], P.mergeT_d[fb, :, t0:t0 + 512], writes=[ga_b])
                    kb.dma("act", gb_[:], P.mergeT_d[16 + fb, :, t0:t0 + 512], writes=[gb_b])
                    a1, a1_b = t1.next()
                    a2, a2_b = t2.next()
                    kb.op("dve", lambda e, pA=pA, ga=ga, a1=a1: e.tensor_tensor(out=a1[:], in0=pA[:], in1=ga[:], op=ALU.mult),
                          reads=[pA_b, ga_b], writes=[a1_b])
                    kb.op("dve", lambda e, pB=pB, gb_=gb_, a2=a2: e.tensor_tensor(out=a2[:], in0=pB[:], in1=gb_[:], op=ALU.mult),
                          reads=[pB_b, gb_b], writes=[a2_b])
                    uu, uu_b = uo.next()
                    kb.op("pool", lambda e, a1=a1, a2=a2, uu=uu: e.tensor_tensor(out=uu[:], in0=a1[:], in1=a2[:], op=ALU.add),
                          reads=[a1_b, a2_b], writes=[uu_b])
                    kb.dma("sp", P.uT_d[fb, :, t0:t0 + 512], uu[:], reads=[uu_b])
        kb.barrier()


def load_T(P, kb, dst, dst_b, src_d, nchunks, t0, ntok):
    for c in range(nchunks):
        kb.dma("sp", dst[:, c, 0:ntok], src_d[c, :, t0:t0 + ntok], writes=[dst_b])


def residual_epi(P, kb, es, src_ap, dst_ap, tok_off):
    rt = Rot([kb.sb(es, f"rt{i}", [128, 512], F32) for i in range(3)])
    ro = Rot([kb.sb(es, f"ro{i}", [128, 512], F32) for i in range(3)])

    def epi(ps, ps_b, j0, pw, t0, tw):
        r_, r_b = rt.next()
        o_, o_b = ro.next()
        r0 = tok_off + t0
        kb.dma("act", r_[:, 0:pw], src_ap[r0:r0 + 128, j0:j0 + pw], writes=[r_b])
        kb.op("dve", lambda e: e.tensor_tensor(out=o_[:, 0:pw], in0=ps[:, 0:pw], in1=r_[:, 0:pw], op=ALU.add),
              reads=[ps_b, r_b], writes=[o_b])
        kb.dma("sp", dst_ap[r0:r0 + 128, j0:j0 + pw], o_[:, 0:pw], reads=[o_b])
    return epi


def wo_phase(P, kb, l, x_src, x_dst):
    with ExitStack() as es:
        uT, uT_b = kb.sb(es, "uT", [128, 16, S], BF16)
        load_T(P, kb, uT, uT_b, P.uT_d, 16, 0, S)
        epi = residual_epi(P, kb, es, x_src, x_dst, 0)
        P.dense(kb, uT, uT_b, 16, S, P.W["w_o"][l], [(0, D, "a", epi)])


def mlp_phase(P, kb, l, x1, x2):
    W = P.W
    TT_ = 1024
    NQ = 4
    HQ = DFF // NQ
    with ExitStack() as es:
        panels = Rot([kb.sb(es, f"mwp{i}", [128, 16, 512], BF16) for i in range(3)])
        pss = Rot([kb.ps(es, f"mlps{i}", [128, 512], F32) for i in range(4)])
        actT, actT_b = kb.sb(es, "actT", [128, 16, TT_], BF16)
        h2T, h2T_b = kb.sb(es, "h2T", [128, 16, TT_], BF16)
        nres = P.norm_res(kb, es, W["ln2_w"][l], npts=3)
        rl = Rot([kb.sb(es, f"rl{i}", [128, 512], F32) for i in range(3)])
        rt = Rot([kb.sb(es, f"rt{i}", [128, 512], F32) for i in range(3)])
        ro = Rot([kb.sb(es, f"ro{i}", [128, 512], F32) for i in range(3)])
        xb = {}

        def epi_up(ps, ps_b, j0, cw, t0, tw):
            r_, r_b = rl.next()
            kb.op("act", lambda e: e.activation(out=r_[0:cw, 0:tw], in_=ps[0:cw, 0:tw], func=AF.Relu),
                  reads=[ps_b], writes=[r_b])
            kb.op("dve", lambda e: e.tensor_tensor(out=actT[0:cw, j0 // 128, t0:t0 + tw], in0=r_[0:cw, 0:tw],
                                                   in1=r_[0:cw, 0:tw], op=ALU.mult),
                  reads=[r_b, actT_b], writes=[actT_b])

        for tt in range(S // TT_):
            tok0 = tt * TT_
            P.norm_T(kb, x1, W["ln2_w"][l], h2T, h2T_b, TT_, tok0=tok0, res=nres)
            for qd in range(NQ):
                src = x1 if qd == 0 else x2

                def epi_dn(ps, ps_b, j0, pw, t0, tw, src=src):
                    r_, r_b = rt.next()
                    o_, o_b = ro.next()
                    r0 = tok0 + t0
                    key = (r0, j0)
                    if key not in xb:
                        xb[key] = Buf(f"x2_{r0}_{j0}")
                    kb.dma("act", r_[:, 0:pw], src[r0:r0 + 128, j0:j0 + pw], reads=[xb[key]], writes=[r_b])
                    kb.op("dve", lambda e: e.tensor_tensor(out=o_[:, 0:pw], in0=ps[:, 0:pw], in1=r_[:, 0:pw], op=ALU.add),
                          reads=[ps_b, r_b], writes=[o_b])
                    kb.dma("sp", x2[r0:r0 + 128, j0:j0 + pw], o_[:, 0:pw], reads=[o_b], writes=[xb[key]])

                P.dense_core(kb, h2T, h2T_b, 16, TT_, W["w_up"][l][:, qd * HQ:(qd + 1) * HQ], [(0, HQ, "b", epi_up)], panels, pss)
                P.dense_core(kb, actT, actT_b, 16, TT_, W["w_down"][l][qd * HQ:(qd + 1) * HQ, :], [(0, D, "a", epi_dn)], panels, pss)
        kb.barrier()


def final_norm(P, kb, x_src, lnw_ap, out_ap):
    with ExitStack() as es:
        lnw, lnw_b = kb.sb(es, "flnw", [128, D], F32)
        kb.dma("sp", lnw[:], bass.AP(tensor=lnw_ap.tensor, offset=lnw_ap.offset, ap=[[0, 128], [1, D]]), writes=[lnw_b])
        xts = Rot([kb.sb(es, f"fxt{i}", [128, D], F32) for i in range(2)])
        ots = Rot([kb.sb(es, f"fot{i}", [128, D], F32) for i in range(2)])
        junk, junk_b = kb.sb(es, "fjunk", [128, D], BF16)
        sts = Rot([kb.sb(es, f"fst{i}", [128, 4], F32) for i in range(2)])
        for tt in range(S // 128):
            xt, xt_b = xts.next()
            ot, ot_b = ots.next()
            st, st_b = sts.next()
            kb.dma("sp", xt[:], x_src[tt * 128:(tt + 1) * 128, :], writes=[xt_b])
            kb.op("act", lambda e: e.activation(out=junk[:], in_=xt[:], func=AF.Square, accum_out=st[:, 0:1]),
                  reads=[xt_b], writes=[junk_b, st_b])
            kb.op("act", lambda e: e.activation(out=st[:, 1:2], in_=st[:, 0:1], func=AF.Sqrt, scale=1.0 / D, bias=P.eps_t[:, 0:1]),
                  reads=[st_b, P.eps_b], writes=[st_b])
            kb.op("dve", lambda e: e.reciprocal(out=st[:, 2:3], in_=st[:, 1:2]), reads=[st_b], writes=[st_b])
            kb.op("dve", lambda e: e.scalar_tensor_tensor(out=ot[:], in0=xt[:], scalar=st[:, 2:3], in1=lnw[:], op0=ALU.mult, op1=ALU.mult),
                  reads=[xt_b, st_b, lnw_b], writes=[ot_b])
            kb.dma("sp", out_ap[tt * 128:(tt + 1) * 128, :], ot[:], reads=[ot_b])
        kb.barrier()
```

```python
import math
import numpy as np
from contextlib import ExitStack
import concourse.bass as bass
import concourse.mybir as mybir
from concourse.bass_utils import run_bass_kernel_spmd

F32 = mybir.dt.float32
BF16 = mybir.dt.bfloat16
I32 = mybir.dt.int32
ALU = mybir.AluOpType
AF = mybir.ActivationFunctionType
AX = mybir.AxisListType

NCORES = 8
BPC = 2
S = 2048
D = 2048
DEPTH = 2
DFF = 8192
IN_COLS = 10792
EPS = 1e-6
NEG = -30000.0

C_Q = 0
C_KV = 1024
C_GATE = 2560
C_GQKV = 2584
C_Z = 5656
C_A = 6680
C_B = 6688
C_MERGE = 6696


class Buf:
    __slots__ = ("name", "w", "r")

    def __init__(self, name):
        self.name = name
        self.w = None
        self.r = {}


class KB:
    SEM_LIMIT = 30000

    def __init__(self, nc, es, n_dma_slots=8):
        self.nc = nc
        self.es = es
        self.eng = {"pe": nc.tensor, "act": nc.scalar, "dve": nc.vector, "pool": nc.gpsimd, "sp": nc.sync}
        self.sems = {}
        self.cur = {}
        self.epoch = {}
        for e in self.eng:
            self.epoch[e] = 0
            self._new_sem(e)
        self.seen = {e: {} for e in self.eng}
        self.dma_slots = {}
        for q in ("sp", "act", "pool"):
            sl = []
            for i in range(n_dma_slots):
                key = f"d_{q}_{i}"
                self.sems[key] = es.enter_context(nc.semaphore(key))
                sl.append([key, 0])
            self.dma_slots[q] = [sl, 0]
        self.n_ins = 0
        self.n_wait = 0
        self.uid = 0
        self.last_tok = {}
        self.know = {}

    def _new_sem(self, e):
        key = f"s_{e}_{self.epoch[e]}"
        self.sems[key] = self.es.enter_context(self.nc.semaphore(key))
        self.cur[e] = [key, 0]
        self.epoch[e] += 1

    def _need(self, e, toks):
        seen = self.seen[e]
        best = {}
        for t in toks:
            if t is None:
                continue
            k, v = t
            if seen.get(k, 0) >= v:
                continue
            if best.get(k, 0) < v:
                best[k] = v
        for k, v in best.items():
            if seen.get(k, 0) >= v:
                continue
            self.eng[e].wait_ge(self.sems[k], v)
            seen[k] = v
            self.n_wait += 1
            kn = self.know.get((k, v))
            if kn:
                for k2, v2 in kn.items():
                    if seen.get(k2, 0) < v2:
                        seen[k2] = v2

    def _deps(self, reads, writes, skip_waw_key=None):
        toks = []
        for b in reads:
            toks.append(b.w)
        for b in writes:
            if b.w is not None and not (skip_waw_key is not None and b.w[0] == skip_waw_key):
                toks.append(b.w)
            for k, v in b.r.items():
                toks.append((k, v))
        return toks

    def _record(self, tok, reads, writes):
        k, v = tok
        for b in reads:
            if b.r.get(k, 0) < v:
                b.r[k] = v
        for b in writes:
            b.w = tok
            b.r = {}

    def op(self, e, fn, reads=(), writes=(), sig=True):
        cur = self.cur[e]
        if cur[1] >= self.SEM_LIMIT:
            self._new_sem(e)
            cur = self.cur[e]
        skip = cur[0] if e == "pe" else None
        self._need(e, self._deps(reads, writes, skip_waw_key=skip))
        ins = fn(self.eng[e])
        self.n_ins += 1
        tok = (cur[0], cur[1] + 1)
        if sig:
            ins.then_inc(self.sems[cur[0]], 1)
            cur[1] += 1
            self.last_tok[e] = tok
            self.know[tok] = dict(self.seen[e])
        self._record(tok, reads, writes)
        return ins

    def dma(self, q, out, in_, reads=(), writes=(), **kw):
        sl, idx = self.dma_slots[q]
        slot = sl[idx % len(sl)]
        self.dma_slots[q][1] = idx + 1
        key, uses = slot
        toks = self._deps(reads, writes)
        if uses > 0:
            toks.append((key, 16 * uses))
        self._need(q, toks)
        ins = self.eng[q].dma_start(out=out, in_=in_, **kw)
        ins.then_inc(self.sems[key], 16)
        slot[1] = uses + 1
        self.n_ins += 1
        tok = (key, 16 * (uses + 1))
        self.know[tok] = dict(self.seen[q])
        self._record(tok, reads, writes)
        return ins

    def barrier(self):
        toks = []
        for e in self.eng:
            if e in self.last_tok:
                toks.append(self.last_tok[e])
        for q in self.dma_slots:
            for key, uses in self.dma_slots[q][0]:
                if uses > 0:
                    toks.append((key, 16 * uses))
        for e in self.eng:
            self._need(e, toks)

    def sb(self, es, name, shape, dtype):
        self.uid += 1
        t = es.enter_context(self.nc.sbuf_tensor(f"{name}_{self.uid}", list(shape), dtype))
        return t, Buf(name)

    def ps(self, es, name, shape, dtype=F32):
        self.uid += 1
        t = es.enter_context(self.nc.psum_tensor(f"{name}_{self.uid}", list(shape), dtype))
        return t, Buf(name)


class Rot:
    def __init__(self, items):
        self.items = items
        self.i = 0

    def next(self):
        it = self.items[self.i % len(self.items)]
        self.i += 1
        return it


class Prog:
    def __init__(self, nc, dbg=()):
        self.nc = nc
        self.dbg = set(dbg)
        self.ext_out = {}

    def dram(self, name, shape, dtype):
        kind = "ExternalOutput" if name in self.dbg else "Internal"
        if ("in:" + name) in self.dbg:
            kind = "ExternalInput"
        t = self.nc.dram_tensor(name, list(shape), dtype, kind=kind)
        if name in self.dbg:
            self.ext_out[name] = t
        return t.ap()

    def setup(self, kb, es):
        nc = self.nc
        self.ident_f, self.ident_f_b = kb.sb(es, "identf", [128, 128], F32)
        self.ident_b, self.ident_b_b = kb.sb(es, "identb", [128, 128], BF16)
        kb.op("pool", lambda e: e.memset(self.ident_f[:], 0.0), writes=[self.ident_f_b])
        kb.op("pool", lambda e: e.affine_select(out=self.ident_f[:], in_=self.ident_f[:], pattern=[[-1, 128]],
                                                compare_op=ALU.not_equal, fill=1.0, base=0, channel_multiplier=1),
              reads=[self.ident_f_b], writes=[self.ident_f_b])
        kb.op("dve", lambda e: e.tensor_copy(out=self.ident_b[:], in_=self.ident_f[:]),
              reads=[self.ident_f_b], writes=[self.ident_b_b])
        self.eps_t, self.eps_b = kb.sb(es, "eps", [128, 1], F32)
        kb.op("pool", lambda e: e.memset(self.eps_t[:], EPS), writes=[self.eps_b])
        self.one_t, self.one_b = kb.sb(es, "onec", [128, 1], F32)
        kb.op("pool", lambda e: e.memset(self.one_t[:], 1.0), writes=[self.one_b])

    def norm_res(self, kb, es, lnw_ap, npts=4):
        R = {}
        R["lnw"] = kb.sb(es, "lnw", [128, D], F32)
        kb.dma("sp", R["lnw"][0][:], bass.AP(tensor=lnw_ap.tensor, offset=lnw_ap.offset, ap=[[0, 128], [1, D]]),
               writes=[R["lnw"][1]])
        R["xts"] = Rot([kb.sb(es, f"xt{i}", [128, D], F32) for i in range(2)])
        R["hbs"] = Rot([kb.sb(es, f"hb{i}", [128, D], BF16) for i in range(2)])
        R["junk"] = kb.sb(es, "junk", [128, D], BF16)
        R["sts"] = Rot([kb.sb(es, f"st{i}", [128, 4], F32) for i in range(2)])
        R["pts"] = Rot([kb.ps(es, f"pt{i}", [128, 4, 128], BF16) for i in range(npts)])
        return R

    def norm_T(self, kb, x_ap, lnw_ap, hT, hT_b, ntok, tok0=0, res=None):
        nc = self.nc
        with ExitStack() as es:
            R = res if res is not None else self.norm_res(kb, es, lnw_ap)
            lnw, lnw_b = R["lnw"]
            xts, hbs, sts, pts = R["xts"], R["hbs"], R["sts"], R["pts"]
            junk, junk_b = R["junk"]
            for tt in range(ntok // 128):
                xt, xt_b = xts.next()
                hb, hb_b = hbs.next()
                st, st_b = sts.next()
                r0 = tok0 + tt * 128
                kb.dma("sp", xt[:], x_ap[r0:r0 + 128, :], writes=[xt_b])
                kb.op("act", lambda e: e.activation(out=junk[:], in_=xt[:], func=AF.Square, accum_out=st[:, 0:1]),
                      reads=[xt_b], writes=[junk_b, st_b])
                kb.op("act", lambda e: e.activation(out=st[:, 1:2], in_=st[:, 0:1], func=AF.Sqrt, scale=1.0 / D, bias=self.eps_t[:, 0:1]),
                      reads=[st_b, self.eps_b], writes=[st_b])
                kb.op("dve", lambda e: e.reciprocal(out=st[:, 2:3], in_=st[:, 1:2]), reads=[st_b], writes=[st_b])
                kb.op("dve", lambda e: e.scalar_tensor_tensor(out=hb[:], in0=xt[:], scalar=st[:, 2:3], in1=lnw[:],
                                                              op0=ALU.mult, op1=ALU.mult),
                      reads=[xt_b, st_b, lnw_b], writes=[hb_b])
                for g in range(4):
                    pt, pt_b = pts.next()
                    for j in range(4):
                        c = g * 4 + j
                        kb.op("pe", lambda e, c=c, j=j: e.transpose(out=pt[:, j, :], in_=hb[:, c * 128:(c + 1) * 128],
                                                                     identity=self.ident_b[:]),
                              reads=[hb_b, self.ident_b_b], writes=[pt_b], sig=(j == 3))
                    eng = "act" if g % 2 == 0 else "dve"
                    dst = hT[:, g * 4:(g + 1) * 4, tt * 128:(tt + 1) * 128]
                    if eng == "act":
                        kb.op("act", lambda e: e.copy(out=dst, in_=pt[:]), reads=[pt_b], writes=[hT_b])
                    else:
                        kb.op("dve", lambda e: e.tensor_copy(out=dst, in_=pt[:]), reads=[pt_b], writes=[hT_b])
            if res is None:
                kb.barrier()

    def dense(self, kb, inT, inT_b, KC, ntok, W_ap, jobs, wq="pool"):
        PW = 512
        with ExitStack() as es:
            panels = Rot([kb.sb(es, f"wp{i}", [128, KC, PW], BF16) for i in range(2)])
            pss = Rot([kb.ps(es, f"dps{i}", [128, 512], F32) for i in range(4)])
            self.dense_core(kb, inT, inT_b, KC, ntok, W_ap, jobs, panels, pss, wq)
            kb.barrier()

    def dense_core(self, kb, inT, inT_b, KC, ntok, W_ap, jobs, panels, pss, wq="pool"):
        PW = 512
        if True:
            Wv = W_ap.rearrange("(kc p) n -> p kc n", p=128)
            for (c0, ncols, form, epi) in jobs:
                for p0 in range(0, ncols, PW):
                    pw = min(PW, ncols - p0)
                    wp, wp_b = panels.next()
                    kb.dma(wq, wp[:, 0:KC, 0:pw], Wv[:, :, c0 + p0:c0 + p0 + pw], writes=[wp_b])
                    if form == "b":
                        for cb in range(0, pw, 128):
                            cw = min(128, pw - cb)
                            for t0 in range(0, ntok, 512):
                                tw = min(512, ntok - t0)
                                ps, ps_b = pss.next()
                                for kc in range(KC):
                                    kb.op("pe", lambda e, kc=kc: e.matmul(ps[0:cw, 0:tw], lhsT=wp[:, kc, cb:cb + cw],
                                                                         rhs=inT[:, kc, t0:t0 + tw],
                                                                         start=(kc == 0), stop=(kc == KC - 1)),
                                          reads=[wp_b, inT_b], writes=[ps_b], sig=(kc == KC - 1))
                                epi(ps, ps_b, p0 + cb, cw, t0, tw)
                    else:
                        for t0 in range(0, ntok, 128):
                            ps, ps_b = pss.next()
                            for kc in range(KC):
                                kb.op("pe", lambda e, kc=kc: e.matmul(ps[:, 0:pw], lhsT=inT[:, kc, t0:t0 + 128],
                                                                     rhs=wp[:, kc, 0:pw],
                                                                     start=(kc == 0), stop=(kc == KC - 1)),
                                      reads=[wp_b, inT_b], writes=[ps_b], sig=(kc == KC - 1))
                            epi(ps, ps_b, p0, pw, t0, 128)

    def make_stage(self, kb, es, n=4):
        self.stg_f = Rot([kb.sb(es, f"stgf{i}", [128, 512], F32) for i in range(n)])
        self.stg_h = Rot([kb.sb(es, f"stgh{i}", [128, 512], BF16) for i in range(n)])
        self.evac_i = 0

    def evac(self, kb, ps, ps_b, rows, cols, dst_ap, dtype=F32, func=None, eng=None):
        st, st_b = (self.stg_f if dtype == F32 else self.stg_h).next()
        if eng is None:
            eng = "act" if (func is not None or self.evac_i % 2 == 0) else "dve"
        self.evac_i += 1
        if eng == "act":
            f = func if func is not None else AF.Copy
            kb.op("act", lambda e: e.activation(out=st[0:rows, 0:cols], in_=ps[0:rows, 0:cols], func=f),
                  reads=[ps_b], writes=[st_b])
        else:
            kb.op("dve", lambda e: e.tensor_copy(out=st[0:rows, 0:cols], in_=ps[0:rows, 0:cols]),
                  reads=[ps_b], writes=[st_b])
        kb.dma("sp", dst_ap, st[0:rows, 0:cols], reads=[st_b])

    def alloc_proj_scratch(self):
        self.qT_d = self.dram("qT_d", [8, 128, S], BF16)
        self.kvT_d = self.dram("kvT_d", [4, 2, 128, S], BF16)
        self.vtok_d = self.dram("vtok_d", [2, S, 256], BF16)
        self.gate_d = self.dram("gate_d", [S, 24], F32)
        self.gqkvT_d = self.dram("gqkvT_d", [24, 128, S], F32)
        self.z_d = self.dram("z_d", [S, 1024], F32)
        self.ab_d = self.dram("ab_d", [S, 16], F32)
        self.mergeT_d = self.dram("mergeT_d", [32, 128, S], F32)

    def proj_phase(self, kb, hT, hT_b, w_in_l):
        with ExitStack() as es:
            self.make_stage(kb, es)

            def epi_q(ps, ps_b, j0, cw, t0, tw):
                self.evac(kb, ps, ps_b, cw, tw, self.qT_d[j0 // 128, :, t0:t0 + tw], dtype=BF16)

            def epi_kvT(kind):
                def f(ps, ps_b, j0, cw, t0, tw):
                    self.evac(kb, ps, ps_b, cw, tw, self.kvT_d[kind, j0 // 128, :, t0:t0 + tw], dtype=BF16)
                return f

            def epi_vtok(kind):
                def f(ps, ps_b, j0, pw, t0, tw):
                    self.evac(kb, ps, ps_b, 128, pw, self.vtok_d[kind, t0:t0 + 128, j0:j0 + pw], dtype=BF16)
                return f

            def epi_gate(ps, ps_b, j0, pw, t0, tw):
                self.evac(kb, ps, ps_b, 128, pw, self.gate_d[t0:t0 + 128, :], func=AF.Sigmoid)

            def epi_gqkv(ps, ps_b, j0, cw, t0, tw):
                self.evac(kb, ps, ps_b, cw, tw, self.gqkvT_d[j0 // 128, :, t0:t0 + tw])

            def epi_z(ps, ps_b, j0, pw, t0, tw):
                self.evac(kb, ps, ps_b, 128, pw, self.z_d[t0:t0 + 128, j0:j0 + pw], func=AF.Silu)

            def epi_ab(ps, ps_b, j0, pw, t0, tw):
                self.evac(kb, ps, ps_b, 128, pw, self.ab_d[t0:t0 + 128, :])

            def epi_merge(ps, ps_b, j0, cw, t0, tw):
                self.evac(kb, ps, ps_b, cw, tw, self.mergeT_d[j0 // 128, :, t0:t0 + tw], func=AF.Sigmoid)

            jobs = [
                (C_Q, 1024, "b", epi_q),
                (C_KV + 0, 256, "b", epi_kvT(0)),
                (C_KV + 256, 256, "b", epi_kvT(1)),
                (C_KV + 512, 256, "b", epi_kvT(2)),
                (C_KV + 768, 256, "a", epi_vtok(0)),
                (C_KV + 1024, 256, "b", epi_kvT(3)),
                (C_KV + 1280, 256, "a", epi_vtok(1)),
                (C_GATE, 24, "a", epi_gate),
                (C_GQKV, 3072, "b", epi_gqkv),
                (C_Z, 1024, "a", epi_z),
                (C_A, 16, "a", epi_ab),
                (C_MERGE, 4096, "b", epi_merge),
            ]
            self.dense(kb, hT, hT_b, 16, S, w_in_l, jobs)


WNAMES = [("rel_table", [32, 8]), ("ln1_w", [DEPTH, D]), ("w_in", [DEPTH, D, IN_COLS]),
          ("cmp_pe_k", [DEPTH, 32, 128]), ("cmp_pe_v", [DEPTH, 32, 128]),
          ("cmp_w1_k", [DEPTH, 4096, 256]), ("cmp_w2_k", [DEPTH, 256, 128]),
          ("cmp_w1_v", [DEPTH, 4096, 256]), ("cmp_w2_v", [DEPTH, 256, 128]),
          ("conv_w", [DEPTH, 4, 3072]), ("a_log", [DEPTH, 8]), ("dt_bias", [DEPTH, 8]),
          ("gdn_norm_w", [DEPTH, 128]), ("w_pa", [DEPTH, 1024, D]), ("w_pb", [DEPTH, 1024, D]),
          ("w_o", [DEPTH, D, D]), ("ln2_w", [DEPTH, D]), ("w_up", [DEPTH, D, DFF]),
          ("w_down", [DEPTH, DFF, D]), ("ln_f_w", [D])]


class LazyW:
    def __init__(self, nc, depth):
        self.nc = nc
        self.depth = depth
        self.aps = {}
        self.shapes = dict(WNAMES)

    def __getitem__(self, n):
        if n not in self.aps:
            shp = list(self.shapes[n])
            if len(shp) > 1 and shp[0] == DEPTH and n != "rel_table":
                shp[0] = self.depth
            self.aps[n] = self.nc.dram_tensor(n, shp, F32, kind="ExternalInput").ap()
        return self.aps[n]


def build(dbg=(), stop=None, nseq=BPC, depth=DEPTH):
    nc = bass.Bass("TRN2", target_bir_lowering=False)
    P = Prog(nc, dbg)
    x = nc.dram_tensor("x", [BPC, S, D], F32, kind="ExternalInput").ap()
    W = LazyW(nc, depth)
    P.W = W
    out = nc.dram_tensor("out", [BPC, S, D], F32, kind="ExternalOutput").ap()
    P.alloc_proj_scratch()
    P.oaT_d = P.dram("oaT_d", [8, 128, S], BF16)
    P.obT_d = P.dram("obT_d", [8, 128, S], BF16)
    P.uT_d = P.dram("uT_d", [16, 128, S], BF16)
    X1 = P.dram("X1_d", [BPC, S, D], F32)
    X2 = P.dram("X2_d", [BPC, S, D], F32)
    P.dbg_nsa = None
    P.dbg_gdn = None
    P.dbg_gdn_n = 0
    P.gdn_heads = 8
    if "gdn_dbg" in P.dbg:
        P.dbg_gdn = {nm: nc.dram_tensor("dbg_" + nm, [128, 128], F32, kind="ExternalOutput").ap()
                     for nm in ["q_tok", "k_tok", "v_tok", "e1", "e2", "TT", "u", "negw", "zt", "xm", "qkt", "yy", "kd", "Sn", "gcol"]}
        P.gdn_heads = GDN_TEST_HEADS
    if "nsa_dbg" in P.dbg:
        P.dbg_nsa = {"kcT": nc.dram_tensor("dbg_kcT", [128, 128], BF16, kind="ExternalOutput").ap(),
                     "vc": nc.dram_tensor("dbg_vc", [128, 164], F32, kind="ExternalOutput").ap(),
                     "imp": nc.dram_tensor("dbg_imp", [128, 16, 32], F32, kind="ExternalOutput").ap(),
                     "selbT": nc.dram_tensor("dbg_selbT", [32, S], BF16, kind="ExternalOutput").ap()}
    with ExitStack() as es:
        kb = KB(nc, es)
        P.setup(kb, es)
        nsa_setup(P, kb, es, W["rel_table"])
        gdn_setup(P, kb, es)
        kb.barrier()
        if stop == "nsa_only":
            nsa_phase(P, kb, 0)
        if stop == "gdn_only":
            gdn_phase(P, kb, 0)
        if stop == "post_only":
            merge_phase(P, kb, 0)
            wo_phase(P, kb, 0, x[0], X1[0])
            mlp_phase(P, kb, 0, X1[0], X2[0])
            final_norm(P, kb, X2[0], W["ln_f_w"], out[0])
        for l in range(depth if stop not in ("setup", "nsa_only", "gdn_only", "post_only") else 0):
            for s in range(nseq):
                x_in = x[s] if l == 0 else X2[s]
                with ExitStack() as es_h:
                    hT, hT_b = kb.sb(es_h, "hT", [128, 16, S], BF16)
                    P.norm_T(kb, x_in, W["ln1_w"][l], hT, hT_b, S)
                    if stop == "norm":
                        hT_d = P.dram("hT_d", [128, 16, S], BF16)
                        kb.dma("sp", hT_d, hT[:], reads=[hT_b])
                        break
                    P.proj_phase(kb, hT, hT_b, W["w_in"][l])
                    kb.barrier()
                if stop == "proj":
                    break
                nsa_phase(P, kb, l)
                if stop == "nsa":
                    break
                gdn_phase(P, kb, l)
                merge_phase(P, kb, l)
                wo_phase(P, kb, l, x_in, X1[s])
                mlp_phase(P, kb, l, X1[s], X2[s])
            if stop is not None:
                break
        if stop is None:
            for s in range(nseq):
                final_norm(P, kb, X2[s], W["ln_f_w"], out[s])
        kb.barrier()
        print("instructions", kb.n_ins, "waits", kb.n_wait)
    return nc, P


_CACHE = {}


def kernel(**inputs):
    if "nc" not in _CACHE:
        _CACHE["nc"] = build()
    nc, P = _CACHE["nc"]
    x = np.ascontiguousarray(inputs["x"], dtype=np.float32)
    wts = {n: np.ascontiguousarray(inputs[n], dtype=np.float32) for n in P.W.aps}
    in_maps = []
    for c in range(NCORES):
        m = dict(wts)
        m["x"] = np.ascontiguousarray(x[c * BPC:(c + 1) * BPC])
        in_maps.append(m)
    res = run_bass_kernel_spmd(nc, in_maps, core_ids=list(range(NCORES)))
    return np.concatenate([np.asarray(r["out"], dtype=np.float32) for r in res.results], axis=0)


SQ = math.sqrt(128.0)
SCALE = 1.0 / SQ
OFFW, WW = 384, 1408
OFFS, WS = 384, 1024
LGW = 127 + WW
LGS = 127 + WS
LGC = 4080
LG = 4096


def _bucket_ranges():
    d = np.arange(0, 4200)
    nf = np.maximum(d, 1).astype(np.float32)
    large = 16 + (np.log(nf / np.float32(16)) / np.float32(math.log(128 / 16)) * np.float32(16)).astype(np.int32)
    large = np.minimum(large, 31)
    bk = np.where(d < 16, d, large)
    out = []
    for b in range(32):
        idx = np.nonzero(bk == b)[0]
        out.append((b, int(idx[0]), int(idx[-1]) + 1))
    return out


def nsa_setup(P, kb, es, rel_table):
    nc = P.nc
    P.Mw_d = P.dram("Mw_d", [8, 128, WW], BF16)
    P.Ms_d = P.dram("Ms_d", [8, 128, WS], BF16)
    P.Mc_d = P.dram("Mc_d", [8, 128, S], BF16)
    G_d = P.dram("G_d", [3, 8, LG], BF16)
    P.t31, P.t31_b = kb.sb(es, "t31", [128, 8], F32)
    kb.dma("sp", P.t31[:], bass.AP(tensor=rel_table.tensor, offset=rel_table.offset + 31 * 8, ap=[[0, 128], [1, 8]]),
           writes=[P.t31_b])
    P.Jb, P.Jb_b = kb.sb(es, "Jb", [128, 128], BF16)
    P.I30k, P.I30k_b = kb.sb(es, "I30k", [128, 128], BF16)
    P.expall, P.expall_b = kb.sb(es, "expall", [128, S], BF16)
    P.FB, P.FB_b = kb.sb(es, "FB", [128, 16, 32], F32)
    P.cover, P.cover_b = kb.sb(es, "cover", [128, 32], F32)
    with ExitStack() as es2:
        tmpf, tmpf_b = kb.sb(es2, "tmpf", [128, 128], F32)
        kb.op("pool", lambda e: e.memset(tmpf[:], 0.0), writes=[tmpf_b])
        kb.op("pool", lambda e: e.affine_select(out=tmpf[:], in_=tmpf[:], pattern=[[1, 128]], compare_op=ALU.not_equal,
                                                fill=1.0, base=-127, channel_multiplier=1), reads=[tmpf_b], writes=[tmpf_b])
        kb.op("dve", lambda e: e.tensor_copy(out=P.Jb[:], in_=tmpf[:]), reads=[tmpf_b], writes=[P.Jb_b])
        kb.op("dve", lambda e: e.tensor_scalar(out=P.I30k[:], in0=P.ident_f[:], scalar1=30000.0, scalar2=None, op0=ALU.mult),
              reads=[P.ident_f_b], writes=[P.I30k_b])
        ex, ex_b = kb.sb(es2, "ex", [32, S], F32)
        kb.op("pool", lambda e: e.memset(ex[:], 1.0), writes=[ex_b])
        kb.op("pool", lambda e: e.affine_select(out=ex[:], in_=ex[:], pattern=[[1, S]], compare_op=ALU.is_ge, fill=0.0,
                                                base=0, channel_multiplier=-64), reads=[ex_b], writes=[ex_b])
        kb.op("pool", lambda e: e.affine_select(out=ex[:], in_=ex[:], pattern=[[-1, S]], compare_op=ALU.is_ge, fill=0.0,
                                                base=63, channel_multiplier=64), reads=[ex_b], writes=[ex_b])
        kb.op("pool", lambda e: e.memset(P.expall[:], 0.0), writes=[P.expall_b])
        kb.op("dve", lambda e: e.tensor_copy(out=P.expall[0:32, :], in_=ex[:]), reads=[ex_b, P.expall_b], writes=[P.expall_b])
        kb.op("pool", lambda e: e.memset(P.cover[:], 1.0), writes=[P.cover_b])
        kb.op("pool", lambda e: e.affine_select(out=P.cover[:], in_=P.cover[:], pattern=[[64, 32]], compare_op=ALU.is_ge,
                                                fill=0.0, base=63, channel_multiplier=-16), reads=[P.cover_b], writes=[P.cover_b])
        kb.op("pool", lambda e: e.affine_select(out=P.cover[:], in_=P.cover[:], pattern=[[-64, 32]], compare_op=ALU.is_ge,
                                                fill=0.0, base=31, channel_multiplier=16), reads=[P.cover_b], writes=[P.cover_b])
        kb.op("pool", lambda e: e.memset(P.FB[:], 0.0), writes=[P.FB_b])
        for st in range(16):
            for half in range(2):
                c = 2 * st + half
                ps_ = slice(64 * half, 64 * half + 64)
                if c + 1 < 32:
                    kb.op("pool", lambda e, ps_=ps_, c=c, st=st: e.memset(P.FB[ps_, st, c + 1:32], -1e30),
                          reads=[P.FB_b], writes=[P.FB_b])
                for m in sorted(set([0, c, max(c - 1, 0)])):
                    kb.op("pool", lambda e, ps_=ps_, m=m, st=st: e.memset(P.FB[ps_, st, m:m + 1], 1000.0),
                          reads=[P.FB_b], writes=[P.FB_b])
        tabT, tabT_b = kb.sb(es2, "tabT", [8, 32], F32)
        with nc.allow_non_contiguous_dma(reason="tiny table transpose"):
            kb.dma("sp", tabT[:], rel_table.rearrange("b h -> h b"), writes=[tabT_b])
        zer, zer_b = kb.sb(es2, "zer", [8, LG], F32)
        kb.op("pool", lambda e: e.memset(zer[:], 0.0), writes=[zer_b])
        rngs = _bucket_ranges()
        for kind, (off, dmax, L) in enumerate([(127 + OFFW, 512, LGW), (127 + OFFS, None, LGS), (2063, None, LGC)]):
            G, G_b = kb.sb(es2, f"G{kind}", [8, LG], F32)
            Gh, Gh_b = kb.sb(es2, f"Gh{kind}", [8, LG], BF16)
            kb.op("pool", lambda e, G=G: e.memset(G[:], NEG), writes=[G_b])
            for (b, lo, hi) in rngs:
                if b == 31:
                    hi = 10 ** 6
                if dmax is not None:
                    hi = min(hi, dmax)
                a0 = lo + off
                a1 = min(hi + off, L)
                if a1 <= a0:
                    continue
                kb.op("dve", lambda e, G=G, a0=a0, a1=a1, b=b: e.tensor_scalar(
                    out=G[:, a0:a1], in0=zer[:, a0:a1], scalar1=tabT[:, b:b + 1], scalar2=SQ, op0=ALU.add, op1=ALU.mult),
                    reads=[zer_b, tabT_b, G_b], writes=[G_b])
            kb.op("dve", lambda e, G=G, Gh=Gh: e.tensor_copy(out=Gh[:], in_=G[:]), reads=[G_b], writes=[Gh_b])
            kb.dma("sp", G_d[kind], Gh[:], reads=[Gh_b])
        kb.barrier()
        mps = Rot([kb.ps(es2, f"mps{i}", [128, 512], F32) for i in range(2)])
        mrev = Rot([kb.sb(es2, f"mrev{i}", [128, S], BF16) for i in range(2)])
        mout = Rot([kb.sb(es2, f"mout{i}", [128, S], BF16) for i in range(2)])
        for kind, (W_, pstep, dst) in enumerate([(WW, 1, P.Mw_d), (WS, 1, P.Ms_d), (S, 16, P.Mc_d)]):
            for h in range(8):
                mr, mr_b = mrev.next()
                mo, mo_b = mout.next()
                g_ap = G_d[kind, h]
                kb.dma("sp", mr[:, 0:W_], bass.AP(tensor=g_ap.tensor, offset=g_ap.offset, ap=[[pstep, 128], [1, W_]]),
                       writes=[mr_b])
                for c0 in range(0, W_, 512):
                    cw = min(512, W_ - c0)
                    ps, ps_b = mps.next()
                    kb.op("pe", lambda e, ps=ps, mr=mr, c0=c0, cw=cw: e.matmul(ps[:, 0:cw], lhsT=P.Jb[:], rhs=mr[:, c0:c0 + cw],
                                                                              start=True, stop=True),
                          reads=[P.Jb_b, mr_b], writes=[ps_b])
                    kb.op("act", lambda e, ps=ps, mo=mo, c0=c0, cw=cw: e.copy(out=mo[:, c0:c0 + cw], in_=ps[:, 0:cw]),
                          reads=[ps_b], writes=[mo_b])
                kb.dma("sp", dst[h], mo[:, 0:W_], reads=[mo_b])
        kb.barrier()


def nsa_phase(P, kb, l):
    nc = P.nc
    W = P.W
    with ExitStack() as es:
        qT, qT_b = kb.sb(es, "qT", [128, 8, S], BF16)
        for h in range(8):
            kb.dma("sp", qT[:, h, :], P.qT_d[h], writes=[qT_b])
        gates, gates_b = kb.sb(es, "gates", [128, 16, 24], F32)
        kb.dma("sp", gates[:], P.gate_d.rearrange("(st p) c -> p st c", p=128), writes=[gates_b])
        sc_ps = Rot([kb.ps(es, f"scps{i}", [128, 512], F32) for i in range(2)])
        o_ps = [kb.ps(es, f"ops{i}", [128, 512], F32) for i in range(4)]
        m_ps = Rot([kb.ps(es, f"mps{i}", [128, 512], F32) for i in range(2)])
        Ebf = Rot([kb.sb(es, f"Ebf{i}", [128, 512], BF16) for i in range(3)])
        Ef = Rot([kb.sb(es, f"Ef{i}", [128, 512], F32) for i in range(2)])
        acc, acc_b = kb.sb(es, "acc", [128, 4, 512], F32)
        small = Rot([kb.sb(es, f"sm{i}", [128, 8], F32) for i in range(8)])
        imp, imp_b = kb.sb(es, "imp", [128, 4, 32], F32)
        score, score_b = kb.sb(es, "score", [128, 4, 32], F32)
        wk, wk_b = kb.sb(es, "wk", [128, 4, 32], F32)
        m8, m8_b = kb.sb(es, "m8", [128, 4, 16], F32)
        selm, selm_b = kb.sb(es, "selm", [128, 4, 128], BF16)
        selbT, selbT_b = kb.sb(es, "selbT", [128, 512], BF16)
        ostg = Rot([kb.sb(es, f"ostg{i}", [128, 512], BF16) for i in range(2)])
        ksT, ksT_b = kb.sb(es, "ksT", [128, S], BF16)
        kwT, kwT_b = kb.sb(es, "kwT", [128, S], BF16)
        kcT_in, kcT_in_b = kb.sb(es, "kcTin", [128, S], BF16)
        vs_aug, vs_aug_b = kb.sb(es, "vsaug", [128, 16, 130], BF16)
        vw_aug, vw_aug_b = kb.sb(es, "vwaug", [128, 16, 130], BF16)
        w1, w1_b = kb.sb(es, "w1", [128, 32, 256], BF16)
        w2, w2_b = kb.sb(es, "w2", [128, 2, 128], BF16)
        pe_t, pe_b = kb.sb(es, "peT", [128, 32], F32)
        pe_raw, pe_raw_b = kb.sb(es, "peraw", [128, 128], F32)
        X, X_b = kb.sb(es, "X", [128, 32, 128], BF16)
        hid = [kb.sb(es, f"hid{i}", [128, 128], F32) for i in range(3)]
        gT = [kb.sb(es, f"gT{i}", [128, 128], BF16) for i in range(2)]
        kcT, kcT_b = kb.sb(es, "kcT", [128, 128], BF16)
        vc_aug, vc_aug_b = kb.sb(es, "vcaug", [128, 164], F32)
        Mc = [kb.sb(es, f"Mc{i}", [128, S], BF16) for i in range(4)]
        Ms = [kb.sb(es, f"Ms{i}", [128, WS], BF16) for i in range(4)]
        Mw = [kb.sb(es, f"Mw{i}", [128, WW], BF16) for i in range(4)]
        kb.op("pool", lambda e: e.memset(selm[:], 0.0), writes=[selm_b])
        kb.op("pool", lambda e: e.memset(pe_raw[:], 0.0), writes=[pe_raw_b])
        kb.op("pool", lambda e: e.memset(vs_aug[:, :, 128:130], 1.0), writes=[vs_aug_b])
        kb.op("pool", lambda e: e.memset(vw_aug[:, :, 128:130], 1.0), writes=[vw_aug_b])

        for g in range(2):
            kb.dma("sp", kcT_in[:], P.kvT_d[0, g], writes=[kcT_in_b])
            kb.dma("sp", ksT[:], P.kvT_d[2, g], writes=[ksT_b])
            kb.dma("sp", kwT[:], P.kvT_d[3, g], writes=[kwT_b])
            kb.dma("sp", vs_aug[:, :, 0:128], P.vtok_d[0, :, g * 128:(g + 1) * 128].rearrange("(kt p) d -> p kt d", p=128),
                   writes=[vs_aug_b])
            kb.dma("sp", vw_aug[:, :, 0:128], P.vtok_d[1, :, g * 128:(g + 1) * 128].rearrange("(kt p) d -> p kt d", p=128),
                   writes=[vw_aug_b])
            for j in range(4):
                h = g * 4 + j
                kb.dma("sp", Mc[j][0][:], P.Mc_d[h], writes=[Mc[j][1]])
                kb.dma("sp", Ms[j][0][:], P.Ms_d[h], writes=[Ms[j][1]])
                kb.dma("sp", Mw[j][0][:], P.Mw_d[h], writes=[Mw[j][1]])
            for kv in range(2):
                if kv == 1:
                    kb.dma("sp", kcT_in[:], P.kvT_d[1, g], writes=[kcT_in_b])
                w1n = "cmp_w1_k" if kv == 0 else "cmp_w1_v"
                w2n = "cmp_w2_k" if kv == 0 else "cmp_w2_v"
                pen = "cmp_pe_k" if kv == 0 else "cmp_pe_v"
                kb.dma("pool", w1[:], W[w1n][l].rearrange("(p d) h -> d p h", d=128), writes=[w1_b])
                kb.dma("pool", w2[:], W[w2n][l].rearrange("(c p) d -> p c d", p=128), writes=[w2_b])
                kb.dma("sp", pe_raw[0:32, :], W[pen][l], writes=[pe_raw_b])
                pps, pps_b = m_ps.next()
                kb.op("pe", lambda e, pps=pps: e.transpose(out=pps[:, 0:128], in_=pe_raw[:], identity=P.ident_f[:]),
                      reads=[pe_raw_b, P.ident_f_b], writes=[pps_b])
                kb.op("act", lambda e, pps=pps: e.copy(out=pe_t[:], in_=pps[:, 0:32]), reads=[pps_b], writes=[pe_b])
                for p in range(32):
                    src = kcT_in[:, p:p + 16 * 126 + 1:16]
                    kb.op("dve", lambda e, p=p, src=src: e.tensor_scalar(out=X[:, p, 0:127], in0=src, scalar1=pe_t[:, p:p + 1],
                                                                        scalar2=None, op0=ALU.add),
                          reads=[kcT_in_b, pe_b], writes=[X_b])
                for c in range(2):
                    ps, ps_b = m_ps.next()
                    for p in range(32):
                        kb.op("pe", lambda e, p=p, c=c, ps=ps: e.matmul(ps[:, 0:127], lhsT=w1[:, p, c * 128:(c + 1) * 128],
                                                                       rhs=X[:, p, 0:127], start=(p == 0), stop=(p == 31)),
                              reads=[w1_b, X_b], writes=[ps_b], sig=(p == 31))
                    (x_, x_b), (t_, t_b), (u_, u_b) = hid
                    kb.op("act", lambda e, ps=ps: e.copy(out=x_[:, 0:127], in_=ps[:, 0:127]), reads=[ps_b], writes=[x_b])
                    kb.op("dve", lambda e: e.tensor_tensor(out=t_[:, 0:127], in0=x_[:, 0:127], in1=x_[:, 0:127], op=ALU.mult),
                          reads=[x_b], writes=[t_b])
                    kb.op("dve", lambda e: e.tensor_scalar(out=t_[:, 0:127], in0=t_[:, 0:127], scalar1=0.044715, scalar2=1.0,
                                                           op0=ALU.mult, op1=ALU.add), reads=[t_b], writes=[t_b])
                    kb.op("dve", lambda e: e.tensor_tensor(out=t_[:, 0:127], in0=t_[:, 0:127], in1=x_[:, 0:127], op=ALU.mult),
                          reads=[t_b, x_b], writes=[t_b])
                    kb.op("act", lambda e: e.activation(out=u_[:, 0:127], in_=t_[:, 0:127], func=AF.Tanh,
                                                        scale=0.7978845608028654), reads=[t_b], writes=[u_b])
                    kb.op("dve", lambda e: e.tensor_scalar(out=u_[:, 0:127], in0=u_[:, 0:127], scalar1=1.0, scalar2=0.5,
                                                           op0=ALU.add, op1=ALU.mult), reads=[u_b], writes=[u_b])
                    kb.op("dve", lambda e, c=c: e.tensor_tensor(out=gT[c][0][:, 0:127], in0=u_[:, 0:127], in1=x_[:, 0:127],
                                                                op=ALU.mult), reads=[u_b, x_b], writes=[gT[c][1]])
                ps, ps_b = m_ps.next()
                if kv == 0:
                    for c in range(2):
                        kb.op("pe", lambda e, c=c, ps=ps: e.matmul(ps[:, 0:127], lhsT=w2[:, c, :], rhs=gT[c][0][:, 0:127],
                                                                  start=(c == 0), stop=(c == 1)),
                              reads=[w2_b, gT[c][1]], writes=[ps_b], sig=(c == 1))
                    kb.op("pool", lambda e: e.memset(kcT[:], 0.0), writes=[kcT_b])
                    kb.op("act", lambda e, ps=ps: e.copy(out=kcT[:, 0:127], in_=ps[:, 0:127]), reads=[ps_b, kcT_b], writes=[kcT_b])
                else:
                    for c in range(2):
                        kb.op("pe", lambda e, c=c, ps=ps: e.matmul(ps[0:127, 0:128], lhsT=gT[c][0][:, 0:127], rhs=w2[:, c, :],
                                                                  start=(c == 0), stop=(c == 1)),
                              reads=[w2_b, gT[c][1]], writes=[ps_b], sig=(c == 1))
                    kb.op("pool", lambda e: e.memset(vc_aug[:], 0.0), writes=[vc_aug_b])
                    kb.op("pool", lambda e: e.memset(vc_aug[:, 128:129], 1.0), reads=[vc_aug_b], writes=[vc_aug_b])
                    kb.op("act", lambda e, ps=ps: e.copy(out=vc_aug[0:127, 0:128], in_=ps[0:127, 0:128]),
                          reads=[ps_b, vc_aug_b], writes=[vc_aug_b])
                    kb.op("dve", lambda e: e.tensor_copy(out=vc_aug[:, 129:161], in_=P.cover[:]),
                          reads=[P.cover_b, vc_aug_b], writes=[vc_aug_b])
            if P.dbg_nsa is not None and g == 0:
                kb.dma("sp", P.dbg_nsa["kcT"], kcT[:], reads=[kcT_b])
                kb.dma("sp", P.dbg_nsa["vc"], vc_aug[:], reads=[vc_aug_b])

            for qt in range(4):
                t0 = qt * 512
                def cmp_scores(j):
                    h = g * 4 + j
                    ps, ps_b = sc_ps.next()
                    kb.op("pe", lambda e: e.matmul(ps[:], lhsT=kcT[:], rhs=qT[:, h, t0:t0 + 512], start=True, stop=False),
                          reads=[kcT_b, qT_b], writes=[ps_b], sig=False)
                    kb.op("pe", lambda e: e.matmul(ps[:], lhsT=P.ident_b[:], rhs=Mc[j][0][:, t0:t0 + 512], start=False, stop=True),
                          reads=[P.ident_b_b, Mc[j][1]], writes=[ps_b])
                    ef, ef_b = Ef.next()
                    kb.op("act", lambda e: e.activation(out=ef[:], in_=ps[:], func=AF.Exp, scale=SCALE),
                          reads=[ps_b], writes=[ef_b])
                    return ef, ef_b

                def cmp_pv(j, ef, ef_b):
                    h = g * 4 + j
                    for sub in range(4):
                        op_, op_b = o_ps[sub]
                        st = qt * 4 + sub
                        kb.op("pe", lambda e, op_=op_, sub=sub: e.matmul(op_[:, 0:161], lhsT=ef[:, sub * 128:(sub + 1) * 128],
                                                                        rhs=vc_aug[:, 0:161], start=True, stop=True),
                              reads=[ef_b, vc_aug_b], writes=[op_b])
                        sm, sm_b = small.next()
                        kb.op("dve", lambda e, sm=sm, op_=op_: e.tensor_scalar(out=sm[:, 0:1], in0=op_[:, 128:129], scalar1=1e-30,
                                                                              scalar2=None, op0=ALU.max), reads=[op_b], writes=[sm_b])
                        kb.op("dve", lambda e, sm=sm: e.reciprocal(out=sm[:, 1:2], in_=sm[:, 0:1]), reads=[sm_b], writes=[sm_b])
                        kb.op("dve", lambda e, sm=sm, st=st: e.tensor_tensor(out=sm[:, 2:3], in0=sm[:, 1:2],
                                                                            in1=gates[:, st, h * 3:h * 3 + 1], op=ALU.mult),
                              reads=[sm_b, gates_b], writes=[sm_b])
                        kb.op("dve", lambda e, sm=sm, op_=op_, sub=sub: e.tensor_scalar(
                            out=acc[:, sub, j * 128:(j + 1) * 128], in0=op_[:, 0:128], scalar1=sm[:, 2:3], scalar2=None, op0=ALU.mult),
                            reads=[op_b, sm_b, acc_b], writes=[acc_b])
                        if j == 0:
                            kb.op("dve", lambda e, sm=sm, op_=op_, sub=sub: e.tensor_scalar(
                                out=imp[:, sub, :], in0=op_[:, 129:161], scalar1=sm[:, 1:2], scalar2=None, op0=ALU.mult),
                                reads=[op_b, sm_b, imp_b], writes=[imp_b])
                        else:
                            kb.op("dve", lambda e, sm=sm, op_=op_, sub=sub: e.scalar_tensor_tensor(
                                out=imp[:, sub, :], in0=op_[:, 129:161], scalar=sm[:, 1:2], in1=imp[:, sub, :],
                                op0=ALU.mult, op1=ALU.add), reads=[op_b, sm_b, imp_b], writes=[imp_b])

                cpend = None
                for j in range(4):
                    ef, ef_b = cmp_scores(j)
                    if cpend is not None:
                        cmp_pv(*cpend)
                    cpend = (j, ef, ef_b)
                cmp_pv(*cpend)
                kb.op("dve", lambda e: e.tensor_tensor(out=score[:], in0=imp[:], in1=P.FB[:, qt * 4:qt * 4 + 4, :], op=ALU.add),
                      reads=[imp_b, P.FB_b], writes=[score_b])
                sp_, sp_b = m_ps.next()
                for sub in range(4):
                    kb.op("dve", lambda e, sub=sub: e.max(out=m8[:, sub, 0:8], in_=score[:, sub, :]), reads=[score_b, m8_b], writes=[m8_b])
                    kb.op("dve", lambda e, sub=sub: e.match_replace(out=wk[:, sub, :], in_to_replace=m8[:, sub, 0:8],
                                                                    in_values=score[:, sub, :], imm_value=-1e30),
                          reads=[score_b, m8_b, wk_b], writes=[wk_b])
                    kb.op("dve", lambda e, sub=sub: e.max(out=m8[:, sub, 8:16], in_=wk[:, sub, :]), reads=[wk_b, m8_b], writes=[m8_b])
                    kb.op("dve", lambda e, sub=sub: e.tensor_scalar(out=m8[:, sub, 15:16], in0=m8[:, sub, 15:16], scalar1=-1e29,
                                                                    scalar2=None, op0=ALU.max), reads=[m8_b], writes=[m8_b])
                    kb.op("dve", lambda e, sub=sub: e.tensor_scalar(out=selm[:, sub, 0:32], in0=score[:, sub, :], scalar1=m8[:, sub, 15:16],
                                                                    scalar2=1.0, op0=ALU.is_ge, op1=ALU.subtract),
                          reads=[score_b, m8_b, selm_b], writes=[selm_b])
                    kb.op("pe", lambda e, sub=sub: e.matmul(sp_[:, sub * 128:(sub + 1) * 128], lhsT=selm[:, sub, :], rhs=P.I30k[:],
                                                           start=True, stop=True),
                          reads=[selm_b, P.I30k_b], writes=[sp_b])
                kb.op("act", lambda e: e.copy(out=selbT[:], in_=sp_[:]), reads=[sp_b], writes=[selbT_b])
                if P.dbg_nsa is not None and g == 0:
                    kb.dma("sp", P.dbg_nsa["imp"][:, qt * 4:qt * 4 + 4, :], imp[:], reads=[imp_b])
                    kb.dma("sp", P.dbg_nsa["selbT"][:, t0:t0 + 512], selbT[0:32, :], reads=[selbT_b])

                def emit_scores(br, j, ki, kt):
                    h = g * 4 + j
                    dlt = t0 - kt * 128
                    ps, ps_b = sc_ps.next()
                    kT_, kT_b_ = (ksT, ksT_b) if br == 0 else (kwT, kwT_b)
                    const_bias = (br == 0 and dlt >= 256)
                    kb.op("pe", lambda e: e.matmul(ps[:], lhsT=kT_[:, kt * 128:(kt + 1) * 128], rhs=qT[:, h, t0:t0 + 512],
                                                   start=True, stop=False),
                          reads=[kT_b_, qT_b], writes=[ps_b], sig=False)
                    if br == 0:
                        kb.op("pe", lambda e: e.matmul(ps[:], lhsT=P.expall[:, kt * 128:(kt + 1) * 128], rhs=selbT[:],
                                                       start=False, stop=const_bias),
                              reads=[P.expall_b, selbT_b], writes=[ps_b], sig=const_bias)
                        if not const_bias:
                            kb.op("pe", lambda e: e.matmul(ps[:], lhsT=P.ident_b[:], rhs=Ms[j][0][:, dlt + OFFS:dlt + OFFS + 512],
                                                           start=False, stop=True),
                                  reads=[P.ident_b_b, Ms[j][1]], writes=[ps_b])
                    else:
                        kb.op("pe", lambda e: e.matmul(ps[:], lhsT=P.ident_b[:], rhs=Mw[j][0][:, dlt + OFFW:dlt + OFFW + 512],
                                                       start=False, stop=True),
                              reads=[P.ident_b_b, Mw[j][1]], writes=[ps_b])
                    eb, eb_b = Ebf.next()
                    if const_bias:
                        kb.op("act", lambda e: e.activation(out=eb[:], in_=ps[:], func=AF.Exp, scale=SCALE, bias=P.t31[:, h:h + 1]),
                              reads=[ps_b, P.t31_b], writes=[eb_b])
                    else:
                        kb.op("act", lambda e: e.activation(out=eb[:], in_=ps[:], func=AF.Exp, scale=SCALE),
                              reads=[ps_b], writes=[eb_b])
                    return eb, eb_b

                def emit_pv(br, j, ki, kt, nk, eb, eb_b):
                    h = g * 4 + j
                    v_, v_b_ = (vs_aug, vs_aug_b) if br == 0 else (vw_aug, vw_aug_b)
                    for sub in range(4):
                        op_, op_b = o_ps[sub]
                        kb.op("pe", lambda e, op_=op_, sub=sub: e.matmul(op_[:, 0:129], lhsT=eb[:, sub * 128:(sub + 1) * 128],
                                                                        rhs=v_[:, kt, 0:129], start=(ki == 0), stop=(ki == nk - 1)),
                              reads=[eb_b, v_b_], writes=[op_b], sig=(sub == 3))
                    if ki == nk - 1:
                        for sub in range(4):
                            op_, op_b = o_ps[sub]
                            st = qt * 4 + sub
                            sm, sm_b = small.next()
                            kb.op("dve", lambda e, sm=sm, op_=op_: e.reciprocal(out=sm[:, 1:2], in_=op_[:, 128:129]),
                                  reads=[op_b], writes=[sm_b])
                            kb.op("dve", lambda e, sm=sm, st=st: e.tensor_tensor(
                                out=sm[:, 2:3], in0=sm[:, 1:2], in1=gates[:, st, h * 3 + 1 + br:h * 3 + 2 + br], op=ALU.mult),
                                reads=[sm_b, gates_b], writes=[sm_b])
                            kb.op("dve", lambda e, sm=sm, op_=op_, sub=sub: e.scalar_tensor_tensor(
                                out=acc[:, sub, j * 128:(j + 1) * 128], in0=op_[:, 0:128], scalar=sm[:, 2:3],
                                in1=acc[:, sub, j * 128:(j + 1) * 128], op0=ALU.mult, op1=ALU.add),
                                reads=[op_b, sm_b, acc_b], writes=[acc_b])

                tiles = []
                for br in range(2):
                    for j in range(4):
                        if br == 0:
                            kts = list(range(0, (t0 + 511) // 128 + 1))
                        else:
                            kts = list(range(max(0, t0 // 128 - 4), t0 // 128 + 4))
                        for ki, kt in enumerate(kts):
                            tiles.append((br, j, ki, kt, len(kts)))
                pend = None
                for (br, j, ki, kt, nk) in tiles:
                    eb, eb_b = emit_scores(br, j, ki, kt)
                    if pend is not None:
                        emit_pv(*pend)
                    pend = (br, j, ki, kt, nk, eb, eb_b)
                emit_pv(*pend)
                for j in range(4):
                    h = g * 4 + j
                    tp, tp_b = m_ps.next()
                    for sub in range(4):
                        kb.op("pe", lambda e, tp=tp, sub=sub, j=j: e.transpose(out=tp[:, sub * 128:(sub + 1) * 128],
                                                                             in_=acc[:, sub, j * 128:(j + 1) * 128],
                                                                             identity=P.ident_f[:]),
                              reads=[acc_b, P.ident_f_b], writes=[tp_b], sig=(sub == 3))
                    og, og_b = ostg.next()
                    kb.op("act", lambda e, tp=tp, og=og: e.copy(out=og[:], in_=tp[:]), reads=[tp_b], writes=[og_b])
                    kb.dma("sp", P.oaT_d[h, :, t0:t0 + 512], og[:], reads=[og_b])
        kb.barrier()


def gdn_setup(P, kb, es):
    P.U_f, P.U_b = kb.sb(es, "U_f", [128, 128], F32)
    P.ones_f, P.ones_b = kb.sb(es, "ones_f", [128, 128], F32)
    P.mpos, P.mpos_b = kb.sb(es, "mpos", [128, 128], F32)
    P.mneg, P.mneg_b = kb.sb(es, "mneg", [128, 128], F32)
    kb.op("pool", lambda e: e.memset(P.ones_f[:], 1.0), writes=[P.ones_b])
    kb.op("pool", lambda e: e.memset(P.U_f[:], 1.0), writes=[P.U_b])
    kb.op("pool", lambda e: e.affine_select(out=P.U_f[:], in_=P.U_f[:], pattern=[[1, 128]], compare_op=ALU.is_ge, fill=0.0,
                                            base=0, channel_multiplier=-1), reads=[P.U_b], writes=[P.U_b])
    kb.op("pool", lambda e: e.memset(P.mpos[:], 0.0), writes=[P.mpos_b])
    kb.op("pool", lambda e: e.affine_select(out=P.mpos[:], in_=P.mpos[:], pattern=[[-1, 128]], compare_op=ALU.is_gt, fill=1e4,
                                            base=0, channel_multiplier=1), reads=[P.mpos_b], writes=[P.mpos_b])
    kb.op("pool", lambda e: e.memset(P.mneg[:], 0.0), writes=[P.mneg_b])
    kb.op("pool", lambda e: e.affine_select(out=P.mneg[:], in_=P.mneg[:], pattern=[[1, 128]], compare_op=ALU.is_ge, fill=-1e4,
                                            base=0, channel_multiplier=-1), reads=[P.mneg_b], writes=[P.mneg_b])


GDN_K = 3
GDN_TEST_HEADS = 1


def gdn_phase(P, kb, l):
    nc = P.nc
    W = P.W
    NCH = S // 128
    with ExitStack() as es:
        ab, ab_b = kb.sb(es, "ab", [128, NCH, 16], F32)
        kb.dma("sp", ab[:], P.ab_d.rearrange("(n p) c -> p n c", p=128), writes=[ab_b])
        dtb, dtb_b = kb.sb(es, "dtb", [128, 8], F32)
        nA, nA_b = kb.sb(es, "nA", [128, 8], F32)
        nw, nw_b = kb.sb(es, "nw", [128, 128], F32)
        mhalf, mhalf_b = kb.sb(es, "mhalf", [128, 1], F32)
        kb.op("pool", lambda e: e.memset(mhalf[:], -0.5), writes=[mhalf_b])
        kb.dma("sp", dtb[:], bass.AP(tensor=W["dt_bias"].tensor, offset=W["dt_bias"][l].offset, ap=[[0, 128], [1, 8]]), writes=[dtb_b])
        kb.dma("sp", nA[:], bass.AP(tensor=W["a_log"].tensor, offset=W["a_log"][l].offset, ap=[[0, 128], [1, 8]]), writes=[nA_b])
        kb.dma("sp", nw[:], bass.AP(tensor=W["gdn_norm_w"].tensor, offset=W["gdn_norm_w"][l].offset, ap=[[0, 128], [1, 128]]),
               writes=[nw_b])
        kb.op("act", lambda e: e.activation(out=nA[:], in_=nA[:], func=AF.Exp), reads=[nA_b], writes=[nA_b])
        kb.op("dve", lambda e: e.tensor_scalar(out=nA[:], in0=nA[:], scalar1=-1.0, scalar2=None, op0=ALU.mult), reads=[nA_b], writes=[nA_b])
        graw, graw_b = kb.sb(es, "graw", [128, NCH, 8], F32)
        beta, beta_b = kb.sb(es, "beta", [128, NCH, 8], F32)
        nbeta, nbeta_b = kb.sb(es, "nbeta", [128, NCH, 8], F32)
        gcol, gcol_b = kb.sb(es, "gcol", [128, NCH, 8], F32)
        eg, eg_b = kb.sb(es, "eg", [128, NCH, 8], F32)
        bg, bg_b = kb.sb(es, "bg", [128, NCH, 8], F32)
        for n in range(NCH):
            kb.op("dve", lambda e, n=n: e.tensor_tensor(out=graw[:, n, :], in0=ab[:, n, 0:8], in1=dtb[:], op=ALU.add),
                  reads=[ab_b, dtb_b, graw_b], writes=[graw_b])
        kb.op("act", lambda e: e.activation(out=graw[:], in_=graw[:], func=AF.Exp), reads=[graw_b], writes=[graw_b])
        kb.op("act", lambda e: e.activation(out=graw[:], in_=graw[:], func=AF.Ln, bias=P.one_t[:, 0:1]), reads=[graw_b, P.one_b], writes=[graw_b])
        for n in range(NCH):
            kb.op("dve", lambda e, n=n: e.tensor_tensor(out=graw[:, n, :], in0=graw[:, n, :], in1=nA[:], op=ALU.mult),
                  reads=[graw_b, nA_b], writes=[graw_b])
        kb.op("act", lambda e: e.activation(out=beta[:], in_=ab[:, :, 8:16], func=AF.Sigmoid), reads=[ab_b], writes=[beta_b])
        kb.op("dve", lambda e: e.tensor_scalar(out=nbeta[:], in0=beta[:], scalar1=-1.0, scalar2=None, op0=ALU.mult),
              reads=[beta_b], writes=[nbeta_b])
        bank = [kb.ps(es, f"gbank{i}", [128, 4, 128], F32) for i in range(6)]
        Q = Rot([(bank[i][0][:, 0, :], bank[i][1]) for i in range(6)])
        big = Rot([kb.ps(es, f"gbig{i}", [128, 512], F32) for i in range(2)])
        for n in range(NCH):
            pq, pq_b = Q.next()
            kb.op("pe", lambda e, n=n, pq=pq: e.matmul(pq[:, 0:8], lhsT=P.U_f[:], rhs=graw[:, n, :], start=True, stop=True),
                  reads=[P.U_b, graw_b], writes=[pq_b])
            kb.op("act", lambda e, n=n, pq=pq: e.copy(out=gcol[:, n, :], in_=pq[:, 0:8]), reads=[pq_b, gcol_b], writes=[gcol_b])
        kb.op("act", lambda e: e.activation(out=eg[:], in_=gcol[:], func=AF.Exp), reads=[gcol_b], writes=[eg_b])
        kb.op("dve", lambda e: e.tensor_tensor(out=bg[:], in0=eg[:], in1=beta[:], op=ALU.mult), reads=[eg_b, beta_b], writes=[bg_b])
        cw, cw_b = kb.sb(es, "cw", [128, 24, 4], F32)
        with ExitStack() as es_c:
            cwraw, cwraw_b = kb.sb(es_c, "cwraw", [128, 3072], F32)
            kb.op("pool", lambda e: e.memset(cwraw[:], 0.0), writes=[cwraw_b])
            kb.dma("sp", cwraw[0:4, :], W["conv_w"][l], reads=[cwraw_b], writes=[cwraw_b])
            for c in range(24):
                pq, pq_b = Q.next()
                kb.op("pe", lambda e, pq=pq, c=c: e.transpose(out=pq, in_=cwraw[:, c * 128:(c + 1) * 128], identity=P.ident_f[:]),
                      reads=[cwraw_b, P.ident_f_b], writes=[pq_b])
                kb.op("act", lambda e, pq=pq, c=c: e.copy(out=cw[:, c, :], in_=pq[:, 0:4]), reads=[pq_b, cw_b], writes=[cw_b])
            kb.barrier()
        xr = [kb.sb(es, f"xr{i}", [128, S], F32) for i in range(3)]
        sqt, sqt_b = kb.sb(es, "sqt", [128, S], F32)
        rn, rn_b = kb.sb(es, "rn", [128, 512], F32)

        def slot_bufs(k):
            B = {}
            B["xc"] = [kb.sb(es, f"xc{k}_{i}", [128, S], F32) for i in range(3)]
            B["tok"] = [kb.sb(es, f"tok{k}_{i}", [128, 128], F32) for i in range(3)]
            for nm, shp in [("vbk", [128, 256]), ("kd", [128, 128]), ("dg", [128, 128]), ("gd", [128, 128]), ("e1", [128, 128]),
                            ("e2", [128, 128]), ("qkt", [128, 128]), ("u", [128, 128]), ("nwk", [128, 128]), ("zt", [128, 128]),
                            ("xm", [128, 128]), ("zn", [128, 128]), ("yy", [128, 128]), ("junk", [128, 128])]:
                B[nm] = kb.sb(es, f"{nm}{k}", shp, F32)
            for nm in ["Pm", "PTm", "IPm", "Rm", "St", "ztok"]:
                B[nm] = Rot([kb.sb(es, f"{nm}{k}_{i}", [128, 128], F32) for i in range(2)])
            B["sm"] = Rot([kb.sb(es, f"gsm{k}_{i}", [128, 8], F32) for i in range(4)])
            B["og"] = Rot([kb.sb(es, f"gost{k}_{i}", [128, 512], BF16) for i in range(2)])
            B["zna"] = kb.sb(es, f"zna{k}", [128, NCH, 128], F32)
            return B

        def head_gen(h, B):
            xc = B["xc"]
            tok = B["tok"]
            vbk, vbk_b = B["vbk"]; kd, kd_b = B["kd"]; dg, dg_b = B["dg"]; gd, gd_b = B["gd"]
            e1, e1_b = B["e1"]; e2, e2_b = B["e2"]; qkt, qkt_b = B["qkt"]; u_, u_b = B["u"]; nwk, nwk_b = B["nwk"]
            zt, zt_b = B["zt"]; xm, xm_b = B["xm"]; zn, zn_b = B["zn"]; yy, yy_b = B["yy"]; junk, junk_b = B["junk"]
            Pm, PTm, IPm, Rm, St, ztok, sm, ogr = B["Pm"], B["PTm"], B["IPm"], B["Rm"], B["St"], B["ztok"], B["sm"], B["og"]
            for i in range(3):
                c = i * 8 + h
                kb.dma("sp", xr[i][0][:], P.gqkvT_d[c], writes=[xr[i][1]])
                x_, x_b = xr[i]
                y_, y_b = xc[i]
                kb.op("dve", lambda e, x_=x_, y_=y_, c=c: e.tensor_scalar(out=y_[:], in0=x_[:], scalar1=cw[:, c, 3:4], scalar2=None,
                                                                         op0=ALU.mult), reads=[x_b, cw_b], writes=[y_b])
                for sft in range(1, 4):
                    kb.op("dve", lambda e, x_=x_, y_=y_, c=c, sft=sft: e.scalar_tensor_tensor(
                        out=y_[:, sft:S], in0=x_[:, 0:S - sft], scalar=cw[:, c, 3 - sft:4 - sft], in1=y_[:, sft:S],
                        op0=ALU.mult, op1=ALU.add), reads=[x_b, cw_b, y_b], writes=[y_b])
                kb.op("act", lambda e, y_=y_: e.activation(out=y_[:], in_=y_[:], func=AF.Silu), reads=[y_b], writes=[y_b])
                if i < 2:
                    kb.op("act", lambda e, y_=y_: e.activation(out=sqt[:], in_=y_[:], func=AF.Square), reads=[y_b], writes=[sqt_b])
                    for t0 in range(0, S, 512):
                        bp, bp_b = big.next()
                        kb.op("pe", lambda e, bp=bp, t0=t0: e.matmul(bp[:], lhsT=P.ones_f[:], rhs=sqt[:, t0:t0 + 512], start=True, stop=True),
                              reads=[P.ones_b, sqt_b], writes=[bp_b])
                        kb.op("act", lambda e, bp=bp: e.activation(out=rn[:], in_=bp[:], func=AF.Sqrt, bias=P.eps_t[:, 0:1]),
                              reads=[bp_b, P.eps_b], writes=[rn_b])
                        kb.op("dve", lambda e: e.reciprocal(out=rn[:], in_=rn[:]), reads=[rn_b], writes=[rn_b])
                        if i == 0:
                            kb.op("dve", lambda e, y_=y_, t0=t0: e.scalar_tensor_tensor(
                                out=y_[:, t0:t0 + 512], in0=rn[:], scalar=SCALE, in1=y_[:, t0:t0 + 512], op0=ALU.mult, op1=ALU.mult),
                                reads=[rn_b, y_b], writes=[y_b])
                        else:
                            kb.op("dve", lambda e, y_=y_, t0=t0: e.tensor_tensor(out=y_[:, t0:t0 + 512], in0=rn[:], in1=y_[:, t0:t0 + 512],
                                                                               op=ALU.mult), reads=[rn_b, y_b], writes=[y_b])
            zna, zna_b = B["zna"]
            kb.dma("sp", zna[:], P.z_d[:, h * 128:(h + 1) * 128].rearrange("(n p) d -> p n d", p=128), writes=[zna_b])
            for n in range(NCH):
                kb.op("pool", lambda e, n=n: e.tensor_tensor(out=zna[:, n, :], in0=zna[:, n, :], in1=nw[:], op=ALU.mult),
                      reads=[zna_b, nw_b], writes=[zna_b])
            yield
            qTn, qTn_b = xc[0]
            kTn, kTn_b = xc[1]
            S_, S_b = St.next()
            kb.op("pool", lambda e, S_=S_: e.memset(S_[:], 0.0), writes=[S_b])
            og, og_b = None, None
            for n in range(NCH):
                cs = slice(n * 128, (n + 1) * 128)
                (q_tok, q_tok_b), (k_tok, k_tok_b), (v_tok, v_tok_b) = tok
                for i in range(3):
                    pq, pq_b = Q.next()
                    kb.op("pe", lambda e, pq=pq, i=i: e.transpose(out=pq, in_=xc[i][0][:, cs], identity=P.ident_f[:]),
                          reads=[xc[i][1], P.ident_f_b], writes=[pq_b])
                    if i == 0:
                        kb.op("act", lambda e, pq=pq: e.activation(out=tok[0][0][:], in_=pq, func=AF.Copy, scale=eg[:, n, h:h + 1]),
                              reads=[pq_b, eg_b], writes=[tok[0][1]])
                    if i == 1:
                        kb.op("act", lambda e, pq=pq: e.copy(out=tok[1][0][:], in_=pq), reads=[pq_b], writes=[tok[1][1]])
                    if i == 1:
                        kb.op("act", lambda e, pq=pq: e.activation(out=vbk[:, 128:256], in_=pq, func=AF.Copy, scale=bg[:, n, h:h + 1]),
                              reads=[pq_b, bg_b, vbk_b], writes=[vbk_b])
                    if i == 2:
                        kb.op("act", lambda e, pq=pq: e.activation(out=vbk[:, 0:128], in_=pq, func=AF.Copy, scale=beta[:, n, h:h + 1]),
                              reads=[pq_b, beta_b, vbk_b], writes=[vbk_b])
                yield
                kb.op("act", lambda e: e.activation(out=gd[:], in_=P.ident_f[:], func=AF.Copy, scale=gcol[:, n, h:h + 1]),
                      reads=[P.ident_f_b, gcol_b], writes=[gd_b])
                pg, pg_b = Q.next()
                kb.op("pe", lambda e, pg=pg: e.matmul(pg, lhsT=P.ones_f[:], rhs=gd[:], start=True, stop=True),
                      reads=[P.ones_b, gd_b], writes=[pg_b])
                kb.op("dve", lambda e, pg=pg: e.scalar_tensor_tensor(out=e1[:], in0=pg, scalar=gcol[:, n, h:h + 1], in1=P.mpos[:],
                                                                    op0=ALU.subtract, op1=ALU.max),
                      reads=[pg_b, gcol_b, P.mpos_b], writes=[e1_b])
                kb.op("dve", lambda e, pg=pg: e.scalar_tensor_tensor(out=e2[:], in0=pg, scalar=gcol[:, n, h:h + 1], in1=P.mneg[:],
                                                                    op0=ALU.subtract, op1=ALU.min),
                      reads=[pg_b, gcol_b, P.mneg_b], writes=[e2_b])
                s1, s1_b = sm.next()
                kb.op("dve", lambda e, pg=pg, s1=s1: e.tensor_copy(out=s1[:, 0:1], in_=pg[:, 127:128]), reads=[pg_b], writes=[s1_b])
                kb.op("act", lambda e: e.activation(out=e1[:], in_=e1[:], func=AF.Exp, scale=-1.0), reads=[e1_b], writes=[e1_b])
                kb.op("act", lambda e: e.activation(out=e2[:], in_=e2[:], func=AF.Exp), reads=[e2_b], writes=[e2_b])
                kb.op("act", lambda e, s1=s1: e.activation(out=s1[:, 1:2], in_=gcol[:, n, h:h + 1], func=AF.Exp, scale=-1.0, bias=s1[:, 0:1]),
                      reads=[s1_b, gcol_b], writes=[s1_b])
                kb.op("act", lambda e, s1=s1: e.activation(out=s1[:, 2:3], in_=s1[:, 0:1], func=AF.Exp), reads=[s1_b], writes=[s1_b])
                kb.op("act", lambda e, s1=s1: e.activation(out=kd[:], in_=k_tok[:], func=AF.Copy, scale=s1[:, 1:2]),
                      reads=[k_tok_b, s1_b], writes=[kd_b])
                yield
                pk, pk_b = Q.next()
                kb.op("pe", lambda e, pk=pk: e.matmul(pk, lhsT=kTn[:, cs], rhs=kTn[:, cs], start=True, stop=True),
                      reads=[kTn_b], writes=[pk_b])
                P0, P0_b = Pm.next()
                kb.op("dve", lambda e, pk=pk, P0=P0: e.scalar_tensor_tensor(out=P0[:], in0=pk, scalar=nbeta[:, n, h:h + 1], in1=e1[:],
                                                                           op0=ALU.mult, op1=ALU.mult),
                      reads=[pk_b, nbeta_b, e1_b], writes=[P0_b])
                pk2, pk2_b = Q.next()
                kb.op("pe", lambda e, pk2=pk2: e.matmul(pk2, lhsT=kTn[:, cs], rhs=qTn[:, cs], start=True, stop=True),
                      reads=[kTn_b, qTn_b], writes=[pk2_b])
                kb.op("dve", lambda e, pk2=pk2: e.tensor_tensor(out=qkt[:], in0=pk2, in1=e2[:], op=ALU.mult),
                      reads=[pk2_b, e2_b], writes=[qkt_b])
                pt_, pt_b = Q.next()
                kb.op("pe", lambda e, pt_=pt_, P0=P0: e.transpose(out=pt_, in_=P0[:], identity=P.ident_f[:]),
                      reads=[P0_b, P.ident_f_b], writes=[pt_b])
                PT0, PT0_b = PTm.next()
                R0, R0_b = Rm.next()
                kb.op("dve", lambda e, pt_=pt_, PT0=PT0: e.tensor_copy(out=PT0[:], in_=pt_), reads=[pt_b], writes=[PT0_b])
                kb.op("dve", lambda e, pt_=pt_, R0=R0: e.tensor_tensor(out=R0[:], in0=pt_, in1=P.ident_f[:], op=ALU.add),
                      reads=[pt_b, P.ident_f_b], writes=[R0_b])
                yield
                Pc, Pc_b, PTc, PTc_b, Rc, Rc_b = P0, P0_b, PT0, PT0_b, R0, R0_b
                for lvl in range(1, 7):
                    pa_, pa_b = Q.next()
                    kb.op("pe", lambda e, pa_=pa_, Pc=Pc, PTc=PTc: e.matmul(pa_, lhsT=PTc[:], rhs=Pc[:], start=True, stop=True),
                          reads=[Pc_b, PTc_b], writes=[pa_b])
                    Pn, Pn_b = Pm.next()
                    kb.op("act", lambda e, pa_=pa_, Pn=Pn: e.copy(out=Pn[:], in_=pa_), reads=[pa_b], writes=[Pn_b])
                    if lvl < 6:
                        pb_, pb_b = Q.next()
                        kb.op("pe", lambda e, pb_=pb_, Pc=Pc, PTc=PTc: e.matmul(pb_, lhsT=Pc[:], rhs=PTc[:], start=True, stop=True),
                              reads=[Pc_b, PTc_b], writes=[pb_b])
                        PTn, PTn_b = PTm.next()
                        kb.op("dve", lambda e, pb_=pb_, PTn=PTn: e.tensor_copy(out=PTn[:], in_=pb_), reads=[pb_b], writes=[PTn_b])
                    yield
                    pr_, pr_b = Q.next()
                    kb.op("pe", lambda e, pr_=pr_, Pn=Pn, Rc=Rc: e.matmul(pr_, lhsT=Pn[:], rhs=Rc[:], start=True, stop=True),
                          reads=[Pn_b, Rc_b], writes=[pr_b])
                    Rn, Rn_b = Rm.next()
                    kb.op("dve", lambda e, pr_=pr_, Rn=Rn, Rc=Rc: e.tensor_tensor(out=Rn[:], in0=pr_, in1=Rc[:], op=ALU.add),
                          reads=[pr_b, Rc_b], writes=[Rn_b])
                    Rc, Rc_b = Rn, Rn_b
                    if lvl < 6:
                        Pc, Pc_b, PTc, PTc_b = Pn, Pn_b, PTn, PTn_b
                    yield
                TT, TT_b = Rc, Rc_b
                bp, bp_b = big.next()
                kb.op("pe", lambda e, bp=bp, TT=TT: e.matmul(bp[:, 0:256], lhsT=TT[:], rhs=vbk[:], start=True, stop=True),
                      reads=[TT_b, vbk_b], writes=[bp_b])
                kb.op("act", lambda e, bp=bp: e.copy(out=u_[:], in_=bp[:, 0:128]), reads=[bp_b], writes=[u_b])
                kb.op("act", lambda e, bp=bp: e.activation(out=nwk[:], in_=bp[:, 128:256], func=AF.Copy, scale=-1.0),
                      reads=[bp_b], writes=[nwk_b])
                yield
                pz, pz_b = Q.next()
                kb.op("pe", lambda e, pz=pz: e.matmul(pz, lhsT=q_tok[:], rhs=P.ident_f[:], start=True, stop=False),
                      reads=[q_tok_b, P.ident_f_b], writes=[pz_b])
                kb.op("pe", lambda e, pz=pz: e.matmul(pz, lhsT=nwk[:], rhs=qkt[:], start=False, stop=True),
                      reads=[nwk_b, qkt_b], writes=[pz_b])
                kb.op("act", lambda e, pz=pz: e.copy(out=zt[:], in_=pz), reads=[pz_b], writes=[zt_b])
                px, px_b = Q.next()
                kb.op("pe", lambda e, px=px: e.matmul(px, lhsT=nwk[:], rhs=kd[:], start=True, stop=True),
                      reads=[nwk_b, kd_b], writes=[px_b])
                kb.op("dve", lambda e, px=px, s1=s1: e.scalar_tensor_tensor(out=xm[:], in0=P.ident_f[:], scalar=s1[:, 2:3], in1=px,
                                                                           op0=ALU.mult, op1=ALU.add),
                      reads=[px_b, s1_b, P.ident_f_b], writes=[xm_b])
                yield
                po, po_b = Q.next()
                kb.op("pe", lambda e, po=po: e.matmul(po, lhsT=qkt[:], rhs=u_[:], start=True, stop=False),
                      reads=[qkt_b, u_b], writes=[po_b])
                kb.op("pe", lambda e, po=po, S_=S_: e.matmul(po, lhsT=zt[:], rhs=S_[:], start=False, stop=True),
                      reads=[zt_b, S_b], writes=[po_b])
                pS, pS_b = Q.next()
                kb.op("pe", lambda e, pS=pS: e.matmul(pS, lhsT=kd[:], rhs=u_[:], start=True, stop=False),
                      reads=[kd_b, u_b], writes=[pS_b])
                kb.op("pe", lambda e, pS=pS, S_=S_: e.matmul(pS, lhsT=xm[:], rhs=S_[:], start=False, stop=True),
                      reads=[xm_b, S_b], writes=[pS_b])
                Sn, Sn_b = St.next()
                kb.op("act", lambda e, pS=pS, Sn=Sn: e.copy(out=Sn[:], in_=pS), reads=[pS_b], writes=[Sn_b])
                S_, S_b = Sn, Sn_b
                s2, s2_b = sm.next()
                kb.op("act", lambda e, po=po, s2=s2: e.activation(out=junk[:], in_=po, func=AF.Square, accum_out=s2[:, 0:1]),
                      reads=[po_b], writes=[junk_b, s2_b])
                kb.op("act", lambda e, s2=s2: e.activation(out=s2[:, 1:2], in_=s2[:, 0:1], func=AF.Sqrt, scale=1.0 / 128, bias=P.eps_t[:, 0:1]),
                      reads=[s2_b, P.eps_b], writes=[s2_b])
                kb.op("dve", lambda e, s2=s2: e.reciprocal(out=s2[:, 2:3], in_=s2[:, 1:2]), reads=[s2_b], writes=[s2_b])
                kb.op("dve", lambda e, po=po, s2=s2: e.scalar_tensor_tensor(out=yy[:], in0=po, scalar=s2[:, 2:3], in1=zna[:, n, :],
                                                                           op0=ALU.mult, op1=ALU.mult),
                      reads=[po_b, s2_b, zna_b], writes=[yy_b])
                yield
                pT, pT_b = Q.next()
                kb.op("pe", lambda e, pT=pT: e.transpose(out=pT, in_=yy[:], identity=P.ident_f[:]),
                      reads=[yy_b, P.ident_f_b], writes=[pT_b])
                if n % 4 == 0:
                    og, og_b = ogr.next()
                kb.op("act", lambda e, pT=pT, og=og, n=n: e.copy(out=og[:, (n % 4) * 128:(n % 4 + 1) * 128], in_=pT),
                      reads=[pT_b, og_b], writes=[og_b])
                if n % 4 == 3:
                    kb.dma("sp", P.obT_d[h, :, (n - 3) * 128:(n + 1) * 128], og[:], reads=[og_b])
                yield

        slots = [slot_bufs(k) for k in range(GDN_K)]
        heads = list(range(P.gdn_heads))
        for r0 in range(0, len(heads), GDN_K):
            gens = [head_gen(h, slots[k]) for k, h in enumerate(heads[r0:r0 + GDN_K])]
            while gens:
                for g_ in list(gens):
                    try:
                        next(g_)
                    except StopIteration:
                        gens.remove(g_)
        kb.barrier()


def merge_phase(P, kb, l):
    W = P.W
    with ExitStack() as es:
        oaT, oaT_b = kb.sb(es, "oaT", [128, 8, S], BF16)
        obT, obT_b = kb.sb(es, "obT", [128, 8, S], BF16)
        for c in range(8):
            kb.dma("sp", oaT[:, c, :], P.oaT_d[c], writes=[oaT_b])
            kb.dma("sp", obT[:, c, :], P.obT_d[c], writes=[obT_b])
        pa_p = Rot([kb.sb(es, f"pap{i}", [128, 8, 512], BF16) for i in range(2)])
        pb_p = Rot([kb.sb(es, f"pbp{i}", [128, 8, 512], BF16) for i in range(2)])
        psA = Rot([kb.ps(es, f"psA{i}", [128, 512], F32) for i in range(3)])
        psB = Rot([kb.ps(es, f"psB{i}", [128, 512], F32) for i in range(3)])
        gA = Rot([kb.sb(es, f"gA{i}", [128, 512], F32) for i in range(2)])
        gB = Rot([kb.sb(es, f"gB{i}", [128, 512], F32) for i in range(2)])
        t1 = Rot([kb.sb(es, f"t1{i}", [128, 512], F32) for i in range(2)])
        t2 = Rot([kb.sb(es, f"t2{i}", [128, 512], F32) for i in range(2)])
        uo = Rot([kb.sb(es, f"uo{i}", [128, 512], BF16) for i in range(3)])
        Wa = W["w_pa"][l].rearrange("(kc p) n -> p kc n", p=128)
        Wb = W["w_pb"][l].rearrange("(kc p) n -> p kc n", p=128)
        for p0 in range(0, D, 512):
            wa, wa_b = pa_p.next()
            wb, wb_b = pb_p.next()
            kb.dma("pool", wa[:], Wa[:, :, p0:p0 + 512], writes=[wa_b])
            kb.dma("pool", wb[:], Wb[:, :, p0:p0 + 512], writes=[wb_b])
            for cb in range(0, 512, 128):
                fb = (p0 + cb) // 128
                for t0 in range(0, S, 512):
                    pA, pA_b = psA.next()
                    pB, pB_b = psB.next()
                    for kc in range(8):
                        kb.op("pe", lambda e, kc=kc, pA=pA: e.matmul(pA[:], lhsT=wa[:, kc, cb:cb + 128], rhs=oaT[:, kc, t0:t0 + 512],
                                                                    start=(kc == 0), stop=(kc == 7)),
                              reads=[wa_b, oaT_b], writes=[pA_b], sig=(kc == 7))
                    for kc in range(8):
                        kb.op("pe", lambda e, kc=kc, pB=pB: e.matmul(pB[:], lhsT=wb[:, kc, cb:cb + 128], rhs=obT[:, kc, t0:t0 + 512],
                                                                    start=(kc == 0), stop=(kc == 7)),
                              reads=[wb_b, obT_b], writes=[pB_b], sig=(kc == 7))
                    ga, ga_b = gA.next()
                    gb_, gb_b = gB.next()
                    kb.dma("act", ga[:], P.mergeT_d[fb, :, t0:t0 + 512], writes=[ga_b])
                    kb.dma("act", gb_[:], P.mergeT_d[16 + fb, :, t0:t0 + 512], writes=[gb_b])
                    a1, a1_b = t1.next()
                    a2, a2_b = t2.next()
                    kb.op("dve", lambda e, pA=pA, ga=ga, a1=a1: e.tensor_tensor(out=a1[:], in0=pA[:], in1=ga[:], op=ALU.mult),
                          reads=[pA_b, ga_b], writes=[a1_b])
                    kb.op("dve", lambda e, pB=pB, gb_=gb_, a2=a2: e.tensor_tensor(out=a2[:], in0=pB[:], in1=gb_[:], op=ALU.mult),
                          reads=[pB_b, gb_b], writes=[a2_b])
                    uu, uu_b = uo.next()
                    kb.op("dve", lambda e, a1=a1, a2=a2, uu=uu: e.tensor_tensor(out=uu[:], in0=a1[:], in1=a2[:], op=ALU.add),
                          reads=[a1_b, a2_b], writes=[uu_b])
                    kb.dma("sp", P.uT_d[fb, :, t0:t0 + 512], uu[:], reads=[uu_b])
        kb.barrier()


def load_T(P, kb, dst, dst_b, src_d, nchunks, t0, ntok):
    for c in range(nchunks):
        kb.dma("sp", dst[:, c, 0:ntok], src_d[c, :, t0:t0 + ntok], writes=[dst_b])


def residual_epi(P, kb, es, src_ap, dst_ap, tok_off):
    rt = Rot([kb.sb(es, f"rt{i}", [128, 512], F32) for i in range(3)])
    ro = Rot([kb.sb(es, f"ro{i}", [128, 512], F32) for i in range(3)])

    def epi(ps, ps_b, j0, pw, t0, tw):
        r_, r_b = rt.next()
        o_, o_b = ro.next()
        r0 = tok_off + t0
        kb.dma("act", r_[:, 0:pw], src_ap[r0:r0 + 128, j0:j0 + pw], writes=[r_b])
        kb.op("dve", lambda e: e.tensor_tensor(out=o_[:, 0:pw], in0=ps[:, 0:pw], in1=r_[:, 0:pw], op=ALU.add),
              reads=[ps_b, r_b], writes=[o_b])
        kb.dma("sp", dst_ap[r0:r0 + 128, j0:j0 + pw], o_[:, 0:pw], reads=[o_b])
    return epi


def wo_phase(P, kb, l, x_src, x_dst):
    with ExitStack() as es:
        uT, uT_b = kb.sb(es, "uT", [128, 16, S], BF16)
        load_T(P, kb, uT, uT_b, P.uT_d, 16, 0, S)
        epi = residual_epi(P, kb, es, x_src, x_dst, 0)
        P.dense(kb, uT, uT_b, 16, S, P.W["w_o"][l], [(0, D, "a", epi)])


def mlp_phase(P, kb, l, x1, x2):
    W = P.W
    TT_ = 1024
    NQ = 4
    HQ = DFF // NQ
    with ExitStack() as es:
        panels = Rot([kb.sb(es, f"mwp{i}", [128, 16, 512], BF16) for i in range(3)])
        pss = Rot([kb.ps(es, f"mlps{i}", [128, 512], F32) for i in range(4)])
        actT, actT_b = kb.sb(es, "actT", [128, 16, TT_], BF16)
        h2T, h2T_b = kb.sb(es, "h2T", [128, 16, TT_], BF16)
        nres = P.norm_res(kb, es, W["ln2_w"][l], npts=3)
        rl = Rot([kb.sb(es, f"rl{i}", [128, 512], F32) for i in range(3)])
        rt = Rot([kb.sb(es, f"rt{i}", [128, 512], F32) for i in range(3)])
        ro = Rot([kb.sb(es, f"ro{i}", [128, 512], F32) for i in range(3)])
        xb = {}

        def epi_up(ps, ps_b, j0, cw, t0, tw):
            r_, r_b = rl.next()
            kb.op("act", lambda e: e.activation(out=r_[0:cw, 0:tw], in_=ps[0:cw, 0:tw], func=AF.Relu),
                  reads=[ps_b], writes=[r_b])
            kb.op("dve", lambda e: e.tensor_tensor(out=actT[0:cw, j0 // 128, t0:t0 + tw], in0=r_[0:cw, 0:tw],
                                                   in1=r_[0:cw, 0:tw], op=ALU.mult),
                  reads=[r_b, actT_b], writes=[actT_b])

        for tt in range(S // TT_):
            tok0 = tt * TT_
            P.norm_T(kb, x1, W["ln2_w"][l], h2T, h2T_b, TT_, tok0=tok0, res=nres)
            for qd in range(NQ):
                src = x1 if qd == 0 else x2

                def epi_dn(ps, ps_b, j0, pw, t0, tw, src=src):
                    r_, r_b = rt.next()
                    o_, o_b = ro.next()
                    r0 = tok0 + t0
                    key = (r0, j0)
                    if key not in xb:
                        xb[key] = Buf(f"x2_{r0}_{j0}")
                    kb.dma("act", r_[:, 0:pw], src[r0:r0 + 128, j0:j0 + pw], reads=[xb[key]], writes=[r_b])
                    kb.op("dve", lambda e: e.tensor_tensor(out=o_[:, 0:pw], in0=ps[:, 0:pw], in1=r_[:, 0:pw], op=ALU.add),
                          reads=[ps_b, r_b], writes=[o_b])
                    kb.dma("sp", x2[r0:r0 + 128, j0:j0 + pw], o_[:, 0:pw], reads=[o_b], writes=[xb[key]])

                P.dense_core(kb, h2T, h2T_b, 16, TT_, W["w_up"][l][:, qd * HQ:(qd + 1) * HQ], [(0, HQ, "b", epi_up)], panels, pss)
                P.dense_core(kb, actT, actT_b, 16, TT_, W["w_down"][l][qd * HQ:(qd + 1) * HQ, :], [(0, D, "a", epi_dn)], panels, pss)
        kb.barrier()


def final_norm(P, kb, x_src, lnw_ap, out_ap):
    with ExitStack() as es:
        lnw, lnw_b = kb.sb(es, "flnw", [128, D], F32)
        kb.dma("sp", lnw[:], bass.AP(tensor=lnw_ap.tensor, offset=lnw_ap.offset, ap=[[0, 128], [1, D]]), writes=[lnw_b])
        xts = Rot([kb.sb(es, f"fxt{i}", [128, D], F32) for i in range(2)])
        ots = Rot([kb.sb(es, f"fot{i}", [128, D], F32) for i in range(2)])
        junk, junk_b = kb.sb(es, "fjunk", [128, D], BF16)
        sts = Rot([kb.sb(es, f"fst{i}", [128, 4], F32) for i in range(2)])
        for tt in range(S // 128):
            xt, xt_b = xts.next()
            ot, ot_b = ots.next()
            st, st_b = sts.next()
            kb.dma("sp", xt[:], x_src[tt * 128:(tt + 1) * 128, :], writes=[xt_b])
            kb.op("act", lambda e: e.activation(out=junk[:], in_=xt[:], func=AF.Square, accum_out=st[:, 0:1]),
                  reads=[xt_b], writes=[junk_b, st_b])
            kb.op("act", lambda e: e.activation(out=st[:, 1:2], in_=st[:, 0:1], func=AF.Sqrt, scale=1.0 / D, bias=P.eps_t[:, 0:1]),
                  reads=[st_b, P.eps_b], writes=[st_b])
            kb.op("dve", lambda e: e.reciprocal(out=st[:, 2:3], in_=st[:, 1:2]), reads=[st_b], writes=[st_b])
            kb.op("dve", lambda e: e.scalar_tensor_tensor(out=ot[:], in0=xt[:], scalar=st[:, 2:3], in1=lnw[:], op0=ALU.mult, op1=ALU.mult),
                  reads=[xt_b, st_b, lnw_b], writes=[ot_b])
            kb.dma("sp", out_ap[tt * 128:(tt + 1) * 128, :], ot[:], reads=[ot_b])
        kb.barrier()
```

```python
import math
import numpy as np
from contextlib import ExitStack
import concourse.bass as bass
import concourse.mybir as mybir
from concourse.bass_utils import run_bass_kernel_spmd

F32 = mybir.dt.float32
BF16 = mybir.dt.bfloat16
I32 = mybir.dt.int32
ALU = mybir.AluOpType
AF = mybir.ActivationFunctionType
AX = mybir.AxisListType

NCORES = 8
BPC = 2
S = 2048
D = 2048
DEPTH = 2
DFF = 8192
IN_COLS = 10792
EPS = 1e-6
NEG = -30000.0

C_Q = 0
C_KV = 1024
C_GATE = 2560
C_GQKV = 2584
C_Z = 5656
C_A = 6680
C_B = 6688
C_MERGE = 6696


class Buf:
    __slots__ = ("name", "w", "r")

    def __init__(self, name):
        self.name = name
        self.w = None
        self.r = {}


class KB:
    SEM_LIMIT = 30000

    def __init__(self, nc, es, n_dma_slots=8):
        self.nc = nc
        self.es = es
        self.eng = {"pe": nc.tensor, "act": nc.scalar, "dve": nc.vector, "pool": nc.gpsimd, "sp": nc.sync}
        self.sems = {}
        self.cur = {}
        self.epoch = {}
        for e in self.eng:
            self.epoch[e] = 0
            self._new_sem(e)
        self.seen = {e: {} for e in self.eng}
        self.dma_slots = {}
        for q in ("sp", "act", "pool"):
            sl = []
            for i in range(n_dma_slots):
                key = f"d_{q}_{i}"
                self.sems[key] = es.enter_context(nc.semaphore(key))
                sl.append([key, 0])
            self.dma_slots[q] = [sl, 0]
        self.n_ins = 0
        self.n_wait = 0
        self.uid = 0
        self.last_tok = {}
        self.know = {}

    def _new_sem(self, e):
        key = f"s_{e}_{self.epoch[e]}"
        self.sems[key] = self.es.enter_context(self.nc.semaphore(key))
        self.cur[e] = [key, 0]
        self.epoch[e] += 1

    def _need(self, e, toks):
        seen = self.seen[e]
        best = {}
        for t in toks:
            if t is None:
                continue
            k, v = t
            if seen.get(k, 0) >= v:
                continue
            if best.get(k, 0) < v:
                best[k] = v
        for k, v in best.items():
            if seen.get(k, 0) >= v:
                continue
            self.eng[e].wait_ge(self.sems[k], v)
            seen[k] = v
            self.n_wait += 1
            kn = self.know.get((k, v))
            if kn:
                for k2, v2 in kn.items():
                    if seen.get(k2, 0) < v2:
                        seen[k2] = v2

    def _deps(self, reads, writes, skip_waw_key=None):
        toks = []
        for b in reads:
            toks.append(b.w)
        for b in writes:
            if b.w is not None and not (skip_waw_key is not None and b.w[0] == skip_waw_key):
                toks.append(b.w)
            for k, v in b.r.items():
                toks.append((k, v))
        return toks

    def _record(self, tok, reads, writes):
        k, v = tok
        for b in reads:
            if b.r.get(k, 0) < v:
                b.r[k] = v
        for b in writes:
            b.w = tok
            b.r = {}

    def op(self, e, fn, reads=(), writes=(), sig=True):
        cur = self.cur[e]
        if cur[1] >= self.SEM_LIMIT:
            self._new_sem(e)
            cur = self.cur[e]
        skip = cur[0] if e == "pe" else None
        self._need(e, self._deps(reads, writes, skip_waw_key=skip))
        ins = fn(self.eng[e])
        self.n_ins += 1
        tok = (cur[0], cur[1] + 1)
        if sig:
            ins.then_inc(self.sems[cur[0]], 1)
            cur[1] += 1
            self.last_tok[e] = tok
            self.know[tok] = dict(self.seen[e])
        self._record(tok, reads, writes)
        return ins

    def dma(self, q, out, in_, reads=(), writes=(), **kw):
        sl, idx = self.dma_slots[q]
        slot = sl[idx % len(sl)]
        self.dma_slots[q][1] = idx + 1
        key, uses = slot
        toks = self._deps(reads, writes)
        if uses > 0:
            toks.append((key, 16 * uses))
        self._need(q, toks)
        ins = self.eng[q].dma_start(out=out, in_=in_, **kw)
        ins.then_inc(self.sems[key], 16)
        slot[1] = uses + 1
        self.n_ins += 1
        tok = (key, 16 * (uses + 1))
        self.know[tok] = dict(self.seen[q])
        self._record(tok, reads, writes)
        return ins

    def barrier(self):
        toks = []
        for e in self.eng:
            if e in self.last_tok:
                toks.append(self.last_tok[e])
        for q in self.dma_slots:
            for key, uses in self.dma_slots[q][0]:
                if uses > 0:
                    toks.append((key, 16 * uses))
        for e in self.eng:
            self._need(e, toks)

    def sb(self, es, name, shape, dtype):
        self.uid += 1
        t = es.enter_context(self.nc.sbuf_tensor(f"{name}_{self.uid}", list(shape), dtype))
        return t, Buf(name)

    def ps(self, es, name, shape, dtype=F32):
        self.uid += 1
        t = es.enter_context(self.nc.psum_tensor(f"{name}_{self.uid}", list(shape), dtype))
        return t, Buf(name)


class Rot:
    def __init__(self, items):
        self.items = items
        self.i = 0

    def next(self):
        it = self.items[self.i % len(self.items)]
        self.i += 1
        return it


class Prog:
    def __init__(self, nc, dbg=()):
        self.nc = nc
        self.dbg = set(dbg)
        self.ext_out = {}

    def dram(self, name, shape, dtype):
        kind = "ExternalOutput" if name in self.dbg else "Internal"
        if ("in:" + name) in self.dbg:
            kind = "ExternalInput"
        t = self.nc.dram_tensor(name, list(shape), dtype, kind=kind)
        if name in self.dbg:
            self.ext_out[name] = t
        return t.ap()

    def setup(self, kb, es):
        nc = self.nc
        self.ident_f, self.ident_f_b = kb.sb(es, "identf", [128, 128], F32)
        self.ident_b, self.ident_b_b = kb.sb(es, "identb", [128, 128], BF16)
        kb.op("pool", lambda e: e.memset(self.ident_f[:], 0.0), writes=[self.ident_f_b])
        kb.op("pool", lambda e: e.affine_select(out=self.ident_f[:], in_=self.ident_f[:], pattern=[[-1, 128]],
                                                compare_op=ALU.not_equal, fill=1.0, base=0, channel_multiplier=1),
              reads=[self.ident_f_b], writes=[self.ident_f_b])
        kb.op("dve", lambda e: e.tensor_copy(out=self.ident_b[:], in_=self.ident_f[:]),
              reads=[self.ident_f_b], writes=[self.ident_b_b])
        self.eps_t, self.eps_b = kb.sb(es, "eps", [128, 1], F32)
        kb.op("pool", lambda e: e.memset(self.eps_t[:], EPS), writes=[self.eps_b])
        self.one_t, self.one_b = kb.sb(es, "onec", [128, 1], F32)
        kb.op("pool", lambda e: e.memset(self.one_t[:], 1.0), writes=[self.one_b])

    def norm_res(self, kb, es, lnw_ap, npts=4):
        R = {}
        R["lnw"] = kb.sb(es, "lnw", [128, D], F32)
        kb.dma("sp", R["lnw"][0][:], bass.AP(tensor=lnw_ap.tensor, offset=lnw_ap.offset, ap=[[0, 128], [1, D]]),
               writes=[R["lnw"][1]])
        R["xts"] = Rot([kb.sb(es, f"xt{i}", [128, D], F32) for i in range(2)])
        R["hbs"] = Rot([kb.sb(es, f"hb{i}", [128, D], BF16) for i in range(2)])
        R["junk"] = kb.sb(es, "junk", [128, D], BF16)
        R["sts"] = Rot([kb.sb(es, f"st{i}", [128, 4], F32) for i in range(2)])
        R["pts"] = Rot([kb.ps(es, f"pt{i}", [128, 4, 128], BF16) for i in range(npts)])
        return R

    def norm_T(self, kb, x_ap, lnw_ap, hT, hT_b, ntok, tok0=0, res=None):
        nc = self.nc
        with ExitStack() as es:
            R = res if res is not None else self.norm_res(kb, es, lnw_ap)
            lnw, lnw_b = R["lnw"]
            xts, hbs, sts, pts = R["xts"], R["hbs"], R["sts"], R["pts"]
            junk, junk_b = R["junk"]
            for tt in range(ntok // 128):
                xt, xt_b = xts.next()
                hb, hb_b = hbs.next()
                st, st_b = sts.next()
                r0 = tok0 + tt * 128
                kb.dma("sp", xt[:], x_ap[r0:r0 + 128, :], writes=[xt_b])
                kb.op("act", lambda e: e.activation(out=junk[:], in_=xt[:], func=AF.Square, accum_out=st[:, 0:1]),
                      reads=[xt_b], writes=[junk_b, st_b])
                kb.op("act", lambda e: e.activation(out=st[:, 1:2], in_=st[:, 0:1], func=AF.Sqrt, scale=1.0 / D, bias=self.eps_t[:, 0:1]),
                      reads=[st_b, self.eps_b], writes=[st_b])
                kb.op("dve", lambda e: e.reciprocal(out=st[:, 2:3], in_=st[:, 1:2]), reads=[st_b], writes=[st_b])
                kb.op("dve", lambda e: e.scalar_tensor_tensor(out=hb[:], in0=xt[:], scalar=st[:, 2:3], in1=lnw[:],
                                                              op0=ALU.mult, op1=ALU.mult),
                      reads=[xt_b, st_b, lnw_b], writes=[hb_b])
                for g in range(4):
                    pt, pt_b = pts.next()
                    for j in range(4):
                        c = g * 4 + j
                        kb.op("pe", lambda e, c=c, j=j: e.transpose(out=pt[:, j, :], in_=hb[:, c * 128:(c + 1) * 128],
                                                                     identity=self.ident_b[:]),
                              reads=[hb_b, self.ident_b_b], writes=[pt_b], sig=(j == 3))
                    eng = "act" if g % 2 == 0 else "dve"
                    dst = hT[:, g * 4:(g + 1) * 4, tt * 128:(tt + 1) * 128]
                    if eng == "act":
                        kb.op("act", lambda e: e.copy(out=dst, in_=pt[:]), reads=[pt_b], writes=[hT_b])
                    else:
                        kb.op("dve", lambda e: e.tensor_copy(out=dst, in_=pt[:]), reads=[pt_b], writes=[hT_b])
            if res is None:
                kb.barrier()

    def dense(self, kb, inT, inT_b, KC, ntok, W_ap, jobs, wq="pool"):
        PW = 512
        with ExitStack() as es:
            panels = Rot([kb.sb(es, f"wp{i}", [128, KC, PW], BF16) for i in range(2)])
            pss = Rot([kb.ps(es, f"dps{i}", [128, 512], F32) for i in range(4)])
            self.dense_core(kb, inT, inT_b, KC, ntok, W_ap, jobs, panels, pss, wq)
            kb.barrier()

    def dense_core(self, kb, inT, inT_b, KC, ntok, W_ap, jobs, panels, pss, wq="pool"):
        PW = 512
        if True:
            Wv = W_ap.rearrange("(kc p) n -> p kc n", p=128)
            for (c0, ncols, form, epi) in jobs:
                for p0 in range(0, ncols, PW):
                    pw = min(PW, ncols - p0)
                    wp, wp_b = panels.next()
                    kb.dma(wq, wp[:, 0:KC, 0:pw], Wv[:, :, c0 + p0:c0 + p0 + pw], writes=[wp_b])
                    if form == "b":
                        for cb in range(0, pw, 128):
                            cw = min(128, pw - cb)
                            for t0 in range(0, ntok, 512):
                                tw = min(512, ntok - t0)
                                ps, ps_b = pss.next()
                                for kc in range(KC):
                                    kb.op("pe", lambda e, kc=kc: e.matmul(ps[0:cw, 0:tw], lhsT=wp[:, kc, cb:cb + cw],
                                                                         rhs=inT[:, kc, t0:t0 + tw],
                                                                         start=(kc == 0), stop=(kc == KC - 1)),
                                          reads=[wp_b, inT_b], writes=[ps_b], sig=(kc == KC - 1))
                                epi(ps, ps_b, p0 + cb, cw, t0, tw)
                    else:
                        for t0 in range(0, ntok, 128):
                            ps, ps_b = pss.next()
                            for kc in range(KC):
                                kb.op("pe", lambda e, kc=kc: e.matmul(ps[:, 0:pw], lhsT=inT[:, kc, t0:t0 + 128],
                                                                     rhs=wp[:, kc, 0:pw],
                                                                     start=(kc == 0), stop=(kc == KC - 1)),
                                      reads=[wp_b, inT_b], writes=[ps_b], sig=(kc == KC - 1))
                            epi(ps, ps_b, p0, pw, t0, 128)

    def make_stage(self, kb, es, n=4):
        self.stg_f = Rot([kb.sb(es, f"stgf{i}", [128, 512], F32) for i in range(n)])
        self.stg_h = Rot([kb.sb(es, f"stgh{i}", [128, 512], BF16) for i in range(n)])
        self.evac_i = 0

    def evac(self, kb, ps, ps_b, rows, cols, dst_ap, dtype=F32, func=None, eng=None):
        st, st_b = (self.stg_f if dtype == F32 else self.stg_h).next()
        if eng is None:
            eng = "act" if (func is not None or self.evac_i % 2 == 0) else "dve"
        self.evac_i += 1
        if eng == "act":
            f = func if func is not None else AF.Copy
            kb.op("act", lambda e: e.activation(out=st[0:rows, 0:cols], in_=ps[0:rows, 0:cols], func=f),
                  reads=[ps_b], writes=[st_b])
        else:
            kb.op("dve", lambda e: e.tensor_copy(out=st[0:rows, 0:cols], in_=ps[0:rows, 0:cols]),
                  reads=[ps_b], writes=[st_b])
        kb.dma("sp", dst_ap, st[0:rows, 0:cols], reads=[st_b])

    def alloc_proj_scratch(self):
        self.qT_d = self.dram("qT_d", [8, 128, S], BF16)
        self.kvT_d = self.dram("kvT_d", [4, 2, 128, S], BF16)
        self.vtok_d = self.dram("vtok_d", [2, S, 256], BF16)
        self.gate_d = self.dram("gate_d", [S, 24], F32)
        self.gqkvT_d = self.dram("gqkvT_d", [24, 128, S], F32)
        self.z_d = self.dram("z_d", [S, 1024], F32)
        self.ab_d = self.dram("ab_d", [S, 16], F32)
        self.mergeT_d = self.dram("mergeT_d", [32, 128, S], F32)

    def proj_phase(self, kb, hT, hT_b, w_in_l, nw_ap):
        with ExitStack() as es:
            self.make_stage(kb, es)

            def epi_q(ps, ps_b, j0, cw, t0, tw):
                self.evac(kb, ps, ps_b, cw, tw, self.qT_d[j0 // 128, :, t0:t0 + tw], dtype=BF16)

            def epi_kvT(kind):
                def f(ps, ps_b, j0, cw, t0, tw):
                    self.evac(kb, ps, ps_b, cw, tw, self.kvT_d[kind, j0 // 128, :, t0:t0 + tw], dtype=BF16)
                return f

            def epi_vtok(kind):
                def f(ps, ps_b, j0, pw, t0, tw):
                    self.evac(kb, ps, ps_b, 128, pw, self.vtok_d[kind, t0:t0 + 128, j0:j0 + pw], dtype=BF16)
                return f

            def epi_gate(ps, ps_b, j0, pw, t0, tw):
                self.evac(kb, ps, ps_b, 128, pw, self.gate_d[t0:t0 + 128, :], func=AF.Sigmoid)

            def epi_gqkv(ps, ps_b, j0, cw, t0, tw):
                self.evac(kb, ps, ps_b, cw, tw, self.gqkvT_d[j0 // 128, :, t0:t0 + tw])

            nwrep, nwrep_b = kb.sb(es, "nwrep", [128, 512], F32)
            for r_ in range(4):
                kb.dma("sp", nwrep[:, r_ * 128:(r_ + 1) * 128],
                       bass.AP(tensor=nw_ap.tensor, offset=nw_ap.offset, ap=[[0, 128], [1, 128]]), writes=[nwrep_b])

            def epi_z(ps, ps_b, j0, pw, t0, tw):
                st, st_b = self.stg_f.next()
                kb.op("act", lambda e: e.activation(out=st[:, 0:pw], in_=ps[:, 0:pw], func=AF.Silu), reads=[ps_b], writes=[st_b])
                kb.op("dve", lambda e: e.tensor_tensor(out=st[:, 0:pw], in0=st[:, 0:pw], in1=nwrep[:, 0:pw], op=ALU.mult),
                      reads=[st_b, nwrep_b], writes=[st_b])
                kb.dma("sp", self.z_d[t0:t0 + 128, j0:j0 + pw], st[:, 0:pw], reads=[st_b])

            def epi_ab(ps, ps_b, j0, pw, t0, tw):
                self.evac(kb, ps, ps_b, 128, pw, self.ab_d[t0:t0 + 128, :])

            def epi_merge(ps, ps_b, j0, cw, t0, tw):
                self.evac(kb, ps, ps_b, cw, tw, self.mergeT_d[j0 // 128, :, t0:t0 + tw], func=AF.Sigmoid)

            jobs = [
                (C_Q, 1024, "b", epi_q),
                (C_KV + 0, 256, "b", epi_kvT(0)),
                (C_KV + 256, 256, "b", epi_kvT(1)),
                (C_KV + 512, 256, "b", epi_kvT(2)),
                (C_KV + 768, 256, "a", epi_vtok(0)),
                (C_KV + 1024, 256, "b", epi_kvT(3)),
                (C_KV + 1280, 256, "a", epi_vtok(1)),
                (C_GATE, 24, "a", epi_gate),
                (C_GQKV, 3072, "b", epi_gqkv),
                (C_Z, 1024, "a", epi_z),
                (C_A, 16, "a", epi_ab),
                (C_MERGE, 4096, "b", epi_merge),
            ]
            self.dense(kb, hT, hT_b, 16, S, w_in_l, jobs)


WNAMES = [("rel_table", [32, 8]), ("ln1_w", [DEPTH, D]), ("w_in", [DEPTH, D, IN_COLS]),
          ("cmp_pe_k", [DEPTH, 32, 128]), ("cmp_pe_v", [DEPTH, 32, 128]),
          ("cmp_w1_k", [DEPTH, 4096, 256]), ("cmp_w2_k", [DEPTH, 256, 128]),
          ("cmp_w1_v", [DEPTH, 4096, 256]), ("cmp_w2_v", [DEPTH, 256, 128]),
          ("conv_w", [DEPTH, 4, 3072]), ("a_log", [DEPTH, 8]), ("dt_bias", [DEPTH, 8]),
          ("gdn_norm_w", [DEPTH, 128]), ("w_pa", [DEPTH, 1024, D]), ("w_pb", [DEPTH, 1024, D]),
          ("w_o", [DEPTH, D, D]), ("ln2_w", [DEPTH, D]), ("w_up", [DEPTH, D, DFF]),
          ("w_down", [DEPTH, DFF, D]), ("ln_f_w", [D])]


class LazyW:
    def __init__(self, nc, depth):
        self.nc = nc
        self.depth = depth
        self.aps = {}
        self.shapes = dict(WNAMES)

    def __getitem__(self, n):
        if n not in self.aps:
            shp = list(self.shapes[n])
            if len(shp) > 1 and shp[0] == DEPTH and n != "rel_table":
                shp[0] = self.depth
            self.aps[n] = self.nc.dram_tensor(n, shp, F32, kind="ExternalInput").ap()
        return self.aps[n]


def build(dbg=(), stop=None, nseq=BPC, depth=DEPTH):
    nc = bass.Bass("TRN2", target_bir_lowering=False)
    P = Prog(nc, dbg)
    x = nc.dram_tensor("x", [BPC, S, D], F32, kind="ExternalInput").ap()
    W = LazyW(nc, depth)
    P.W = W
    out = nc.dram_tensor("out", [BPC, S, D], F32, kind="ExternalOutput").ap()
    P.alloc_proj_scratch()
    P.oaT_d = P.dram("oaT_d", [8, 128, S], BF16)
    P.obT_d = P.dram("obT_d", [8, 128, S], BF16)
    P.uT_d = P.dram("uT_d", [16, 128, S], BF16)
    X1 = P.dram("X1_d", [BPC, S, D], F32)
    X2 = P.dram("X2_d", [BPC, S, D], F32)
    P.dbg_nsa = None
    P.dbg_gdn = None
    P.dbg_gdn_n = 0
    P.gdn_heads = 8
    if "gdn_dbg" in P.dbg:
        P.dbg_gdn = {nm: nc.dram_tensor("dbg_" + nm, [128, 128], F32, kind="ExternalOutput").ap()
                     for nm in ["q_tok", "k_tok", "v_tok", "e1", "e2", "TT", "u", "negw", "zt", "xm", "qkt", "yy", "kd", "Sn", "gcol"]}
        P.gdn_heads = GDN_TEST_HEADS
    if "nsa_dbg" in P.dbg:
        P.dbg_nsa = {"kcT": nc.dram_tensor("dbg_kcT", [128, 128], BF16, kind="ExternalOutput").ap(),
                     "vc": nc.dram_tensor("dbg_vc", [128, 164], F32, kind="ExternalOutput").ap(),
                     "imp": nc.dram_tensor("dbg_imp", [128, 16, 32], F32, kind="ExternalOutput").ap(),
                     "selbT": nc.dram_tensor("dbg_selbT", [32, S], BF16, kind="ExternalOutput").ap()}
    with ExitStack() as es:
        kb = KB(nc, es)
        P.setup(kb, es)
        nsa_setup(P, kb, es, W["rel_table"])
        gdn_setup(P, kb, es)
        kb.barrier()
        if stop == "nsa_only":
            nsa_phase(P, kb, 0)
        if stop == "gdn_only":
            gdn_phase(P, kb, 0)
        if stop == "post_only":
            merge_phase(P, kb, 0)
            wo_phase(P, kb, 0, x[0], X1[0])
            mlp_phase(P, kb, 0, X1[0], X2[0])
            final_norm(P, kb, X2[0], W["ln_f_w"], out[0])
        for l in range(depth if stop not in ("setup", "nsa_only", "gdn_only", "post_only") else 0):
            for s in range(nseq):
                x_in = x[s] if l == 0 else X2[s]
                with ExitStack() as es_h:
                    hT, hT_b = kb.sb(es_h, "hT", [128, 16, S], BF16)
                    P.norm_T(kb, x_in, W["ln1_w"][l], hT, hT_b, S)
                    if stop == "norm":
                        hT_d = P.dram("hT_d", [128, 16, S], BF16)
                        kb.dma("sp", hT_d, hT[:], reads=[hT_b])
                        break
                    P.proj_phase(kb, hT, hT_b, W["w_in"][l], W["gdn_norm_w"][l])
                    kb.barrier()
                if stop == "proj":
                    break
                nsa_phase(P, kb, l)
                if stop == "nsa":
                    break
                gdn_phase(P, kb, l)
                merge_phase(P, kb, l)
                wo_phase(P, kb, l, x_in, X1[s])
                mlp_phase(P, kb, l, X1[s], X2[s])
            if stop is not None:
                break
        if stop is None:
            for s in range(nseq):
                final_norm(P, kb, X2[s], W["ln_f_w"], out[s])
        kb.barrier()
        print("instructions", kb.n_ins, "waits", kb.n_wait)
    return nc, P


_CACHE = {}


def kernel(**inputs):
    if "nc" not in _CACHE:
        _CACHE["nc"] = build()
    nc, P = _CACHE["nc"]
    x = np.ascontiguousarray(inputs["x"], dtype=np.float32)
    wts = {n: np.ascontiguousarray(inputs[n], dtype=np.float32) for n in P.W.aps}
    in_maps = []
    for c in range(NCORES):
        m = dict(wts)
        m["x"] = np.ascontiguousarray(x[c * BPC:(c + 1) * BPC])
        in_maps.append(m)
    res = run_bass_kernel_spmd(nc, in_maps, core_ids=list(range(NCORES)))
    return np.concatenate([np.asarray(r["out"], dtype=np.float32) for r in res.results], axis=0)


SQ = math.sqrt(128.0)
SCALE = 1.0 / SQ
OFFW, WW = 384, 1408
OFFS, WS = 384, 1024
LGW = 127 + WW
LGS = 127 + WS
LGC = 4080
LG = 4096


def _bucket_ranges():
    d = np.arange(0, 4200)
    nf = np.maximum(d, 1).astype(np.float32)
    large = 16 + (np.log(nf / np.float32(16)) / np.float32(math.log(128 / 16)) * np.float32(16)).astype(np.int32)
    large = np.minimum(large, 31)
    bk = np.where(d < 16, d, large)
    out = []
    for b in range(32):
        idx = np.nonzero(bk == b)[0]
        out.append((b, int(idx[0]), int(idx[-1]) + 1))
    return out


def nsa_setup(P, kb, es, rel_table):
    nc = P.nc
    P.Mw_d = P.dram("Mw_d", [8, 128, WW], BF16)
    P.Ms_d = P.dram("Ms_d", [8, 128, WS], BF16)
    P.Mc_d = P.dram("Mc_d", [8, 128, S], BF16)
    G_d = P.dram("G_d", [3, 8, LG], BF16)
    P.t31, P.t31_b = kb.sb(es, "t31", [128, 8], F32)
    kb.dma("sp", P.t31[:], bass.AP(tensor=rel_table.tensor, offset=rel_table.offset + 31 * 8, ap=[[0, 128], [1, 8]]),
           writes=[P.t31_b])
    P.Jb, P.Jb_b = kb.sb(es, "Jb", [128, 128], BF16)
    P.I30k, P.I30k_b = kb.sb(es, "I30k", [128, 128], BF16)
    P.expall, P.expall_b = kb.sb(es, "expall", [128, S], BF16)
    P.FB, P.FB_b = kb.sb(es, "FB", [128, 16, 32], F32)
    P.cover, P.cover_b = kb.sb(es, "cover", [128, 32], F32)
    with ExitStack() as es2:
        tmpf, tmpf_b = kb.sb(es2, "tmpf", [128, 128], F32)
        kb.op("pool", lambda e: e.memset(tmpf[:], 0.0), writes=[tmpf_b])
        kb.op("pool", lambda e: e.affine_select(out=tmpf[:], in_=tmpf[:], pattern=[[1, 128]], compare_op=ALU.not_equal,
                                                fill=1.0, base=-127, channel_multiplier=1), reads=[tmpf_b], writes=[tmpf_b])
        kb.op("dve", lambda e: e.tensor_copy(out=P.Jb[:], in_=tmpf[:]), reads=[tmpf_b], writes=[P.Jb_b])
        kb.op("dve", lambda e: e.tensor_scalar(out=P.I30k[:], in0=P.ident_f[:], scalar1=30000.0, scalar2=None, op0=ALU.mult),
              reads=[P.ident_f_b], writes=[P.I30k_b])
        ex, ex_b = kb.sb(es2, "ex", [32, S], F32)
        kb.op("pool", lambda e: e.memset(ex[:], 1.0), writes=[ex_b])
        kb.op("pool", lambda e: e.affine_select(out=ex[:], in_=ex[:], pattern=[[1, S]], compare_op=ALU.is_ge, fill=0.0,
                                                base=0, channel_multiplier=-64), reads=[ex_b], writes=[ex_b])
        kb.op("pool", lambda e: e.affine_select(out=ex[:], in_=ex[:], pattern=[[-1, S]], compare_op=ALU.is_ge, fill=0.0,
                                                base=63, channel_multiplier=64), reads=[ex_b], writes=[ex_b])
        kb.op("pool", lambda e: e.memset(P.expall[:], 0.0), writes=[P.expall_b])
        kb.op("dve", lambda e: e.tensor_copy(out=P.expall[0:32, :], in_=ex[:]), reads=[ex_b, P.expall_b], writes=[P.expall_b])
        kb.op("pool", lambda e: e.memset(P.cover[:], 1.0), writes=[P.cover_b])
        kb.op("pool", lambda e: e.affine_select(out=P.cover[:], in_=P.cover[:], pattern=[[64, 32]], compare_op=ALU.is_ge,
                                                fill=0.0, base=63, channel_multiplier=-16), reads=[P.cover_b], writes=[P.cover_b])
        kb.op("pool", lambda e: e.affine_select(out=P.cover[:], in_=P.cover[:], pattern=[[-64, 32]], compare_op=ALU.is_ge,
                                                fill=0.0, base=31, channel_multiplier=16), reads=[P.cover_b], writes=[P.cover_b])
        kb.op("pool", lambda e: e.memset(P.FB[:], 0.0), writes=[P.FB_b])
        for st in range(16):
            for half in range(2):
                c = 2 * st + half
                ps_ = slice(64 * half, 64 * half + 64)
                if c + 1 < 32:
                    kb.op("pool", lambda e, ps_=ps_, c=c, st=st: e.memset(P.FB[ps_, st, c + 1:32], -1e30),
                          reads=[P.FB_b], writes=[P.FB_b])
                for m in sorted(set([0, c, max(c - 1, 0)])):
                    kb.op("pool", lambda e, ps_=ps_, m=m, st=st: e.memset(P.FB[ps_, st, m:m + 1], 1000.0),
                          reads=[P.FB_b], writes=[P.FB_b])
        tabT, tabT_b = kb.sb(es2, "tabT", [8, 32], F32)
        with nc.allow_non_contiguous_dma(reason="tiny table transpose"):
            kb.dma("sp", tabT[:], rel_table.rearrange("b h -> h b"), writes=[tabT_b])
        zer, zer_b = kb.sb(es2, "zer", [8, LG], F32)
        kb.op("pool", lambda e: e.memset(zer[:], 0.0), writes=[zer_b])
        rngs = _bucket_ranges()
        for kind, (off, dmax, L) in enumerate([(127 + OFFW, 512, LGW), (127 + OFFS, None, LGS), (2063, None, LGC)]):
            G, G_b = kb.sb(es2, f"G{kind}", [8, LG], F32)
            Gh, Gh_b = kb.sb(es2, f"Gh{kind}", [8, LG], BF16)
            kb.op("pool", lambda e, G=G: e.memset(G[:], NEG), writes=[G_b])
            for (b, lo, hi) in rngs:
                if b == 31:
                    hi = 10 ** 6
                if dmax is not None:
                    hi = min(hi, dmax)
                a0 = lo + off
                a1 = min(hi + off, L)
                if a1 <= a0:
                    continue
                kb.op("dve", lambda e, G=G, a0=a0, a1=a1, b=b: e.tensor_scalar(
                    out=G[:, a0:a1], in0=zer[:, a0:a1], scalar1=tabT[:, b:b + 1], scalar2=SQ, op0=ALU.add, op1=ALU.mult),
                    reads=[zer_b, tabT_b, G_b], writes=[G_b])
            kb.op("dve", lambda e, G=G, Gh=Gh: e.tensor_copy(out=Gh[:], in_=G[:]), reads=[G_b], writes=[Gh_b])
            kb.dma("sp", G_d[kind], Gh[:], reads=[Gh_b])
        kb.barrier()
        mps = Rot([kb.ps(es2, f"mps{i}", [128, 512], F32) for i in range(2)])
        mrev = Rot([kb.sb(es2, f"mrev{i}", [128, S], BF16) for i in range(2)])
        mout = Rot([kb.sb(es2, f"mout{i}", [128, S], BF16) for i in range(2)])
        for kind, (W_, pstep, dst) in enumerate([(WW, 1, P.Mw_d), (WS, 1, P.Ms_d), (S, 16, P.Mc_d)]):
            for h in range(8):
                mr, mr_b = mrev.next()
                mo, mo_b = mout.next()
                g_ap = G_d[kind, h]
                kb.dma("sp", mr[:, 0:W_], bass.AP(tensor=g_ap.tensor, offset=g_ap.offset, ap=[[pstep, 128], [1, W_]]),
                       writes=[mr_b])
                for c0 in range(0, W_, 512):
                    cw = min(512, W_ - c0)
                    ps, ps_b = mps.next()
                    kb.op("pe", lambda e, ps=ps, mr=mr, c0=c0, cw=cw: e.matmul(ps[:, 0:cw], lhsT=P.Jb[:], rhs=mr[:, c0:c0 + cw],
                                                                              start=True, stop=True),
                          reads=[P.Jb_b, mr_b], writes=[ps_b])
                    kb.op("act", lambda e, ps=ps, mo=mo, c0=c0, cw=cw: e.copy(out=mo[:, c0:c0 + cw], in_=ps[:, 0:cw]),
                          reads=[ps_b], writes=[mo_b])
                kb.dma("sp", dst[h], mo[:, 0:W_], reads=[mo_b])
        kb.barrier()


def nsa_phase(P, kb, l):
    nc = P.nc
    W = P.W
    with ExitStack() as es:
        qT, qT_b = kb.sb(es, "qT", [128, 8, S], BF16)
        for h in range(8):
            kb.dma("sp", qT[:, h, :], P.qT_d[h], writes=[qT_b])
        gates, gates_b = kb.sb(es, "gates", [128, 16, 24], F32)
        kb.dma("sp", gates[:], P.gate_d.rearrange("(st p) c -> p st c", p=128), writes=[gates_b])
        sc_ps = Rot([kb.ps(es, f"scps{i}", [128, 512], F32) for i in range(2)])
        o_ps = [kb.ps(es, f"ops{i}", [128, 512], F32) for i in range(4)]
        m_ps = Rot([kb.ps(es, f"mps{i}", [128, 512], F32) for i in range(2)])
        Ebf = Rot([kb.sb(es, f"Ebf{i}", [128, 512], BF16) for i in range(3)])
        Ef = Rot([kb.sb(es, f"Ef{i}", [128, 512], F32) for i in range(2)])
        acc, acc_b = kb.sb(es, "acc", [128, 4, 512], F32)
        small = Rot([kb.sb(es, f"sm{i}", [128, 8], F32) for i in range(8)])
        imp, imp_b = kb.sb(es, "imp", [128, 4, 32], F32)
        score, score_b = kb.sb(es, "score", [128, 4, 32], F32)
        wk, wk_b = kb.sb(es, "wk", [128, 4, 32], F32)
        m8, m8_b = kb.sb(es, "m8", [128, 4, 16], F32)
        selm, selm_b = kb.sb(es, "selm", [128, 4, 128], BF16)
        selbT, selbT_b = kb.sb(es, "selbT", [128, 512], BF16)
        ostg = Rot([kb.sb(es, f"ostg{i}", [128, 512], BF16) for i in range(2)])
        ksT, ksT_b = kb.sb(es, "ksT", [128, S], BF16)
        kwT, kwT_b = kb.sb(es, "kwT", [128, S], BF16)
        kcT_in, kcT_in_b = kb.sb(es, "kcTin", [128, S], BF16)
        vs_aug, vs_aug_b = kb.sb(es, "vsaug", [128, 16, 130], BF16)
        vw_aug, vw_aug_b = kb.sb(es, "vwaug", [128, 16, 130], BF16)
        w1, w1_b = kb.sb(es, "w1", [128, 32, 256], BF16)
        w2, w2_b = kb.sb(es, "w2", [128, 2, 128], BF16)
        pe_t, pe_b = kb.sb(es, "peT", [128, 32], F32)
        pe_raw, pe_raw_b = kb.sb(es, "peraw", [128, 128], F32)
        X, X_b = kb.sb(es, "X", [128, 32, 128], BF16)
        hid = [kb.sb(es, f"hid{i}", [128, 128], F32) for i in range(3)]
        gT = [kb.sb(es, f"gT{i}", [128, 128], BF16) for i in range(2)]
        kcT, kcT_b = kb.sb(es, "kcT", [128, 128], BF16)
        vc_aug, vc_aug_b = kb.sb(es, "vcaug", [128, 164], F32)
        Mc = [kb.sb(es, f"Mc{i}", [128, S], BF16) for i in range(4)]
        Ms = [kb.sb(es, f"Ms{i}", [128, WS], BF16) for i in range(4)]
        Mw = [kb.sb(es, f"Mw{i}", [128, WW], BF16) for i in range(4)]
        kb.op("pool", lambda e: e.memset(selm[:], 0.0), writes=[selm_b])
        kb.op("pool", lambda e: e.memset(pe_raw[:], 0.0), writes=[pe_raw_b])
        kb.op("pool", lambda e: e.memset(vs_aug[:, :, 128:130], 1.0), writes=[vs_aug_b])
        kb.op("pool", lambda e: e.memset(vw_aug[:, :, 128:130], 1.0), writes=[vw_aug_b])

        for g in range(2):
            kb.dma("sp", kcT_in[:], P.kvT_d[0, g], writes=[kcT_in_b])
            kb.dma("sp", ksT[:], P.kvT_d[2, g], writes=[ksT_b])
            kb.dma("sp", kwT[:], P.kvT_d[3, g], writes=[kwT_b])
            kb.dma("sp", vs_aug[:, :, 0:128], P.vtok_d[0, :, g * 128:(g + 1) * 128].rearrange("(kt p) d -> p kt d", p=128),
                   writes=[vs_aug_b])
            kb.dma("sp", vw_aug[:, :, 0:128], P.vtok_d[1, :, g * 128:(g + 1) * 128].rearrange("(kt p) d -> p kt d", p=128),
                   writes=[vw_aug_b])
            for j in range(4):
                h = g * 4 + j
                kb.dma("sp", Mc[j][0][:], P.Mc_d[h], writes=[Mc[j][1]])
                kb.dma("sp", Ms[j][0][:], P.Ms_d[h], writes=[Ms[j][1]])
                kb.dma("sp", Mw[j][0][:], P.Mw_d[h], writes=[Mw[j][1]])
            for kv in range(2):
                if kv == 1:
                    kb.dma("sp", kcT_in[:], P.kvT_d[1, g], writes=[kcT_in_b])
                w1n = "cmp_w1_k" if kv == 0 else "cmp_w1_v"
                w2n = "cmp_w2_k" if kv == 0 else "cmp_w2_v"
                pen = "cmp_pe_k" if kv == 0 else "cmp_pe_v"
                kb.dma("pool", w1[:], W[w1n][l].rearrange("(p d) h -> d p h", d=128), writes=[w1_b])
                kb.dma("pool", w2[:], W[w2n][l].rearrange("(c p) d -> p c d", p=128), writes=[w2_b])
                kb.dma("sp", pe_raw[0:32, :], W[pen][l], writes=[pe_raw_b])
                pps, pps_b = m_ps.next()
                kb.op("pe", lambda e, pps=pps: e.transpose(out=pps[:, 0:128], in_=pe_raw[:], identity=P.ident_f[:]),
                      reads=[pe_raw_b, P.ident_f_b], writes=[pps_b])
                kb.op("act", lambda e, pps=pps: e.copy(out=pe_t[:], in_=pps[:, 0:32]), reads=[pps_b], writes=[pe_b])
                for p in range(32):
                    src = kcT_in[:, p:p + 16 * 126 + 1:16]
                    kb.op("dve", lambda e, p=p, src=src: e.tensor_scalar(out=X[:, p, 0:127], in0=src, scalar1=pe_t[:, p:p + 1],
                                                                        scalar2=None, op0=ALU.add),
                          reads=[kcT_in_b, pe_b], writes=[X_b])
                for c in range(2):
                    ps, ps_b = m_ps.next()
                    for p in range(32):
                        kb.op("pe", lambda e, p=p, c=c, ps=ps: e.matmul(ps[:, 0:127], lhsT=w1[:, p, c * 128:(c + 1) * 128],
                                                                       rhs=X[:, p, 0:127], start=(p == 0), stop=(p == 31)),
                              reads=[w1_b, X_b], writes=[ps_b], sig=(p == 31))
                    (x_, x_b), (t_, t_b), (u_, u_b) = hid
                    kb.op("act", lambda e, ps=ps: e.copy(out=x_[:, 0:127], in_=ps[:, 0:127]), reads=[ps_b], writes=[x_b])
                    kb.op("dve", lambda e: e.tensor_tensor(out=t_[:, 0:127], in0=x_[:, 0:127], in1=x_[:, 0:127], op=ALU.mult),
                          reads=[x_b], writes=[t_b])
                    kb.op("dve", lambda e: e.tensor_scalar(out=t_[:, 0:127], in0=t_[:, 0:127], scalar1=0.044715, scalar2=1.0,
                                                           op0=ALU.mult, op1=ALU.add), reads=[t_b], writes=[t_b])
                    kb.op("dve", lambda e: e.tensor_tensor(out=t_[:, 0:127], in0=t_[:, 0:127], in1=x_[:, 0:127], op=ALU.mult),
                          reads=[t_b, x_b], writes=[t_b])
                    kb.op("act", lambda e: e.activation(out=u_[:, 0:127], in_=t_[:, 0:127], func=AF.Tanh,
                                                        scale=0.7978845608028654), reads=[t_b], writes=[u_b])
                    kb.op("dve", lambda e: e.tensor_scalar(out=u_[:, 0:127], in0=u_[:, 0:127], scalar1=1.0, scalar2=0.5,
                                                           op0=ALU.add, op1=ALU.mult), reads=[u_b], writes=[u_b])
                    kb.op("dve", lambda e, c=c: e.tensor_tensor(out=gT[c][0][:, 0:127], in0=u_[:, 0:127], in1=x_[:, 0:127],
                                                                op=ALU.mult), reads=[u_b, x_b], writes=[gT[c][1]])
                ps, ps_b = m_ps.next()
                if kv == 0:
                    for c in range(2):
                        kb.op("pe", lambda e, c=c, ps=ps: e.matmul(ps[:, 0:127], lhsT=w2[:, c, :], rhs=gT[c][0][:, 0:127],
                                                                  start=(c == 0), stop=(c == 1)),
                              reads=[w2_b, gT[c][1]], writes=[ps_b], sig=(c == 1))
                    kb.op("pool", lambda e: e.memset(kcT[:], 0.0), writes=[kcT_b])
                    kb.op("act", lambda e, ps=ps: e.copy(out=kcT[:, 0:127], in_=ps[:, 0:127]), reads=[ps_b, kcT_b], writes=[kcT_b])
                else:
                    for c in range(2):
                        kb.op("pe", lambda e, c=c, ps=ps: e.matmul(ps[0:127, 0:128], lhsT=gT[c][0][:, 0:127], rhs=w2[:, c, :],
                                                                  start=(c == 0), stop=(c == 1)),
                              reads=[w2_b, gT[c][1]], writes=[ps_b], sig=(c == 1))
                    kb.op("pool", lambda e: e.memset(vc_aug[:], 0.0), writes=[vc_aug_b])
                    kb.op("pool", lambda e: e.memset(vc_aug[:, 128:129], 1.0), reads=[vc_aug_b], writes=[vc_aug_b])
                    kb.op("act", lambda e, ps=ps: e.copy(out=vc_aug[0:127, 0:128], in_=ps[0:127, 0:128]),
                          reads=[ps_b, vc_aug_b], writes=[vc_aug_b])
                    kb.op("dve", lambda e: e.tensor_copy(out=vc_aug[:, 129:161], in_=P.cover[:]),
                          reads=[P.cover_b, vc_aug_b], writes=[vc_aug_b])
            if P.dbg_nsa is not None and g == 0:
                kb.dma("sp", P.dbg_nsa["kcT"], kcT[:], reads=[kcT_b])
                kb.dma("sp", P.dbg_nsa["vc"], vc_aug[:], reads=[vc_aug_b])

            for qt in range(4):
                t0 = qt * 512
                def cmp_scores(j):
                    h = g * 4 + j
                    ps, ps_b = sc_ps.next()
                    kb.op("pe", lambda e: e.matmul(ps[:], lhsT=kcT[:], rhs=qT[:, h, t0:t0 + 512], start=True, stop=False),
                          reads=[kcT_b, qT_b], writes=[ps_b], sig=False)
                    kb.op("pe", lambda e: e.matmul(ps[:], lhsT=P.ident_b[:], rhs=Mc[j][0][:, t0:t0 + 512], start=False, stop=True),
                          reads=[P.ident_b_b, Mc[j][1]], writes=[ps_b])
                    ef, ef_b = Ef.next()
                    kb.op("act", lambda e: e.activation(out=ef[:], in_=ps[:], func=AF.Exp, scale=SCALE),
                          reads=[ps_b], writes=[ef_b])
                    return ef, ef_b

                def cmp_pv(j, ef, ef_b):
                    h = g * 4 + j
                    for sub in range(4):
                        op_, op_b = o_ps[sub]
                        st = qt * 4 + sub
                        kb.op("pe", lambda e, op_=op_, sub=sub: e.matmul(op_[:, 0:161], lhsT=ef[:, sub * 128:(sub + 1) * 128],
                                                                        rhs=vc_aug[:, 0:161], start=True, stop=True),
                              reads=[ef_b, vc_aug_b], writes=[op_b])
                        sm, sm_b = small.next()
                        kb.op("dve", lambda e, sm=sm, op_=op_: e.tensor_scalar(out=sm[:, 0:1], in0=op_[:, 128:129], scalar1=1e-30,
                                                                              scalar2=None, op0=ALU.max), reads=[op_b], writes=[sm_b])
                        kb.op("dve", lambda e, sm=sm: e.reciprocal(out=sm[:, 1:2], in_=sm[:, 0:1]), reads=[sm_b], writes=[sm_b])
                        kb.op("dve", lambda e, sm=sm, st=st: e.tensor_tensor(out=sm[:, 2:3], in0=sm[:, 1:2],
                                                                            in1=gates[:, st, h * 3:h * 3 + 1], op=ALU.mult),
                              reads=[sm_b, gates_b], writes=[sm_b])
                        kb.op("dve", lambda e, sm=sm, op_=op_, sub=sub: e.tensor_scalar(
                            out=acc[:, sub, j * 128:(j + 1) * 128], in0=op_[:, 0:128], scalar1=sm[:, 2:3], scalar2=None, op0=ALU.mult),
                            reads=[op_b, sm_b, acc_b], writes=[acc_b])
                        if j == 0:
                            kb.op("dve", lambda e, sm=sm, op_=op_, sub=sub: e.tensor_scalar(
                                out=imp[:, sub, :], in0=op_[:, 129:161], scalar1=sm[:, 1:2], scalar2=None, op0=ALU.mult),
                                reads=[op_b, sm_b, imp_b], writes=[imp_b])
                        else:
                            kb.op("dve", lambda e, sm=sm, op_=op_, sub=sub: e.scalar_tensor_tensor(
                                out=imp[:, sub, :], in0=op_[:, 129:161], scalar=sm[:, 1:2], in1=imp[:, sub, :],
                                op0=ALU.mult, op1=ALU.add), reads=[op_b, sm_b, imp_b], writes=[imp_b])

                cpend = None
                for j in range(4):
                    ef, ef_b = cmp_scores(j)
                    if cpend is not None:
                        cmp_pv(*cpend)
                    cpend = (j, ef, ef_b)
                cmp_pv(*cpend)
                kb.op("dve", lambda e: e.tensor_tensor(out=score[:], in0=imp[:], in1=P.FB[:, qt * 4:qt * 4 + 4, :], op=ALU.add),
                      reads=[imp_b, P.FB_b], writes=[score_b])
                sp_, sp_b = m_ps.next()
                for sub in range(4):
                    kb.op("dve", lambda e, sub=sub: e.max(out=m8[:, sub, 0:8], in_=score[:, sub, :]), reads=[score_b, m8_b], writes=[m8_b])
                    kb.op("dve", lambda e, sub=sub: e.match_replace(out=wk[:, sub, :], in_to_replace=m8[:, sub, 0:8],
                                                                    in_values=score[:, sub, :], imm_value=-1e30),
                          reads=[score_b, m8_b, wk_b], writes=[wk_b])
                    kb.op("dve", lambda e, sub=sub: e.max(out=m8[:, sub, 8:16], in_=wk[:, sub, :]), reads=[wk_b, m8_b], writes=[m8_b])
                    kb.op("dve", lambda e, sub=sub: e.tensor_scalar(out=m8[:, sub, 15:16], in0=m8[:, sub, 15:16], scalar1=-1e29,
                                                                    scalar2=None, op0=ALU.max), reads=[m8_b], writes=[m8_b])
                    kb.op("dve", lambda e, sub=sub: e.tensor_scalar(out=selm[:, sub, 0:32], in0=score[:, sub, :], scalar1=m8[:, sub, 15:16],
                                                                    scalar2=1.0, op0=ALU.is_ge, op1=ALU.subtract),
                          reads=[score_b, m8_b, selm_b], writes=[selm_b])
                    kb.op("pe", lambda e, sub=sub: e.matmul(sp_[:, sub * 128:(sub + 1) * 128], lhsT=selm[:, sub, :], rhs=P.I30k[:],
                                                           start=True, stop=True),
                          reads=[selm_b, P.I30k_b], writes=[sp_b])
                kb.op("act", lambda e: e.copy(out=selbT[:], in_=sp_[:]), reads=[sp_b], writes=[selbT_b])
                if P.dbg_nsa is not None and g == 0:
                    kb.dma("sp", P.dbg_nsa["imp"][:, qt * 4:qt * 4 + 4, :], imp[:], reads=[imp_b])
                    kb.dma("sp", P.dbg_nsa["selbT"][:, t0:t0 + 512], selbT[0:32, :], reads=[selbT_b])

                def emit_scores(br, j, ki, kt):
                    h = g * 4 + j
                    dlt = t0 - kt * 128
                    ps, ps_b = sc_ps.next()
                    kT_, kT_b_ = (ksT, ksT_b) if br == 0 else (kwT, kwT_b)
                    const_bias = (br == 0 and dlt >= 256)
                    kb.op("pe", lambda e: e.matmul(ps[:], lhsT=kT_[:, kt * 128:(kt + 1) * 128], rhs=qT[:, h, t0:t0 + 512],
                                                   start=True, stop=False),
                          reads=[kT_b_, qT_b], writes=[ps_b], sig=False)
                    if br == 0:
                        kb.op("pe", lambda e: e.matmul(ps[:], lhsT=P.expall[:, kt * 128:(kt + 1) * 128], rhs=selbT[:],
                                                       start=False, stop=const_bias),
                              reads=[P.expall_b, selbT_b], writes=[ps_b], sig=const_bias)
                        if not const_bias:
                            kb.op("pe", lambda e: e.matmul(ps[:], lhsT=P.ident_b[:], rhs=Ms[j][0][:, dlt + OFFS:dlt + OFFS + 512],
                                                           start=False, stop=True),
                                  reads=[P.ident_b_b, Ms[j][1]], writes=[ps_b])
                    else:
                        kb.op("pe", lambda e: e.matmul(ps[:], lhsT=P.ident_b[:], rhs=Mw[j][0][:, dlt + OFFW:dlt + OFFW + 512],
                                                       start=False, stop=True),
                              reads=[P.ident_b_b, Mw[j][1]], writes=[ps_b])
                    eb, eb_b = Ebf.next()
                    if const_bias:
                        kb.op("act", lambda e: e.activation(out=eb[:], in_=ps[:], func=AF.Exp, scale=SCALE, bias=P.t31[:, h:h + 1]),
                              reads=[ps_b, P.t31_b], writes=[eb_b])
                    else:
                        kb.op("act", lambda e: e.activation(out=eb[:], in_=ps[:], func=AF.Exp, scale=SCALE),
                              reads=[ps_b], writes=[eb_b])
                    return eb, eb_b

                def emit_pv(br, j, ki, kt, nk, eb, eb_b):
                    h = g * 4 + j
                    v_, v_b_ = (vs_aug, vs_aug_b) if br == 0 else (vw_aug, vw_aug_b)
                    for sub in range(4):
                        op_, op_b = o_ps[sub]
                        kb.op("pe", lambda e, op_=op_, sub=sub: e.matmul(op_[:, 0:129], lhsT=eb[:, sub * 128:(sub + 1) * 128],
                                                                        rhs=v_[:, kt, 0:129], start=(ki == 0), stop=(ki == nk - 1)),
                              reads=[eb_b, v_b_], writes=[op_b], sig=(sub == 3))
                    if ki == nk - 1:
                        for sub in range(4):
                            op_, op_b = o_ps[sub]
                            st = qt * 4 + sub
                            sm, sm_b = small.next()
                            kb.op("dve", lambda e, sm=sm, op_=op_: e.reciprocal(out=sm[:, 1:2], in_=op_[:, 128:129]),
                                  reads=[op_b], writes=[sm_b])
                            kb.op("dve", lambda e, sm=sm, st=st: e.tensor_tensor(
                                out=sm[:, 2:3], in0=sm[:, 1:2], in1=gates[:, st, h * 3 + 1 + br:h * 3 + 2 + br], op=ALU.mult),
                                reads=[sm_b, gates_b], writes=[sm_b])
                            kb.op("dve", lambda e, sm=sm, op_=op_, sub=sub: e.scalar_tensor_tensor(
                                out=acc[:, sub, j * 128:(j + 1) * 128], in0=op_[:, 0:128], scalar=sm[:, 2:3],
                                in1=acc[:, sub, j * 128:(j + 1) * 128], op0=ALU.mult, op1=ALU.add),
                                reads=[op_b, sm_b, acc_b], writes=[acc_b])

                tiles = []
                for br in range(2):
                    for j in range(4):
                        if br == 0:
                            kts = list(range(0, (t0 + 511) // 128 + 1))
                        else:
                            kts = list(range(max(0, t0 // 128 - 4), t0 // 128 + 4))
                        for ki, kt in enumerate(kts):
                            tiles.append((br, j, ki, kt, len(kts)))
                pend = None
                for (br, j, ki, kt, nk) in tiles:
                    eb, eb_b = emit_scores(br, j, ki, kt)
                    if pend is not None:
                        emit_pv(*pend)
                    pend = (br, j, ki, kt, nk, eb, eb_b)
                emit_pv(*pend)
                for j in range(4):
                    h = g * 4 + j
                    tp, tp_b = m_ps.next()
                    for sub in range(4):
                        kb.op("pe", lambda e, tp=tp, sub=sub, j=j: e.transpose(out=tp[:, sub * 128:(sub + 1) * 128],
                                                                             in_=acc[:, sub, j * 128:(j + 1) * 128],
                                                                             identity=P.ident_f[:]),
                              reads=[acc_b, P.ident_f_b], writes=[tp_b], sig=(sub == 3))
                    og, og_b = ostg.next()
                    kb.op("act", lambda e, tp=tp, og=og: e.copy(out=og[:], in_=tp[:]), reads=[tp_b], writes=[og_b])
                    kb.dma("sp", P.oaT_d[h, :, t0:t0 + 512], og[:], reads=[og_b])
        kb.barrier()


def gdn_setup(P, kb, es):
    P.U_f, P.U_b = kb.sb(es, "U_f", [128, 128], F32)
    P.ones_f, P.ones_b = kb.sb(es, "ones_f", [128, 128], F32)
    P.mpos, P.mpos_b = kb.sb(es, "mpos", [128, 128], F32)
    P.mneg, P.mneg_b = kb.sb(es, "mneg", [128, 128], F32)
    kb.op("pool", lambda e: e.memset(P.ones_f[:], 1.0), writes=[P.ones_b])
    kb.op("pool", lambda e: e.memset(P.U_f[:], 1.0), writes=[P.U_b])
    kb.op("pool", lambda e: e.affine_select(out=P.U_f[:], in_=P.U_f[:], pattern=[[1, 128]], compare_op=ALU.is_ge, fill=0.0,
                                            base=0, channel_multiplier=-1), reads=[P.U_b], writes=[P.U_b])
    kb.op("pool", lambda e: e.memset(P.mpos[:], 0.0), writes=[P.mpos_b])
    kb.op("pool", lambda e: e.affine_select(out=P.mpos[:], in_=P.mpos[:], pattern=[[-1, 128]], compare_op=ALU.is_gt, fill=1e4,
                                            base=0, channel_multiplier=1), reads=[P.mpos_b], writes=[P.mpos_b])
    kb.op("pool", lambda e: e.memset(P.mneg[:], 0.0), writes=[P.mneg_b])
    kb.op("pool", lambda e: e.affine_select(out=P.mneg[:], in_=P.mneg[:], pattern=[[1, 128]], compare_op=ALU.is_ge, fill=-1e4,
                                            base=0, channel_multiplier=-1), reads=[P.mneg_b], writes=[P.mneg_b])


GDN_K = 4
GDN_TEST_HEADS = 1


def gdn_phase(P, kb, l):
    nc = P.nc
    W = P.W
    NCH = S // 128
    with ExitStack() as es:
        ab, ab_b = kb.sb(es, "ab", [128, NCH, 16], F32)
        kb.dma("sp", ab[:], P.ab_d.rearrange("(n p) c -> p n c", p=128), writes=[ab_b])
        dtb, dtb_b = kb.sb(es, "dtb", [128, 8], F32)
        nA, nA_b = kb.sb(es, "nA", [128, 8], F32)
        nw, nw_b = kb.sb(es, "nw", [128, 128], F32)
        mhalf, mhalf_b = kb.sb(es, "mhalf", [128, 1], F32)
        kb.op("pool", lambda e: e.memset(mhalf[:], -0.5), writes=[mhalf_b])
        kb.dma("sp", dtb[:], bass.AP(tensor=W["dt_bias"].tensor, offset=W["dt_bias"][l].offset, ap=[[0, 128], [1, 8]]), writes=[dtb_b])
        kb.dma("sp", nA[:], bass.AP(tensor=W["a_log"].tensor, offset=W["a_log"][l].offset, ap=[[0, 128], [1, 8]]), writes=[nA_b])
        kb.dma("sp", nw[:], bass.AP(tensor=W["gdn_norm_w"].tensor, offset=W["gdn_norm_w"][l].offset, ap=[[0, 128], [1, 128]]),
               writes=[nw_b])
        kb.op("act", lambda e: e.activation(out=nA[:], in_=nA[:], func=AF.Exp), reads=[nA_b], writes=[nA_b])
        kb.op("dve", lambda e: e.tensor_scalar(out=nA[:], in0=nA[:], scalar1=-1.0, scalar2=None, op0=ALU.mult), reads=[nA_b], writes=[nA_b])
        graw, graw_b = kb.sb(es, "graw", [128, NCH, 8], F32)
        beta, beta_b = kb.sb(es, "beta", [128, NCH, 8], F32)
        nbeta, nbeta_b = kb.sb(es, "nbeta", [128, NCH, 8], F32)
        gcol, gcol_b = kb.sb(es, "gcol", [128, NCH, 8], F32)
        eg, eg_b = kb.sb(es, "eg", [128, NCH, 8], F32)
        bg, bg_b = kb.sb(es, "bg", [128, NCH, 8], F32)
        for n in range(NCH):
            kb.op("dve", lambda e, n=n: e.tensor_tensor(out=graw[:, n, :], in0=ab[:, n, 0:8], in1=dtb[:], op=ALU.add),
                  reads=[ab_b, dtb_b, graw_b], writes=[graw_b])
        kb.op("act", lambda e: e.activation(out=graw[:], in_=graw[:], func=AF.Exp), reads=[graw_b], writes=[graw_b])
        kb.op("act", lambda e: e.activation(out=graw[:], in_=graw[:], func=AF.Ln, bias=P.one_t[:, 0:1]), reads=[graw_b, P.one_b], writes=[graw_b])
        for n in range(NCH):
            kb.op("dve", lambda e, n=n: e.tensor_tensor(out=graw[:, n, :], in0=graw[:, n, :], in1=nA[:], op=ALU.mult),
                  reads=[graw_b, nA_b], writes=[graw_b])
        kb.op("act", lambda e: e.activation(out=beta[:], in_=ab[:, :, 8:16], func=AF.Sigmoid), reads=[ab_b], writes=[beta_b])
        kb.op("dve", lambda e: e.tensor_scalar(out=nbeta[:], in0=beta[:], scalar1=-1.0, scalar2=None, op0=ALU.mult),
              reads=[beta_b], writes=[nbeta_b])
        bank = [kb.ps(es, f"gbank{i}", [128, 4, 128], F32) for i in range(6)]
        Q = Rot([(bank[i][0][:, 0, :], bank[i][1]) for i in range(6)])
        big = Rot([kb.ps(es, f"gbig{i}", [128, 512], F32) for i in range(2)])
        for n in range(NCH):
            pq, pq_b = Q.next()
            kb.op("pe", lambda e, n=n, pq=pq: e.matmul(pq[:, 0:8], lhsT=P.U_f[:], rhs=graw[:, n, :], start=True, stop=True),
                  reads=[P.U_b, graw_b], writes=[pq_b])
            kb.op("act", lambda e, n=n, pq=pq: e.copy(out=gcol[:, n, :], in_=pq[:, 0:8]), reads=[pq_b, gcol_b], writes=[gcol_b])
        kb.op("act", lambda e: e.activation(out=eg[:], in_=gcol[:], func=AF.Exp), reads=[gcol_b], writes=[eg_b])
        kb.op("dve", lambda e: e.tensor_tensor(out=bg[:], in0=eg[:], in1=beta[:], op=ALU.mult), reads=[eg_b, beta_b], writes=[bg_b])
        cw, cw_b = kb.sb(es, "cw", [128, 24, 4], F32)
        with ExitStack() as es_c:
            cwraw, cwraw_b = kb.sb(es_c, "cwraw", [128, 3072], F32)
            kb.op("pool", lambda e: e.memset(cwraw[:], 0.0), writes=[cwraw_b])
            kb.dma("sp", cwraw[0:4, :], W["conv_w"][l], reads=[cwraw_b], writes=[cwraw_b])
            for c in range(24):
                pq, pq_b = Q.next()
                kb.op("pe", lambda e, pq=pq, c=c: e.transpose(out=pq, in_=cwraw[:, c * 128:(c + 1) * 128], identity=P.ident_f[:]),
                      reads=[cwraw_b, P.ident_f_b], writes=[pq_b])
                kb.op("act", lambda e, pq=pq, c=c: e.copy(out=cw[:, c, :], in_=pq[:, 0:4]), reads=[pq_b, cw_b], writes=[cw_b])
            kb.barrier()
        xr1 = kb.sb(es, "xr0", [128, S], F32)
        xr = [xr1, xr1, xr1]
        sqt, sqt_b = kb.sb(es, "sqt", [128, S], F32)
        rn, rn_b = kb.sb(es, "rn", [128, 512], F32)

        def slot_bufs(k):
            B = {}
            B["xc"] = [kb.sb(es, f"xc{k}_{i}", [128, S], F32) for i in range(3)]
            B["tok"] = [kb.sb(es, f"tok{k}_{i}", [128, 128], F32) for i in range(3)]
            for nm, shp in [("vbk", [128, 256]), ("kd", [128, 128]), ("gd", [128, 128]), ("e1", [128, 128]),
                            ("e2", [128, 128]), ("qkt", [128, 128]), ("u", [128, 128]), ("nwk", [128, 128]), ("zt", [128, 128]),
                            ("xm", [128, 128]), ("yy", [128, 128]), ("junk", [128, 128])]:
                B[nm] = kb.sb(es, f"{nm}{k}", shp, F32)
            for nm in ["Pm", "PTm", "Rm", "St", "ztok"]:
                B[nm] = Rot([kb.sb(es, f"{nm}{k}_{i}", [128, 128], F32) for i in range(2)])
            B["sm"] = Rot([kb.sb(es, f"gsm{k}_{i}", [128, 8], F32) for i in range(4)])
            B["og"] = Rot([kb.sb(es, f"gost{k}_{i}", [128, 512], BF16) for i in range(2)])
            return B

        def head_gen(h, B):
            xc = B["xc"]
            tok = B["tok"]
            vbk, vbk_b = B["vbk"]; kd, kd_b = B["kd"]; gd, gd_b = B["gd"]
            e1, e1_b = B["e1"]; e2, e2_b = B["e2"]; qkt, qkt_b = B["qkt"]; u_, u_b = B["u"]; nwk, nwk_b = B["nwk"]
            zt, zt_b = B["zt"]; xm, xm_b = B["xm"]; yy, yy_b = B["yy"]; junk, junk_b = B["junk"]
            Pm, PTm, Rm, St, ztok, sm, ogr = B["Pm"], B["PTm"], B["Rm"], B["St"], B["ztok"], B["sm"], B["og"]
            for i in range(3):
                c = i * 8 + h
                kb.dma("sp", xr[i][0][:], P.gqkvT_d[c], writes=[xr[i][1]])
                x_, x_b = xr[i]
                y_, y_b = xc[i]
                kb.op("dve", lambda e, x_=x_, y_=y_, c=c: e.tensor_scalar(out=y_[:], in0=x_[:], scalar1=cw[:, c, 3:4], scalar2=None,
                                                                         op0=ALU.mult), reads=[x_b, cw_b], writes=[y_b])
                for sft in range(1, 4):
                    kb.op("dve", lambda e, x_=x_, y_=y_, c=c, sft=sft: e.scalar_tensor_tensor(
                        out=y_[:, sft:S], in0=x_[:, 0:S - sft], scalar=cw[:, c, 3 - sft:4 - sft], in1=y_[:, sft:S],
                        op0=ALU.mult, op1=ALU.add), reads=[x_b, cw_b, y_b], writes=[y_b])
                kb.op("act", lambda e, y_=y_: e.activation(out=y_[:], in_=y_[:], func=AF.Silu), reads=[y_b], writes=[y_b])
                if i < 2:
                    kb.op("act", lambda e, y_=y_: e.activation(out=sqt[:], in_=y_[:], func=AF.Square), reads=[y_b], writes=[sqt_b])
                    for t0 in range(0, S, 512):
                        bp, bp_b = big.next()
                        kb.op("pe", lambda e, bp=bp, t0=t0: e.matmul(bp[:], lhsT=P.ones_f[:], rhs=sqt[:, t0:t0 + 512], start=True, stop=True),
                              reads=[P.ones_b, sqt_b], writes=[bp_b])
                        kb.op("act", lambda e, bp=bp: e.activation(out=rn[:], in_=bp[:], func=AF.Sqrt, bias=P.eps_t[:, 0:1]),
                              reads=[bp_b, P.eps_b], writes=[rn_b])
                        kb.op("dve", lambda e: e.reciprocal(out=rn[:], in_=rn[:]), reads=[rn_b], writes=[rn_b])
                        if i == 0:
                            kb.op("dve", lambda e, y_=y_, t0=t0: e.scalar_tensor_tensor(
                                out=y_[:, t0:t0 + 512], in0=rn[:], scalar=SCALE, in1=y_[:, t0:t0 + 512], op0=ALU.mult, op1=ALU.mult),
                                reads=[rn_b, y_b], writes=[y_b])
                        else:
                            kb.op("dve", lambda e, y_=y_, t0=t0: e.tensor_tensor(out=y_[:, t0:t0 + 512], in0=rn[:], in1=y_[:, t0:t0 + 512],
                                                                               op=ALU.mult), reads=[rn_b, y_b], writes=[y_b])
            yield
            qTn, qTn_b = xc[0]
            kTn, kTn_b = xc[1]
            S_, S_b = St.next()
            kb.op("pool", lambda e, S_=S_: e.memset(S_[:], 0.0), writes=[S_b])
            og, og_b = None, None
            for n in range(NCH):
                cs = slice(n * 128, (n + 1) * 128)
                (q_tok, q_tok_b), (k_tok, k_tok_b), (v_tok, v_tok_b) = tok
                for i in range(3):
                    pq, pq_b = Q.next()
                    kb.op("pe", lambda e, pq=pq, i=i: e.transpose(out=pq, in_=xc[i][0][:, cs], identity=P.ident_f[:]),
                          reads=[xc[i][1], P.ident_f_b], writes=[pq_b])
                    if i == 0:
                        kb.op("act", lambda e, pq=pq: e.activation(out=tok[0][0][:], in_=pq, func=AF.Copy, scale=eg[:, n, h:h + 1]),
                              reads=[pq_b, eg_b], writes=[tok[0][1]])
                    if i == 1:
                        kb.op("act", lambda e, pq=pq: e.copy(out=tok[1][0][:], in_=pq), reads=[pq_b], writes=[tok[1][1]])
                    if i == 1:
                        kb.op("act", lambda e, pq=pq: e.activation(out=vbk[:, 128:256], in_=pq, func=AF.Copy, scale=bg[:, n, h:h + 1]),
                              reads=[pq_b, bg_b, vbk_b], writes=[vbk_b])
                    if i == 2:
                        kb.op("act", lambda e, pq=pq: e.activation(out=vbk[:, 0:128], in_=pq, func=AF.Copy, scale=beta[:, n, h:h + 1]),
                              reads=[pq_b, beta_b, vbk_b], writes=[vbk_b])
                yield
                kb.op("act", lambda e: e.activation(out=gd[:], in_=P.ident_f[:], func=AF.Copy, scale=gcol[:, n, h:h + 1]),
                      reads=[P.ident_f_b, gcol_b], writes=[gd_b])
                pg, pg_b = Q.next()
                kb.op("pe", lambda e, pg=pg: e.matmul(pg, lhsT=P.ones_f[:], rhs=gd[:], start=True, stop=True),
                      reads=[P.ones_b, gd_b], writes=[pg_b])
                kb.op("dve", lambda e, pg=pg: e.scalar_tensor_tensor(out=e1[:], in0=pg, scalar=gcol[:, n, h:h + 1], in1=P.mpos[:],
                                                                    op0=ALU.subtract, op1=ALU.max),
                      reads=[pg_b, gcol_b, P.mpos_b], writes=[e1_b])
                kb.op("dve", lambda e, pg=pg: e.scalar_tensor_tensor(out=e2[:], in0=pg, scalar=gcol[:, n, h:h + 1], in1=P.mneg[:],
                                                                    op0=ALU.subtract, op1=ALU.min),
                      reads=[pg_b, gcol_b, P.mneg_b], writes=[e2_b])
                s1, s1_b = sm.next()
                kb.op("dve", lambda e, pg=pg, s1=s1: e.tensor_copy(out=s1[:, 0:1], in_=pg[:, 127:128]), reads=[pg_b], writes=[s1_b])
                kb.op("act", lambda e: e.activation(out=e1[:], in_=e1[:], func=AF.Exp, scale=-1.0), reads=[e1_b], writes=[e1_b])
                kb.op("act", lambda e: e.activation(out=e2[:], in_=e2[:], func=AF.Exp), reads=[e2_b], writes=[e2_b])
                kb.op("act", lambda e, s1=s1: e.activation(out=s1[:, 1:2], in_=gcol[:, n, h:h + 1], func=AF.Exp, scale=-1.0, bias=s1[:, 0:1]),
                      reads=[s1_b, gcol_b], writes=[s1_b])
                kb.op("act", lambda e, s1=s1: e.activation(out=s1[:, 2:3], in_=s1[:, 0:1], func=AF.Exp), reads=[s1_b], writes=[s1_b])
                kb.op("act", lambda e, s1=s1: e.activation(out=kd[:], in_=k_tok[:], func=AF.Copy, scale=s1[:, 1:2]),
                      reads=[k_tok_b, s1_b], writes=[kd_b])
                yield
                pk, pk_b = Q.next()
                kb.op("pe", lambda e, pk=pk: e.matmul(pk, lhsT=kTn[:, cs], rhs=kTn[:, cs], start=True, stop=True),
                      reads=[kTn_b], writes=[pk_b])
                P0, P0_b = Pm.next()
                kb.op("dve", lambda e, pk=pk, P0=P0: e.scalar_tensor_tensor(out=P0[:], in0=pk, scalar=nbeta[:, n, h:h + 1], in1=e1[:],
                                                                           op0=ALU.mult, op1=ALU.mult),
                      reads=[pk_b, nbeta_b, e1_b], writes=[P0_b])
                pk2, pk2_b = Q.next()
                kb.op("pe", lambda e, pk2=pk2: e.matmul(pk2, lhsT=kTn[:, cs], rhs=qTn[:, cs], start=True, stop=True),
                      reads=[kTn_b, qTn_b], writes=[pk2_b])
                kb.op("dve", lambda e, pk2=pk2: e.tensor_tensor(out=qkt[:], in0=pk2, in1=e2[:], op=ALU.mult),
                      reads=[pk2_b, e2_b], writes=[qkt_b])
                pt_, pt_b = Q.next()
                kb.op("pe", lambda e, pt_=pt_, P0=P0: e.transpose(out=pt_, in_=P0[:], identity=P.ident_f[:]),
                      reads=[P0_b, P.ident_f_b], writes=[pt_b])
                PT0, PT0_b = PTm.next()
                R0, R0_b = Rm.next()
                kb.op("dve", lambda e, pt_=pt_, PT0=PT0: e.tensor_copy(out=PT0[:], in_=pt_), reads=[pt_b], writes=[PT0_b])
                kb.op("dve", lambda e, pt_=pt_, R0=R0: e.tensor_tensor(out=R0[:], in0=pt_, in1=P.ident_f[:], op=ALU.add),
                      reads=[pt_b, P.ident_f_b], writes=[R0_b])
                yield
                Pc, Pc_b, PTc, PTc_b, Rc, Rc_b = P0, P0_b, PT0, PT0_b, R0, R0_b
                for lvl in range(1, 7):
                    pa_, pa_b = Q.next()
                    kb.op("pe", lambda e, pa_=pa_, Pc=Pc, PTc=PTc: e.matmul(pa_, lhsT=PTc[:], rhs=Pc[:], start=True, stop=True),
                          reads=[Pc_b, PTc_b], writes=[pa_b])
                    Pn, Pn_b = Pm.next()
                    kb.op("act", lambda e, pa_=pa_, Pn=Pn: e.copy(out=Pn[:], in_=pa_), reads=[pa_b], writes=[Pn_b])
                    if lvl < 6:
                        pb_, pb_b = Q.next()
                        kb.op("pe", lambda e, pb_=pb_, Pc=Pc, PTc=PTc: e.matmul(pb_, lhsT=Pc[:], rhs=PTc[:], start=True, stop=True),
                              reads=[Pc_b, PTc_b], writes=[pb_b])
                        PTn, PTn_b = PTm.next()
                        kb.op("dve", lambda e, pb_=pb_, PTn=PTn: e.tensor_copy(out=PTn[:], in_=pb_), reads=[pb_b], writes=[PTn_b])
                    yield
                    pr_, pr_b = Q.next()
                    kb.op("pe", lambda e, pr_=pr_, Pn=Pn, Rc=Rc: e.matmul(pr_, lhsT=Pn[:], rhs=Rc[:], start=True, stop=True),
                          reads=[Pn_b, Rc_b], writes=[pr_b])
                    Rn, Rn_b = Rm.next()
                    kb.op("dve", lambda e, pr_=pr_, Rn=Rn, Rc=Rc: e.tensor_tensor(out=Rn[:], in0=pr_, in1=Rc[:], op=ALU.add),
                          reads=[pr_b, Rc_b], writes=[Rn_b])
                    Rc, Rc_b = Rn, Rn_b
                    if lvl < 6:
                        Pc, Pc_b, PTc, PTc_b = Pn, Pn_b, PTn, PTn_b
                    yield
                TT, TT_b = Rc, Rc_b
                bp, bp_b = big.next()
                kb.op("pe", lambda e, bp=bp, TT=TT: e.matmul(bp[:, 0:256], lhsT=TT[:], rhs=vbk[:], start=True, stop=True),
                      reads=[TT_b, vbk_b], writes=[bp_b])
                kb.op("act", lambda e, bp=bp: e.copy(out=u_[:], in_=bp[:, 0:128]), reads=[bp_b], writes=[u_b])
                kb.op("act", lambda e, bp=bp: e.activation(out=nwk[:], in_=bp[:, 128:256], func=AF.Copy, scale=-1.0),
                      reads=[bp_b], writes=[nwk_b])
                yield
                pz, pz_b = Q.next()
                kb.op("pe", lambda e, pz=pz: e.matmul(pz, lhsT=q_tok[:], rhs=P.ident_f[:], start=True, stop=False),
                      reads=[q_tok_b, P.ident_f_b], writes=[pz_b])
                kb.op("pe", lambda e, pz=pz: e.matmul(pz, lhsT=nwk[:], rhs=qkt[:], start=False, stop=True),
                      reads=[nwk_b, qkt_b], writes=[pz_b])
                kb.op("act", lambda e, pz=pz: e.copy(out=zt[:], in_=pz), reads=[pz_b], writes=[zt_b])
                px, px_b = Q.next()
                kb.op("pe", lambda e, px=px: e.matmul(px, lhsT=nwk[:], rhs=kd[:], start=True, stop=True),
                      reads=[nwk_b, kd_b], writes=[px_b])
                kb.op("dve", lambda e, px=px, s1=s1: e.scalar_tensor_tensor(out=xm[:], in0=P.ident_f[:], scalar=s1[:, 2:3], in1=px,
                                                                           op0=ALU.mult, op1=ALU.add),
                      reads=[px_b, s1_b, P.ident_f_b], writes=[xm_b])
                yield
                po, po_b = Q.next()
                kb.op("pe", lambda e, po=po: e.matmul(po, lhsT=qkt[:], rhs=u_[:], start=True, stop=False),
                      reads=[qkt_b, u_b], writes=[po_b])
                kb.op("pe", lambda e, po=po, S_=S_: e.matmul(po, lhsT=zt[:], rhs=S_[:], start=False, stop=True),
                      reads=[zt_b, S_b], writes=[po_b])
                pS, pS_b = Q.next()
                kb.op("pe", lambda e, pS=pS: e.matmul(pS, lhsT=kd[:], rhs=u_[:], start=True, stop=False),
                      reads=[kd_b, u_b], writes=[pS_b])
                kb.op("pe", lambda e, pS=pS, S_=S_: e.matmul(pS, lhsT=xm[:], rhs=S_[:], start=False, stop=True),
                      reads=[xm_b, S_b], writes=[pS_b])
                Sn, Sn_b = St.next()
                kb.op("act", lambda e, pS=pS, Sn=Sn: e.copy(out=Sn[:], in_=pS), reads=[pS_b], writes=[Sn_b])
                S_, S_b = Sn, Sn_b
                zk, zk_b = ztok.next()
                kb.dma("act", zk[:], P.z_d[n * 128:(n + 1) * 128, h * 128:(h + 1) * 128], writes=[zk_b])
                s2, s2_b = sm.next()
                kb.op("act", lambda e, po=po, s2=s2: e.activation(out=junk[:], in_=po, func=AF.Square, accum_out=s2[:, 0:1]),
                      reads=[po_b], writes=[junk_b, s2_b])
                kb.op("act", lambda e, s2=s2: e.activation(out=s2[:, 1:2], in_=s2[:, 0:1], func=AF.Sqrt, scale=1.0 / 128, bias=P.eps_t[:, 0:1]),
                      reads=[s2_b, P.eps_b], writes=[s2_b])
                kb.op("dve", lambda e, s2=s2: e.reciprocal(out=s2[:, 2:3], in_=s2[:, 1:2]), reads=[s2_b], writes=[s2_b])
                kb.op("dve", lambda e, po=po, s2=s2: e.scalar_tensor_tensor(out=yy[:], in0=po, scalar=s2[:, 2:3], in1=zk[:],
                                                                           op0=ALU.mult, op1=ALU.mult),
                      reads=[po_b, s2_b, zk_b], writes=[yy_b])
                yield
                pT, pT_b = Q.next()
                kb.op("pe", lambda e, pT=pT: e.transpose(out=pT, in_=yy[:], identity=P.ident_f[:]),
                      reads=[yy_b, P.ident_f_b], writes=[pT_b])
                if n % 4 == 0:
                    og, og_b = ogr.next()
                kb.op("act", lambda e, pT=pT, og=og, n=n: e.copy(out=og[:, (n % 4) * 128:(n % 4 + 1) * 128], in_=pT),
                      reads=[pT_b, og_b], writes=[og_b])
                if n % 4 == 3:
                    kb.dma("sp", P.obT_d[h, :, (n - 3) * 128:(n + 1) * 128], og[:], reads=[og_b])
                yield

        slots = [slot_bufs(k) for k in range(GDN_K)]
        heads = list(range(P.gdn_heads))
        for r0 in range(0, len(heads), GDN_K):
            gens = [head_gen(h, slots[k]) for k, h in enumerate(heads[r0:r0 + GDN_K])]
            while gens:
                for g_ in list(gens):
                    try:
                        next(g_)
                    except StopIteration:
                        gens.remove(g_)
        kb.barrier()


def merge_phase(P, kb, l):
    W = P.W
    with ExitStack() as es:
        oaT, oaT_b = kb.sb(es, "oaT", [128, 8, S], BF16)
        obT, obT_b = kb.sb(es, "obT", [128, 8, S], BF16)
        for c in range(8):
            kb.dma("sp", oaT[:, c, :], P.oaT_d[c], writes=[oaT_b])
            kb.dma("sp", obT[:, c, :], P.obT_d[c], writes=[obT_b])
        pa_p = Rot([kb.sb(es, f"pap{i}", [128, 8, 512], BF16) for i in range(2)])
        pb_p = Rot([kb.sb(es, f"pbp{i}", [128, 8, 512], BF16) for i in range(2)])
        psA = Rot([kb.ps(es, f"psA{i}", [128, 512], F32) for i in range(3)])
        psB = Rot([kb.ps(es, f"psB{i}", [128, 512], F32) for i in range(3)])
        gA = Rot([kb.sb(es, f"gA{i}", [128, 512], F32) for i in range(2)])
        gB = Rot([kb.sb(es, f"gB{i}", [128, 512], F32) for i in range(2)])
        t1 = Rot([kb.sb(es, f"t1{i}", [128, 512], F32) for i in range(2)])
        t2 = Rot([kb.sb(es, f"t2{i}", [128, 512], F32) for i in range(2)])
        uo = Rot([kb.sb(es, f"uo{i}", [128, 512], BF16) for i in range(3)])
        Wa = W["w_pa"][l].rearrange("(kc p) n -> p kc n", p=128)
        Wb = W["w_pb"][l].rearrange("(kc p) n -> p kc n", p=128)
        for p0 in range(0, D, 512):
            wa, wa_b = pa_p.next()
            wb, wb_b = pb_p.next()
            kb.dma("pool", wa[:], Wa[:, :, p0:p0 + 512], writes=[wa_b])
            kb.dma("pool", wb[:], Wb[:, :, p0:p0 + 512], writes=[wb_b])
            for cb in range(0, 512, 128):
                fb = (p0 + cb) // 128
                for t0 in range(0, S, 512):
                    pA, pA_b = psA.next()
                    pB, pB_b = psB.next()
                    for kc in range(8):
                        kb.op("pe", lambda e, kc=kc, pA=pA: e.matmul(pA[:], lhsT=wa[:, kc, cb:cb + 128], rhs=oaT[:, kc, t0:t0 + 512],
                                                                    start=(kc == 0), stop=(kc == 7)),
                              reads=[wa_b, oaT_b], writes=[pA_b], sig=(kc == 7))
                    for kc in range(8):
                        kb.op("pe", lambda e, kc=kc, pB=pB: e.matmul(pB[:], lhsT=wb[:, kc, cb:cb + 128], rhs=obT[:, kc, t0:t0 + 512],
                                                                    start=(kc == 0), stop=(kc == 7)),
                              reads=[wb_b, obT_b], writes=[pB_b], sig=(kc == 7))
                    ga, ga_b = gA.next()
                    gb_, gb_b = gB.next()
                    kb.dma("act", ga[:], P.mergeT_d[fb, :, t0:t0 + 512], writes=[ga_b])
                    kb.dma("act", gb_[:], P.mergeT_d[16 + fb, :, t0:t0 + 512], writes=[gb_b])
                    a1, a1_b = t1.next()
                    a2, a2_b = t2.next()
                    kb.op("dve", lambda e, pA=pA, ga=ga, a1=a1: e.tensor_tensor(out=a1[:], in0=pA[:], in1=ga[:], op=ALU.mult),
                          reads=[pA_b, ga_b], writes=[a1_b])
                    kb.op("dve", lambda e, pB=pB, gb_=gb_, a2=a2: e.tensor_tensor(out=a2[:], in0=pB[:], in1=gb_[:], op=ALU.mult),
                          reads=[pB_b, gb_b], writes=[a2_b])
                    uu, uu_b = uo.next()
                    kb.op("dve", lambda e, a1=a1, a2=a2, uu=uu: e.tensor_tensor(out=uu[:], in0=a1[:], in1=a2[:], op=ALU.add),
                          reads=[a1_b, a2_b], writes=[uu_b])
                    kb.dma("sp", P.uT_d[fb, :, t0:t0 + 512], uu[:], reads=[uu_b])
        kb.barrier()


def load_T(P, kb, dst, dst_b, src_d, nchunks, t0, ntok):
    for c in range(nchunks):
        kb.dma("sp", dst[:, c, 0:ntok], src_d[c, :, t0:t0 + ntok], writes=[dst_b])


def residual_epi(P, kb, es, src_ap, dst_ap, tok_off):
    rt = Rot([kb.sb(es, f"rt{i}", [128, 512], F32) for i in range(3)])
    ro = Rot([kb.sb(es, f"ro{i}", [128, 512], F32) for i in range(3)])

    def epi(ps, ps_b, j0, pw, t0, tw):
        r_, r_b = rt.next()
        o_, o_b = ro.next()
        r0 = tok_off + t0
        kb.dma("act", r_[:, 0:pw], src_ap[r0:r0 + 128, j0:j0 + pw], writes=[r_b])
        kb.op("dve", lambda e: e.tensor_tensor(out=o_[:, 0:pw], in0=ps[:, 0:pw], in1=r_[:, 0:pw], op=ALU.add),
              reads=[ps_b, r_b], writes=[o_b])
        kb.dma("sp", dst_ap[r0:r0 + 128, j0:j0 + pw], o_[:, 0:pw], reads=[o_b])
    return epi


def wo_phase(P, kb, l, x_src, x_dst):
    with ExitStack() as es:
        uT, uT_b = kb.sb(es, "uT", [128, 16, S], BF16)
        load_T(P, kb, uT, uT_b, P.uT_d, 16, 0, S)
        epi = residual_epi(P, kb, es, x_src, x_dst, 0)
        P.dense(kb, uT, uT_b, 16, S, P.W["w_o"][l], [(0, D, "a", epi)])


def mlp_phase(P, kb, l, x1, x2):
    W = P.W
    TT_ = 1024
    NQ = 4
    HQ = DFF // NQ
    with ExitStack() as es:
        panels = Rot([kb.sb(es, f"mwp{i}", [128, 16, 512], BF16) for i in range(3)])
        pss = Rot([kb.ps(es, f"mlps{i}", [128, 512], F32) for i in range(4)])
        actT, actT_b = kb.sb(es, "actT", [128, 16, TT_], BF16)
        h2T, h2T_b = kb.sb(es, "h2T", [128, 16, TT_], BF16)
        nres = P.norm_res(kb, es, W["ln2_w"][l], npts=3)
        rl = Rot([kb.sb(es, f"rl{i}", [128, 512], F32) for i in range(3)])
        rt = Rot([kb.sb(es, f"rt{i}", [128, 512], F32) for i in range(3)])
        ro = Rot([kb.sb(es, f"ro{i}", [128, 512], F32) for i in range(3)])
        xb = {}

        def epi_up(ps, ps_b, j0, cw, t0, tw):
            r_, r_b = rl.next()
            kb.op("act", lambda e: e.activation(out=r_[0:cw, 0:tw], in_=ps[0:cw, 0:tw], func=AF.Relu),
                  reads=[ps_b], writes=[r_b])
            kb.op("dve", lambda e: e.tensor_tensor(out=actT[0:cw, j0 // 128, t0:t0 + tw], in0=r_[0:cw, 0:tw],
                                                   in1=r_[0:cw, 0:tw], op=ALU.mult),
                  reads=[r_b, actT_b], writes=[actT_b])

        for tt in range(S // TT_):
            tok0 = tt * TT_
            P.norm_T(kb, x1, W["ln2_w"][l], h2T, h2T_b, TT_, tok0=tok0, res=nres)
            for qd in range(NQ):
                src = x1 if qd == 0 else x2

                def epi_dn(ps, ps_b, j0, pw, t0, tw, src=src):
                    r_, r_b = rt.next()
                    o_, o_b = ro.next()
                    r0 = tok0 + t0
                    key = (r0, j0)
                    if key not in xb:
                        xb[key] = Buf(f"x2_{r0}_{j0}")
                    kb.dma("act", r_[:, 0:pw], src[r0:r0 + 128, j0:j0 + pw], reads=[xb[key]], writes=[r_b])
                    kb.op("dve", lambda e: e.tensor_tensor(out=o_[:, 0:pw], in0=ps[:, 0:pw], in1=r_[:, 0:pw], op=ALU.add),
                          reads=[ps_b, r_b], writes=[o_b])
                    kb.dma("sp", x2[r0:r0 + 128, j0:j0 + pw], o_[:, 0:pw], reads=[o_b], writes=[xb[key]])

                P.dense_core(kb, h2T, h2T_b, 16, TT_, W["w_up"][l][:, qd * HQ:(qd + 1) * HQ], [(0, HQ, "b", epi_up)], panels, pss)
                P.dense_core(kb, actT, actT_b, 16, TT_, W["w_down"][l][qd * HQ:(qd + 1) * HQ, :], [(0, D, "a", epi_dn)], panels, pss)
        kb.barrier()


def final_norm(P, kb, x_src, lnw_ap, out_ap):
    with ExitStack() as es:
        lnw, lnw_b = kb.sb(es, "flnw", [128, D], F32)
        kb.dma("sp", lnw[:], bass.AP(tensor=lnw_ap.tensor, offset=lnw_ap.offset, ap=[[0, 128], [1, D]]), writes=[lnw_b])
        xts = Rot([kb.sb(es, f"fxt{i}", [128, D], F32) for i in range(2)])
        ots = Rot([kb.sb(es, f"fot{i}", [128, D], F32) for i in range(2)])
        junk, junk_b = kb.sb(es, "fjunk", [128, D], BF16)
        sts = Rot([kb.sb(es, f"fst{i}", [128, 4], F32) for i in range(2)])
        for tt in range(S // 128):
            xt, xt_b = xts.next()
            ot, ot_b = ots.next()
            st, st_b = sts.next()
            kb.dma("sp", xt[:], x_src[tt * 128:(tt + 1) * 128, :], writes=[xt_b])
            kb.op("act", lambda e: e.activation(out=junk[:], in_=xt[:], func=AF.Square, accum_out=st[:, 0:1]),
                  reads=[xt_b], writes=[junk_b, st_b])
            kb.op("act", lambda e: e.activation(out=st[:, 1:2], in_=st[:, 0:1], func=AF.Sqrt, scale=1.0 / D, bias=P.eps_t[:, 0:1]),
                  reads=[st_b, P.eps_b], writes=[st_b])
            kb.op("dve", lambda e: e.reciprocal(out=st[:, 2:3], in_=st[:, 1:2]), reads=[st_b], writes=[st_b])
            kb.op("dve", lambda e: e.scalar_tensor_tensor(out=ot[:], in0=xt[:], scalar=st[:, 2:3], in1=lnw[:], op0=ALU.mult, op1=ALU.mult),
                  reads=[xt_b, st_b, lnw_b], writes=[ot_b])
            kb.dma("sp", out_ap[tt * 128:(tt + 1) * 128, :], ot[:], reads=[ot_b])
        kb.barrier()
```

```python
import math
import numpy as np
from contextlib import ExitStack
import concourse.bass as bass
import concourse.mybir as mybir
from concourse.bass_utils import run_bass_kernel_spmd

F32 = mybir.dt.float32
BF16 = mybir.dt.bfloat16
I32 = mybir.dt.int32
ALU = mybir.AluOpType
AF = mybir.ActivationFunctionType
AX = mybir.AxisListType

NCORES = 8
BPC = 2
S = 2048
D = 2048
DEPTH = 2
DFF = 8192
IN_COLS = 10792
EPS = 1e-6
NEG = -30000.0

C_Q = 0
C_KV = 1024
C_GATE = 2560
C_GQKV = 2584
C_Z = 5656
C_A = 6680
C_B = 6688
C_MERGE = 6696


class Buf:
    __slots__ = ("name", "w", "r")

    def __init__(self, name):
        self.name = name
        self.w = None
        self.r = {}


class KB:
    SEM_LIMIT = 30000

    def __init__(self, nc, es, n_dma_slots=8):
        self.nc = nc
        self.es = es
        self.eng = {"pe": nc.tensor, "act": nc.scalar, "dve": nc.vector, "pool": nc.gpsimd, "sp": nc.sync}
        self.sems = {}
        self.cur = {}
        self.epoch = {}
        for e in self.eng:
            self.epoch[e] = 0
            self._new_sem(e)
        self.seen = {e: {} for e in self.eng}
        self.dma_slots = {}
        for q in ("sp", "act", "pool"):
            sl = []
            for i in range(n_dma_slots):
                key = f"d_{q}_{i}"
                self.sems[key] = es.enter_context(nc.semaphore(key))
                sl.append([key, 0])
            self.dma_slots[q] = [sl, 0]
        self.n_ins = 0
        self.n_wait = 0
        self.uid = 0
        self.last_tok = {}
        self.know = {}

    def _new_sem(self, e):
        key = f"s_{e}_{self.epoch[e]}"
        self.sems[key] = self.es.enter_context(self.nc.semaphore(key))
        self.cur[e] = [key, 0]
        self.epoch[e] += 1

    def _need(self, e, toks):
        seen = self.seen[e]
        best = {}
        for t in toks:
            if t is None:
                continue
            k, v = t
            if seen.get(k, 0) >= v:
                continue
            if best.get(k, 0) < v:
                best[k] = v
        for k, v in best.items():
            if seen.get(k, 0) >= v:
                continue
            self.eng[e].wait_ge(self.sems[k], v)
            seen[k] = v
            self.n_wait += 1
            kn = self.know.get((k, v))
            if kn:
                for k2, v2 in kn.items():
                    if seen.get(k2, 0) < v2:
                        seen[k2] = v2

    def _deps(self, reads, writes, skip_waw_key=None):
        toks = []
        for b in reads:
            toks.append(b.w)
        for b in writes:
            if b.w is not None and not (skip_waw_key is not None and b.w[0] == skip_waw_key):
                toks.append(b.w)
            for k, v in b.r.items():
                toks.append((k, v))
        return toks

    def _record(self, tok, reads, writes):
        k, v = tok
        for b in reads:
            if b.r.get(k, 0) < v:
                b.r[k] = v
        for b in writes:
            b.w = tok
            b.r = {}

    def op(self, e, fn, reads=(), writes=(), sig=True):
        cur = self.cur[e]
        if cur[1] >= self.SEM_LIMIT:
            self._new_sem(e)
            cur = self.cur[e]
        skip = cur[0] if e == "pe" else None
        self._need(e, self._deps(reads, writes, skip_waw_key=skip))
        ins = fn(self.eng[e])
        self.n_ins += 1
        tok = (cur[0], cur[1] + 1)
        if sig:
            ins.then_inc(self.sems[cur[0]], 1)
            cur[1] += 1
            self.last_tok[e] = tok
            self.know[tok] = dict(self.seen[e])
        self._record(tok, reads, writes)
        return ins

    def dma(self, q, out, in_, reads=(), writes=(), **kw):
        sl, idx = self.dma_slots[q]
        slot = sl[idx % len(sl)]
        self.dma_slots[q][1] = idx + 1
        key, uses = slot
        toks = self._deps(reads, writes)
        if uses > 0:
            toks.append((key, 16 * uses))
        self._need(q, toks)
        ins = self.eng[q].dma_start(out=out, in_=in_, **kw)
        ins.then_inc(self.sems[key], 16)
        slot[1] = uses + 1
        self.n_ins += 1
        tok = (key, 16 * (uses + 1))
        self.know[tok] = dict(self.seen[q])
        self._record(tok, reads, writes)
        return ins

    def barrier(self):
        toks = []
        for e in self.eng:
            if e in self.last_tok:
                toks.append(self.last_tok[e])
        for q in self.dma_slots:
            for key, uses in self.dma_slots[q][0]:
                if uses > 0:
                    toks.append((key, 16 * uses))
        for e in self.eng:
            self._need(e, toks)

    def sb(self, es, name, shape, dtype):
        self.uid += 1
        t = es.enter_context(self.nc.sbuf_tensor(f"{name}_{self.uid}", list(shape), dtype))
        return t, Buf(name)

    def ps(self, es, name, shape, dtype=F32):
        self.uid += 1
        t = es.enter_context(self.nc.psum_tensor(f"{name}_{self.uid}", list(shape), dtype))
        return t, Buf(name)


class Rot:
    def __init__(self, items):
        self.items = items
        self.i = 0

    def next(self):
        it = self.items[self.i % len(self.items)]
        self.i += 1
        return it


class Prog:
    def __init__(self, nc, dbg=()):
        self.nc = nc
        self.dbg = set(dbg)
        self.ext_out = {}

    def dram(self, name, shape, dtype):
        kind = "ExternalOutput" if name in self.dbg else "Internal"
        if ("in:" + name) in self.dbg:
            kind = "ExternalInput"
        t = self.nc.dram_tensor(name, list(shape), dtype, kind=kind)
        if name in self.dbg:
            self.ext_out[name] = t
        return t.ap()

    def setup(self, kb, es):
        nc = self.nc
        self.ident_f, self.ident_f_b = kb.sb(es, "identf", [128, 128], F32)
        self.ident_b, self.ident_b_b = kb.sb(es, "identb", [128, 128], BF16)
        kb.op("pool", lambda e: e.memset(self.ident_f[:], 0.0), writes=[self.ident_f_b])
        kb.op("pool", lambda e: e.affine_select(out=self.ident_f[:], in_=self.ident_f[:], pattern=[[-1, 128]],
                                                compare_op=ALU.not_equal, fill=1.0, base=0, channel_multiplier=1),
              reads=[self.ident_f_b], writes=[self.ident_f_b])
        kb.op("dve", lambda e: e.tensor_copy(out=self.ident_b[:], in_=self.ident_f[:]),
              reads=[self.ident_f_b], writes=[self.ident_b_b])
        self.eps_t, self.eps_b = kb.sb(es, "eps", [128, 1], F32)
        kb.op("pool", lambda e: e.memset(self.eps_t[:], EPS), writes=[self.eps_b])
        self.one_t, self.one_b = kb.sb(es, "onec", [128, 1], F32)
        kb.op("pool", lambda e: e.memset(self.one_t[:], 1.0), writes=[self.one_b])

    def norm_res(self, kb, es, lnw_ap, npts=4):
        R = {}
        R["lnw"] = kb.sb(es, "lnw", [128, D], F32)
        kb.dma("sp", R["lnw"][0][:], bass.AP(tensor=lnw_ap.tensor, offset=lnw_ap.offset, ap=[[0, 128], [1, D]]),
               writes=[R["lnw"][1]])
        R["xts"] = Rot([kb.sb(es, f"xt{i}", [128, D], F32) for i in range(2)])
        R["hbs"] = Rot([kb.sb(es, f"hb{i}", [128, D], BF16) for i in range(2)])
        R["junk"] = kb.sb(es, "junk", [128, D], BF16)
        R["sts"] = Rot([kb.sb(es, f"st{i}", [128, 4], F32) for i in range(2)])
        R["pts"] = Rot([kb.ps(es, f"pt{i}", [128, 4, 128], BF16) for i in range(npts)])
        return R

    def norm_T(self, kb, x_ap, lnw_ap, hT, hT_b, ntok, tok0=0, res=None):
        nc = self.nc
        with ExitStack() as es:
            R = res if res is not None else self.norm_res(kb, es, lnw_ap)
            lnw, lnw_b = R["lnw"]
            xts, hbs, sts, pts = R["xts"], R["hbs"], R["sts"], R["pts"]
            junk, junk_b = R["junk"]
            for tt in range(ntok // 128):
                xt, xt_b = xts.next()
                hb, hb_b = hbs.next()
                st, st_b = sts.next()
                r0 = tok0 + tt * 128
                kb.dma("sp", xt[:], x_ap[r0:r0 + 128, :], writes=[xt_b])
                kb.op("act", lambda e: e.activation(out=junk[:], in_=xt[:], func=AF.Square, accum_out=st[:, 0:1]),
                      reads=[xt_b], writes=[junk_b, st_b])
                kb.op("act", lambda e: e.activation(out=st[:, 1:2], in_=st[:, 0:1], func=AF.Sqrt, scale=1.0 / D, bias=self.eps_t[:, 0:1]),
                      reads=[st_b, self.eps_b], writes=[st_b])
                kb.op("dve", lambda e: e.reciprocal(out=st[:, 2:3], in_=st[:, 1:2]), reads=[st_b], writes=[st_b])
                kb.op("dve", lambda e: e.scalar_tensor_tensor(out=hb[:], in0=xt[:], scalar=st[:, 2:3], in1=lnw[:],
                                                              op0=ALU.mult, op1=ALU.mult),
                      reads=[xt_b, st_b, lnw_b], writes=[hb_b])
                for g in range(4):
                    pt, pt_b = pts.next()
                    for j in range(4):
                        c = g * 4 + j
                        kb.op("pe", lambda e, c=c, j=j: e.transpose(out=pt[:, j, :], in_=hb[:, c * 128:(c + 1) * 128],
                                                                     identity=self.ident_b[:]),
                              reads=[hb_b, self.ident_b_b], writes=[pt_b], sig=(j == 3))
                    eng = "act" if g % 2 == 0 else "dve"
                    dst = hT[:, g * 4:(g + 1) * 4, tt * 128:(tt + 1) * 128]
                    if eng == "act":
                        kb.op("act", lambda e: e.copy(out=dst, in_=pt[:]), reads=[pt_b], writes=[hT_b])
                    else:
                        kb.op("dve", lambda e: e.tensor_copy(out=dst, in_=pt[:]), reads=[pt_b], writes=[hT_b])
            if res is None:
                kb.barrier()

    def dense(self, kb, inT, inT_b, KC, ntok, W_ap, jobs, wq="pool"):
        PW = 512
        with ExitStack() as es:
            panels = Rot([kb.sb(es, f"wp{i}", [128, KC, PW], BF16) for i in range(2)])
            pss = Rot([kb.ps(es, f"dps{i}", [128, 512], F32) for i in range(4)])
            self.dense_core(kb, inT, inT_b, KC, ntok, W_ap, jobs, panels, pss, wq)
            kb.barrier()

    def dense_core(self, kb, inT, inT_b, KC, ntok, W_ap, jobs, panels, pss, wq="pool"):
        PW = 512
        if True:
            Wv = W_ap.rearrange("(kc p) n -> p kc n", p=128)
            for (c0, ncols, form, epi) in jobs:
                for p0 in range(0, ncols, PW):
                    pw = min(PW, ncols - p0)
                    wp, wp_b = panels.next()
                    kb.dma(wq, wp[:, 0:KC, 0:pw], Wv[:, :, c0 + p0:c0 + p0 + pw], writes=[wp_b])
                    if form == "b":
                        for cb in range(0, pw, 128):
                            cw = min(128, pw - cb)
                            for t0 in range(0, ntok, 512):
                                tw = min(512, ntok - t0)
                                ps, ps_b = pss.next()
                                for kc in range(KC):
                                    kb.op("pe", lambda e, kc=kc: e.matmul(ps[0:cw, 0:tw], lhsT=wp[:, kc, cb:cb + cw],
                                                                         rhs=inT[:, kc, t0:t0 + tw],
                                                                         start=(kc == 0), stop=(kc == KC - 1)),
                                          reads=[wp_b, inT_b], writes=[ps_b], sig=(kc == KC - 1))
                                epi(ps, ps_b, p0 + cb, cw, t0, tw)
                    else:
                        for t0 in range(0, ntok, 128):
                            ps, ps_b = pss.next()
                            for kc in range(KC):
                                kb.op("pe", lambda e, kc=kc: e.matmul(ps[:, 0:pw], lhsT=inT[:, kc, t0:t0 + 128],
                                                                     rhs=wp[:, kc, 0:pw],
                                                                     start=(kc == 0), stop=(kc == KC - 1)),
                                      reads=[wp_b, inT_b], writes=[ps_b], sig=(kc == KC - 1))
                            epi(ps, ps_b, p0, pw, t0, 128)

    def make_stage(self, kb, es, n=4):
        self.stg_f = Rot([kb.sb(es, f"stgf{i}", [128, 512], F32) for i in range(n)])
        self.stg_h = Rot([kb.sb(es, f"stgh{i}", [128, 512], BF16) for i in range(n)])
        self.evac_i = 0

    def evac(self, kb, ps, ps_b, rows, cols, dst_ap, dtype=F32, func=None, eng=None):
        st, st_b = (self.stg_f if dtype == F32 else self.stg_h).next()
        if eng is None:
            eng = "act" if (func is not None or self.evac_i % 2 == 0) else "dve"
        self.evac_i += 1
        if eng == "act":
            f = func if func is not None else AF.Copy
            kb.op("act", lambda e: e.activation(out=st[0:rows, 0:cols], in_=ps[0:rows, 0:cols], func=f),
                  reads=[ps_b], writes=[st_b])
        else:
            kb.op("dve", lambda e: e.tensor_copy(out=st[0:rows, 0:cols], in_=ps[0:rows, 0:cols]),
                  reads=[ps_b], writes=[st_b])
        kb.dma("sp", dst_ap, st[0:rows, 0:cols], reads=[st_b])

    def alloc_proj_scratch(self):
        self.qT_d = self.dram("qT_d", [8, 128, S], BF16)
        self.kvT_d = self.dram("kvT_d", [4, 2, 128, S], BF16)
        self.vtok_d = self.dram("vtok_d", [2, S, 256], BF16)
        self.gate_d = self.dram("gate_d", [S, 24], F32)
        self.gqkvT_d = self.dram("gqkvT_d", [24, 128, S], F32)
        self.z_d = self.dram("z_d", [S, 1024], F32)
        self.ab_d = self.dram("ab_d", [S, 16], F32)
        self.mergeT_d = self.dram("mergeT_d", [32, 128, S], F32)

    def proj_phase(self, kb, hT, hT_b, w_in_l, nw_ap):
        with ExitStack() as es:
            self.make_stage(kb, es)

            def epi_q(ps, ps_b, j0, cw, t0, tw):
                self.evac(kb, ps, ps_b, cw, tw, self.qT_d[j0 // 128, :, t0:t0 + tw], dtype=BF16)

            def epi_kvT(kind):
                def f(ps, ps_b, j0, cw, t0, tw):
                    self.evac(kb, ps, ps_b, cw, tw, self.kvT_d[kind, j0 // 128, :, t0:t0 + tw], dtype=BF16)
                return f

            def epi_vtok(kind):
                def f(ps, ps_b, j0, pw, t0, tw):
                    self.evac(kb, ps, ps_b, 128, pw, self.vtok_d[kind, t0:t0 + 128, j0:j0 + pw], dtype=BF16)
                return f

            def epi_gate(ps, ps_b, j0, pw, t0, tw):
                self.evac(kb, ps, ps_b, 128, pw, self.gate_d[t0:t0 + 128, :], func=AF.Sigmoid)

            def epi_gqkv(ps, ps_b, j0, cw, t0, tw):
                self.evac(kb, ps, ps_b, cw, tw, self.gqkvT_d[j0 // 128, :, t0:t0 + tw])

            nwrep, nwrep_b = kb.sb(es, "nwrep", [128, 512], F32)
            for r_ in range(4):
                kb.dma("sp", nwrep[:, r_ * 128:(r_ + 1) * 128],
                       bass.AP(tensor=nw_ap.tensor, offset=nw_ap.offset, ap=[[0, 128], [1, 128]]), writes=[nwrep_b])

            def epi_z(ps, ps_b, j0, pw, t0, tw):
                st, st_b = self.stg_f.next()
                kb.op("act", lambda e: e.activation(out=st[:, 0:pw], in_=ps[:, 0:pw], func=AF.Silu), reads=[ps_b], writes=[st_b])
                kb.op("dve", lambda e: e.tensor_tensor(out=st[:, 0:pw], in0=st[:, 0:pw], in1=nwrep[:, 0:pw], op=ALU.mult),
                      reads=[st_b, nwrep_b], writes=[st_b])
                kb.dma("sp", self.z_d[t0:t0 + 128, j0:j0 + pw], st[:, 0:pw], reads=[st_b])

            def epi_ab(ps, ps_b, j0, pw, t0, tw):
                self.evac(kb, ps, ps_b, 128, pw, self.ab_d[t0:t0 + 128, :])

            def epi_merge(ps, ps_b, j0, cw, t0, tw):
                self.evac(kb, ps, ps_b, cw, tw, self.mergeT_d[j0 // 128, :, t0:t0 + tw], func=AF.Sigmoid)

            jobs = [
                (C_Q, 1024, "b", epi_q),
                (C_KV + 0, 256, "b", epi_kvT(0)),
                (C_KV + 256, 256, "b", epi_kvT(1)),
                (C_KV + 512, 256, "b", epi_kvT(2)),
                (C_KV + 768, 256, "a", epi_vtok(0)),
                (C_KV + 1024, 256, "b", epi_kvT(3)),
                (C_KV + 1280, 256, "a", epi_vtok(1)),
                (C_GATE, 24, "a", epi_gate),
                (C_GQKV, 3072, "b", epi_gqkv),
                (C_Z, 1024, "a", epi_z),
                (C_A, 16, "a", epi_ab),
                (C_MERGE, 4096, "b", epi_merge),
            ]
            self.dense(kb, hT, hT_b, 16, S, w_in_l, jobs)


WNAMES = [("rel_table", [32, 8]), ("ln1_w", [DEPTH, D]), ("w_in", [DEPTH, D, IN_COLS]),
          ("cmp_pe_k", [DEPTH, 32, 128]), ("cmp_pe_v", [DEPTH, 32, 128]),
          ("cmp_w1_k", [DEPTH, 4096, 256]), ("cmp_w2_k", [DEPTH, 256, 128]),
          ("cmp_w1_v", [DEPTH, 4096, 256]), ("cmp_w2_v", [DEPTH, 256, 128]),
          ("conv_w", [DEPTH, 4, 3072]), ("a_log", [DEPTH, 8]), ("dt_bias", [DEPTH, 8]),
          ("gdn_norm_w", [DEPTH, 128]), ("w_pa", [DEPTH, 1024, D]), ("w_pb", [DEPTH, 1024, D]),
          ("w_o", [DEPTH, D, D]), ("ln2_w", [DEPTH, D]), ("w_up", [DEPTH, D, DFF]),
          ("w_down", [DEPTH, DFF, D]), ("ln_f_w", [D])]


class LazyW:
    def __init__(self, nc, depth):
        self.nc = nc
        self.depth = depth
        self.aps = {}
        self.shapes = dict(WNAMES)

    def __getitem__(self, n):
        if n not in self.aps:
            shp = list(self.shapes[n])
            if len(shp) > 1 and shp[0] == DEPTH and n != "rel_table":
                shp[0] = self.depth
            self.aps[n] = self.nc.dram_tensor(n, shp, F32, kind="ExternalInput").ap()
        return self.aps[n]


def build(dbg=(), stop=None, nseq=BPC, depth=DEPTH):
    nc = bass.Bass("TRN2", target_bir_lowering=False)
    P = Prog(nc, dbg)
    x = nc.dram_tensor("x", [BPC, S, D], F32, kind="ExternalInput").ap()
    W = LazyW(nc, depth)
    P.W = W
    out = nc.dram_tensor("out", [BPC, S, D], F32, kind="ExternalOutput").ap()
    P.alloc_proj_scratch()
    P.oaT_d = P.dram("oaT_d", [8, 128, S], BF16)
    P.obT_d = P.dram("obT_d", [8, 128, S], BF16)
    P.uT_d = P.dram("uT_d", [16, 128, S], BF16)
    X1 = P.dram("X1_d", [BPC, S, D], F32)
    X2 = P.dram("X2_d", [BPC, S, D], F32)
    P.dbg_nsa = None
    P.dbg_gdn = None
    P.dbg_gdn_n = 0
    P.gdn_heads = 8
    if "gdn_dbg" in P.dbg:
        P.dbg_gdn = {nm: nc.dram_tensor("dbg_" + nm, [128, 128], F32, kind="ExternalOutput").ap()
                     for nm in ["q_tok", "k_tok", "v_tok", "e1", "e2", "TT", "u", "negw", "zt", "xm", "qkt", "yy", "kd", "Sn", "gcol"]}
        P.gdn_heads = GDN_TEST_HEADS
    if "nsa_dbg" in P.dbg:
        P.dbg_nsa = {"kcT": nc.dram_tensor("dbg_kcT", [128, 128], BF16, kind="ExternalOutput").ap(),
                     "vc": nc.dram_tensor("dbg_vc", [128, 164], F32, kind="ExternalOutput").ap(),
                     "imp": nc.dram_tensor("dbg_imp", [128, 16, 32], F32, kind="ExternalOutput").ap(),
                     "selbT": nc.dram_tensor("dbg_selbT", [32, S], BF16, kind="ExternalOutput").ap()}
    with ExitStack() as es:
        kb = KB(nc, es)
        P.setup(kb, es)
        nsa_setup(P, kb, es, W["rel_table"])
        gdn_setup(P, kb, es)
        kb.barrier()
        if stop == "nsa_only":
            nsa_phase(P, kb, 0)
        if stop == "gdn_only":
            gdn_phase(P, kb, 0)
        if stop == "post_only":
            merge_phase(P, kb, 0)
            wo_phase(P, kb, 0, x[0], X1[0])
            mlp_phase(P, kb, 0, X1[0], X2[0])
            final_norm(P, kb, X2[0], W["ln_f_w"], out[0])
        for l in range(depth if stop not in ("setup", "nsa_only", "gdn_only", "post_only") else 0):
            for s in range(nseq):
                x_in = x[s] if l == 0 else X2[s]
                with ExitStack() as es_h:
                    hT, hT_b = kb.sb(es_h, "hT", [128, 16, S], BF16)
                    P.norm_T(kb, x_in, W["ln1_w"][l], hT, hT_b, S)
                    if stop == "norm":
                        hT_d = P.dram("hT_d", [128, 16, S], BF16)
                        kb.dma("sp", hT_d, hT[:], reads=[hT_b])
                        break
                    P.proj_phase(kb, hT, hT_b, W["w_in"][l], W["gdn_norm_w"][l])
                    kb.barrier()
                if stop == "proj":
                    break
                nsa_phase(P, kb, l)
                if stop == "nsa":
                    break
                gdn_phase(P, kb, l)
                merge_phase(P, kb, l)
                wo_phase(P, kb, l, x_in, X1[s])
                mlp_phase(P, kb, l, X1[s], X2[s])
            if stop is not None:
                break
        if stop is None:
            for s in range(nseq):
                final_norm(P, kb, X2[s], W["ln_f_w"], out[s])
        kb.barrier()
        print("instructions", kb.n_ins, "waits", kb.n_wait)
    return nc, P


_CACHE = {}


def kernel(**inputs):
    if "nc" not in _CACHE:
        _CACHE["nc"] = build()
    nc, P = _CACHE["nc"]
    x = np.ascontiguousarray(inputs["x"], dtype=np.float32)
    wts = {n: np.ascontiguousarray(inputs[n], dtype=np.float32) for n in P.W.aps}
    in_maps = []
    for c in range(NCORES):
        m = dict(wts)
        m["x"] = np.ascontiguousarray(x[c * BPC:(c + 1) * BPC])
        in_maps.append(m)
    res = run_bass_kernel_spmd(nc, in_maps, core_ids=list(range(NCORES)))
    return np.concatenate([np.asarray(r["out"], dtype=np.float32) for r in res.results], axis=0)


SQ = math.sqrt(128.0)
SCALE = 1.0 / SQ
OFFW, WW = 384, 1408
OFFS, WS = 384, 1024
LGW = 127 + WW
LGS = 127 + WS
LGC = 4080
LG = 4096


def _bucket_ranges():
    d = np.arange(0, 4200)
    nf = np.maximum(d, 1).astype(np.float32)
    large = 16 + (np.log(nf / np.float32(16)) / np.float32(math.log(128 / 16)) * np.float32(16)).astype(np.int32)
    large = np.minimum(large, 31)
    bk = np.where(d < 16, d, large)
    out = []
    for b in range(32):
        idx = np.nonzero(bk == b)[0]
        out.append((b, int(idx[0]), int(idx[-1]) + 1))
    return out


def nsa_setup(P, kb, es, rel_table):
    nc = P.nc
    P.Mw_d = P.dram("Mw_d", [8, 128, WW], BF16)
    P.Ms_d = P.dram("Ms_d", [8, 128, WS], BF16)
    P.Mc_d = P.dram("Mc_d", [8, 128, S], BF16)
    G_d = P.dram("G_d", [3, 8, LG], BF16)
    P.t31, P.t31_b = kb.sb(es, "t31", [128, 8], F32)
    kb.dma("sp", P.t31[:], bass.AP(tensor=rel_table.tensor, offset=rel_table.offset + 31 * 8, ap=[[0, 128], [1, 8]]),
           writes=[P.t31_b])
    P.Jb, P.Jb_b = kb.sb(es, "Jb", [128, 128], BF16)
    P.I30k, P.I30k_b = kb.sb(es, "I30k", [128, 128], BF16)
    P.expall, P.expall_b = kb.sb(es, "expall", [128, S], BF16)
    P.FB, P.FB_b = kb.sb(es, "FB", [128, 16, 32], F32)
    P.cover, P.cover_b = kb.sb(es, "cover", [128, 32], F32)
    with ExitStack() as es2:
        tmpf, tmpf_b = kb.sb(es2, "tmpf", [128, 128], F32)
        kb.op("pool", lambda e: e.memset(tmpf[:], 0.0), writes=[tmpf_b])
        kb.op("pool", lambda e: e.affine_select(out=tmpf[:], in_=tmpf[:], pattern=[[1, 128]], compare_op=ALU.not_equal,
                                                fill=1.0, base=-127, channel_multiplier=1), reads=[tmpf_b], writes=[tmpf_b])
        kb.op("dve", lambda e: e.tensor_copy(out=P.Jb[:], in_=tmpf[:]), reads=[tmpf_b], writes=[P.Jb_b])
        kb.op("dve", lambda e: e.tensor_scalar(out=P.I30k[:], in0=P.ident_f[:], scalar1=30000.0, scalar2=None, op0=ALU.mult),
              reads=[P.ident_f_b], writes=[P.I30k_b])
        ex, ex_b = kb.sb(es2, "ex", [32, S], F32)
        kb.op("pool", lambda e: e.memset(ex[:], 1.0), writes=[ex_b])
        kb.op("pool", lambda e: e.affine_select(out=ex[:], in_=ex[:], pattern=[[1, S]], compare_op=ALU.is_ge, fill=0.0,
                                                base=0, channel_multiplier=-64), reads=[ex_b], writes=[ex_b])
        kb.op("pool", lambda e: e.affine_select(out=ex[:], in_=ex[:], pattern=[[-1, S]], compare_op=ALU.is_ge, fill=0.0,
                                                base=63, channel_multiplier=64), reads=[ex_b], writes=[ex_b])
        kb.op("pool", lambda e: e.memset(P.expall[:], 0.0), writes=[P.expall_b])
        kb.op("dve", lambda e: e.tensor_copy(out=P.expall[0:32, :], in_=ex[:]), reads=[ex_b, P.expall_b], writes=[P.expall_b])
        kb.op("pool", lambda e: e.memset(P.cover[:], 1.0), writes=[P.cover_b])
        kb.op("pool", lambda e: e.affine_select(out=P.cover[:], in_=P.cover[:], pattern=[[64, 32]], compare_op=ALU.is_ge,
                                                fill=0.0, base=63, channel_multiplier=-16), reads=[P.cover_b], writes=[P.cover_b])
        kb.op("pool", lambda e: e.affine_select(out=P.cover[:], in_=P.cover[:], pattern=[[-64, 32]], compare_op=ALU.is_ge,
                                                fill=0.0, base=31, channel_multiplier=16), reads=[P.cover_b], writes=[P.cover_b])
        kb.op("pool", lambda e: e.memset(P.FB[:], 0.0), writes=[P.FB_b])
        for st in range(16):
            for half in range(2):
                c = 2 * st + half
                ps_ = slice(64 * half, 64 * half + 64)
                if c + 1 < 32:
                    kb.op("pool", lambda e, ps_=ps_, c=c, st=st: e.memset(P.FB[ps_, st, c + 1:32], -1e30),
                          reads=[P.FB_b], writes=[P.FB_b])
                for m in sorted(set([0, c, max(c - 1, 0)])):
                    kb.op("pool", lambda e, ps_=ps_, m=m, st=st: e.memset(P.FB[ps_, st, m:m + 1], 1000.0),
                          reads=[P.FB_b], writes=[P.FB_b])
        tabT, tabT_b = kb.sb(es2, "tabT", [8, 32], F32)
        with nc.allow_non_contiguous_dma(reason="tiny table transpose"):
            kb.dma("sp", tabT[:], rel_table.rearrange("b h -> h b"), writes=[tabT_b])
        zer, zer_b = kb.sb(es2, "zer", [8, LG], F32)
        kb.op("pool", lambda e: e.memset(zer[:], 0.0), writes=[zer_b])
        rngs = _bucket_ranges()
        for kind, (off, dmax, L) in enumerate([(127 + OFFW, 512, LGW), (127 + OFFS, None, LGS), (2063, None, LGC)]):
            G, G_b = kb.sb(es2, f"G{kind}", [8, LG], F32)
            Gh, Gh_b = kb.sb(es2, f"Gh{kind}", [8, LG], BF16)
            kb.op("pool", lambda e, G=G: e.memset(G[:], NEG), writes=[G_b])
            for (b, lo, hi) in rngs:
                if b == 31:
                    hi = 10 ** 6
                if dmax is not None:
                    hi = min(hi, dmax)
                a0 = lo + off
                a1 = min(hi + off, L)
                if a1 <= a0:
                    continue
                kb.op("dve", lambda e, G=G, a0=a0, a1=a1, b=b: e.tensor_scalar(
                    out=G[:, a0:a1], in0=zer[:, a0:a1], scalar1=tabT[:, b:b + 1], scalar2=SQ, op0=ALU.add, op1=ALU.mult),
                    reads=[zer_b, tabT_b, G_b], writes=[G_b])
            kb.op("dve", lambda e, G=G, Gh=Gh: e.tensor_copy(out=Gh[:], in_=G[:]), reads=[G_b], writes=[Gh_b])
            kb.dma("sp", G_d[kind], Gh[:], reads=[Gh_b])
        kb.barrier()
        mps = Rot([kb.ps(es2, f"mps{i}", [128, 512], F32) for i in range(2)])
        mrev = Rot([kb.sb(es2, f"mrev{i}", [128, S], BF16) for i in range(2)])
        mout = Rot([kb.sb(es2, f"mout{i}", [128, S], BF16) for i in range(2)])
        for kind, (W_, pstep, dst) in enumerate([(WW, 1, P.Mw_d), (WS, 1, P.Ms_d), (S, 16, P.Mc_d)]):
            for h in range(8):
                mr, mr_b = mrev.next()
                mo, mo_b = mout.next()
                g_ap = G_d[kind, h]
                kb.dma("sp", mr[:, 0:W_], bass.AP(tensor=g_ap.tensor, offset=g_ap.offset, ap=[[pstep, 128], [1, W_]]),
                       writes=[mr_b])
                for c0 in range(0, W_, 512):
                    cw = min(512, W_ - c0)
                    ps, ps_b = mps.next()
                    kb.op("pe", lambda e, ps=ps, mr=mr, c0=c0, cw=cw: e.matmul(ps[:, 0:cw], lhsT=P.Jb[:], rhs=mr[:, c0:c0 + cw],
                                                                              start=True, stop=True),
                          reads=[P.Jb_b, mr_b], writes=[ps_b])
                    kb.op("act", lambda e, ps=ps, mo=mo, c0=c0, cw=cw: e.copy(out=mo[:, c0:c0 + cw], in_=ps[:, 0:cw]),
                          reads=[ps_b], writes=[mo_b])
                kb.dma("sp", dst[h], mo[:, 0:W_], reads=[mo_b])
        kb.barrier()


def nsa_phase(P, kb, l):
    nc = P.nc
    W = P.W
    with ExitStack() as es:
        qT, qT_b = kb.sb(es, "qT", [128, 8, S], BF16)
        for h in range(8):
            kb.dma("sp", qT[:, h, :], P.qT_d[h], writes=[qT_b])
        gates, gates_b = kb.sb(es, "gates", [128, 16, 24], F32)
        kb.dma("sp", gates[:], P.gate_d.rearrange("(st p) c -> p st c", p=128), writes=[gates_b])
        sc_ps = Rot([kb.ps(es, f"scps{i}", [128, 512], F32) for i in range(2)])
        o_ps = [kb.ps(es, f"ops{i}", [128, 512], F32) for i in range(4)]
        m_ps = Rot([kb.ps(es, f"mps{i}", [128, 512], F32) for i in range(2)])
        Ebf = Rot([kb.sb(es, f"Ebf{i}", [128, 512], BF16) for i in range(3)])
        Ef = Rot([kb.sb(es, f"Ef{i}", [128, 512], F32) for i in range(2)])
        acc, acc_b = kb.sb(es, "acc", [128, 4, 512], F32)
        small = Rot([kb.sb(es, f"sm{i}", [128, 8], F32) for i in range(8)])
        imp, imp_b = kb.sb(es, "imp", [128, 4, 32], F32)
        score, score_b = kb.sb(es, "score", [128, 4, 32], F32)
        wk, wk_b = kb.sb(es, "wk", [128, 4, 32], F32)
        m8, m8_b = kb.sb(es, "m8", [128, 4, 16], F32)
        selm, selm_b = kb.sb(es, "selm", [128, 4, 128], BF16)
        selbT, selbT_b = kb.sb(es, "selbT", [128, 512], BF16)
        ostg = Rot([kb.sb(es, f"ostg{i}", [128, 512], BF16) for i in range(2)])
        ksT, ksT_b = kb.sb(es, "ksT", [128, S], BF16)
        kwT, kwT_b = kb.sb(es, "kwT", [128, S], BF16)
        kcT_in, kcT_in_b = kb.sb(es, "kcTin", [128, S], BF16)
        vs_aug, vs_aug_b = kb.sb(es, "vsaug", [128, 16, 130], BF16)
        vw_aug, vw_aug_b = kb.sb(es, "vwaug", [128, 16, 130], BF16)
        w1, w1_b = kb.sb(es, "w1", [128, 32, 256], BF16)
        w2, w2_b = kb.sb(es, "w2", [128, 2, 128], BF16)
        pe_t, pe_b = kb.sb(es, "peT", [128, 32], F32)
        pe_raw, pe_raw_b = kb.sb(es, "peraw", [128, 128], F32)
        X, X_b = kb.sb(es, "X", [128, 32, 128], BF16)
        hid = [kb.sb(es, f"hid{i}", [128, 128], F32) for i in range(3)]
        gT = [kb.sb(es, f"gT{i}", [128, 128], BF16) for i in range(2)]
        kcT, kcT_b = kb.sb(es, "kcT", [128, 128], BF16)
        vc_aug, vc_aug_b = kb.sb(es, "vcaug", [128, 164], F32)
        Mc = [kb.sb(es, f"Mc{i}", [128, S], BF16) for i in range(4)]
        Ms = [kb.sb(es, f"Ms{i}", [128, WS], BF16) for i in range(4)]
        Mw = [kb.sb(es, f"Mw{i}", [128, WW], BF16) for i in range(4)]
        kb.op("pool", lambda e: e.memset(selm[:], 0.0), writes=[selm_b])
        kb.op("pool", lambda e: e.memset(pe_raw[:], 0.0), writes=[pe_raw_b])
        kb.op("pool", lambda e: e.memset(vs_aug[:, :, 128:130], 1.0), writes=[vs_aug_b])
        kb.op("pool", lambda e: e.memset(vw_aug[:, :, 128:130], 1.0), writes=[vw_aug_b])

        for g in range(2):
            kb.dma("sp", kcT_in[:], P.kvT_d[0, g], writes=[kcT_in_b])
            kb.dma("sp", ksT[:], P.kvT_d[2, g], writes=[ksT_b])
            kb.dma("sp", kwT[:], P.kvT_d[3, g], writes=[kwT_b])
            kb.dma("sp", vs_aug[:, :, 0:128], P.vtok_d[0, :, g * 128:(g + 1) * 128].rearrange("(kt p) d -> p kt d", p=128),
                   writes=[vs_aug_b])
            kb.dma("sp", vw_aug[:, :, 0:128], P.vtok_d[1, :, g * 128:(g + 1) * 128].rearrange("(kt p) d -> p kt d", p=128),
                   writes=[vw_aug_b])
            for j in range(4):
                h = g * 4 + j
                kb.dma("sp", Mc[j][0][:], P.Mc_d[h], writes=[Mc[j][1]])
                kb.dma("sp", Ms[j][0][:], P.Ms_d[h], writes=[Ms[j][1]])
                kb.dma("sp", Mw[j][0][:], P.Mw_d[h], writes=[Mw[j][1]])
            for kv in range(2):
                if kv == 1:
                    kb.dma("sp", kcT_in[:], P.kvT_d[1, g], writes=[kcT_in_b])
                w1n = "cmp_w1_k" if kv == 0 else "cmp_w1_v"
                w2n = "cmp_w2_k" if kv == 0 else "cmp_w2_v"
                pen = "cmp_pe_k" if kv == 0 else "cmp_pe_v"
                kb.dma("pool", w1[:], W[w1n][l].rearrange("(p d) h -> d p h", d=128), writes=[w1_b])
                kb.dma("pool", w2[:], W[w2n][l].rearrange("(c p) d -> p c d", p=128), writes=[w2_b])
                kb.dma("sp", pe_raw[0:32, :], W[pen][l], writes=[pe_raw_b])
                pps, pps_b = m_ps.next()
                kb.op("pe", lambda e, pps=pps: e.transpose(out=pps[:, 0:128], in_=pe_raw[:], identity=P.ident_f[:]),
                      reads=[pe_raw_b, P.ident_f_b], writes=[pps_b])
                kb.op("act", lambda e, pps=pps: e.copy(out=pe_t[:], in_=pps[:, 0:32]), reads=[pps_b], writes=[pe_b])
                for p in range(32):
                    src = kcT_in[:, p:p + 16 * 126 + 1:16]
                    kb.op("dve", lambda e, p=p, src=src: e.tensor_scalar(out=X[:, p, 0:127], in0=src, scalar1=pe_t[:, p:p + 1],
                                                                        scalar2=None, op0=ALU.add),
                          reads=[kcT_in_b, pe_b], writes=[X_b])
                for c in range(2):
                    ps, ps_b = m_ps.next()
                    for p in range(32):
                        kb.op("pe", lambda e, p=p, c=c, ps=ps: e.matmul(ps[:, 0:127], lhsT=w1[:, p, c * 128:(c + 1) * 128],
                                                                       rhs=X[:, p, 0:127], start=(p == 0), stop=(p == 31)),
                              reads=[w1_b, X_b], writes=[ps_b], sig=(p == 31))
                    (x_, x_b), (t_, t_b), (u_, u_b) = hid
                    kb.op("act", lambda e, ps=ps: e.copy(out=x_[:, 0:127], in_=ps[:, 0:127]), reads=[ps_b], writes=[x_b])
                    kb.op("dve", lambda e: e.tensor_tensor(out=t_[:, 0:127], in0=x_[:, 0:127], in1=x_[:, 0:127], op=ALU.mult),
                          reads=[x_b], writes=[t_b])
                    kb.op("dve", lambda e: e.tensor_scalar(out=t_[:, 0:127], in0=t_[:, 0:127], scalar1=0.044715, scalar2=1.0,
                                                           op0=ALU.mult, op1=ALU.add), reads=[t_b], writes=[t_b])
                    kb.op("dve", lambda e: e.tensor_tensor(out=t_[:, 0:127], in0=t_[:, 0:127], in1=x_[:, 0:127], op=ALU.mult),
                          reads=[t_b, x_b], writes=[t_b])
                    kb.op("act", lambda e: e.activation(out=u_[:, 0:127], in_=t_[:, 0:127], func=AF.Tanh,
                                                        scale=0.7978845608028654), reads=[t_b], writes=[u_b])
                    kb.op("dve", lambda e: e.tensor_scalar(out=u_[:, 0:127], in0=u_[:, 0:127], scalar1=1.0, scalar2=0.5,
                                                           op0=ALU.add, op1=ALU.mult), reads=[u_b], writes=[u_b])
                    kb.op("dve", lambda e, c=c: e.tensor_tensor(out=gT[c][0][:, 0:127], in0=u_[:, 0:127], in1=x_[:, 0:127],
                                                                op=ALU.mult), reads=[u_b, x_b], writes=[gT[c][1]])
                ps, ps_b = m_ps.next()
                if kv == 0:
                    for c in range(2):
                        kb.op("pe", lambda e, c=c, ps=ps: e.matmul(ps[:, 0:127], lhsT=w2[:, c, :], rhs=gT[c][0][:, 0:127],
                                                                  start=(c == 0), stop=(c == 1)),
                              reads=[w2_b, gT[c][1]], writes=[ps_b], sig=(c == 1))
                    kb.op("pool", lambda e: e.memset(kcT[:], 0.0), writes=[kcT_b])
                    kb.op("act", lambda e, ps=ps: e.copy(out=kcT[:, 0:127], in_=ps[:, 0:127]), reads=[ps_b, kcT_b], writes=[kcT_b])
                else:
                    for c in range(2):
                        kb.op("pe", lambda e, c=c, ps=ps: e.matmul(ps[0:127, 0:128], lhsT=gT[c][0][:, 0:127], rhs=w2[:, c, :],
                                                                  start=(c == 0), stop=(c == 1)),
                              reads=[w2_b, gT[c][1]], writes=[ps_b], sig=(c == 1))
                    kb.op("pool", lambda e: e.memset(vc_aug[:], 0.0), writes=[vc_aug_b])
                    kb.op("pool", lambda e: e.memset(vc_aug[:, 128:129], 1.0), reads=[vc_aug_b], writes=[vc_aug_b])
                    kb.op("act", lambda e, ps=ps: e.copy(out=vc_aug[0:127, 0:128], in_=ps[0:127, 0:128]),
                          reads=[ps_b, vc_aug_b], writes=[vc_aug_b])
                    kb.op("dve", lambda e: e.tensor_copy(out=vc_aug[:, 129:161], in_=P.cover[:]),
                          reads=[P.cover_b, vc_aug_b], writes=[vc_aug_b])
            if P.dbg_nsa is not None and g == 0:
                kb.dma("sp", P.dbg_nsa["kcT"], kcT[:], reads=[kcT_b])
                kb.dma("sp", P.dbg_nsa["vc"], vc_aug[:], reads=[vc_aug_b])

            for qt in range(4):
                t0 = qt * 512
                need_sel = (t0 + 511) // 64 >= 16
                def cmp_scores(j):
                    h = g * 4 + j
                    ps, ps_b = sc_ps.next()
                    kb.op("pe", lambda e: e.matmul(ps[:], lhsT=kcT[:], rhs=qT[:, h, t0:t0 + 512], start=True, stop=False),
                          reads=[kcT_b, qT_b], writes=[ps_b], sig=False)
                    kb.op("pe", lambda e: e.matmul(ps[:], lhsT=P.ident_b[:], rhs=Mc[j][0][:, t0:t0 + 512], start=False, stop=True),
                          reads=[P.ident_b_b, Mc[j][1]], writes=[ps_b])
                    ef, ef_b = Ef.next()
                    kb.op("act", lambda e: e.activation(out=ef[:], in_=ps[:], func=AF.Exp, scale=SCALE),
                          reads=[ps_b], writes=[ef_b])
                    return ef, ef_b

                def cmp_pv(j, ef, ef_b):
                    h = g * 4 + j
                    for sub in range(4):
                        op_, op_b = o_ps[sub]
                        st = qt * 4 + sub
                        kb.op("pe", lambda e, op_=op_, sub=sub: e.matmul(op_[:, 0:161], lhsT=ef[:, sub * 128:(sub + 1) * 128],
                                                                        rhs=vc_aug[:, 0:161], start=True, stop=True),
                              reads=[ef_b, vc_aug_b], writes=[op_b])
                        sm, sm_b = small.next()
                        kb.op("dve", lambda e, sm=sm, op_=op_: e.tensor_scalar(out=sm[:, 0:1], in0=op_[:, 128:129], scalar1=1e-30,
                                                                              scalar2=None, op0=ALU.max), reads=[op_b], writes=[sm_b])
                        kb.op("dve", lambda e, sm=sm: e.reciprocal(out=sm[:, 1:2], in_=sm[:, 0:1]), reads=[sm_b], writes=[sm_b])
                        kb.op("dve", lambda e, sm=sm, st=st: e.tensor_tensor(out=sm[:, 2:3], in0=sm[:, 1:2],
                                                                            in1=gates[:, st, h * 3:h * 3 + 1], op=ALU.mult),
                              reads=[sm_b, gates_b], writes=[sm_b])
                        kb.op("dve", lambda e, sm=sm, op_=op_, sub=sub: e.tensor_scalar(
                            out=acc[:, sub, j * 128:(j + 1) * 128], in0=op_[:, 0:128], scalar1=sm[:, 2:3], scalar2=None, op0=ALU.mult),
                            reads=[op_b, sm_b, acc_b], writes=[acc_b])
                        if not need_sel:
                            pass
                        elif j == 0:
                            kb.op("dve", lambda e, sm=sm, op_=op_, sub=sub: e.tensor_scalar(
                                out=imp[:, sub, :], in0=op_[:, 129:161], scalar1=sm[:, 1:2], scalar2=None, op0=ALU.mult),
                                reads=[op_b, sm_b, imp_b], writes=[imp_b])
                        else:
                            kb.op("dve", lambda e, sm=sm, op_=op_, sub=sub: e.scalar_tensor_tensor(
                                out=imp[:, sub, :], in0=op_[:, 129:161], scalar=sm[:, 1:2], in1=imp[:, sub, :],
                                op0=ALU.mult, op1=ALU.add), reads=[op_b, sm_b, imp_b], writes=[imp_b])

                cpend = None
                for j in range(4):
                    ef, ef_b = cmp_scores(j)
                    if cpend is not None:
                        cmp_pv(*cpend)
                    cpend = (j, ef, ef_b)
                cmp_pv(*cpend)
                if need_sel:
                    kb.op("dve", lambda e: e.tensor_tensor(out=score[:], in0=imp[:], in1=P.FB[:, qt * 4:qt * 4 + 4, :], op=ALU.add),
                          reads=[imp_b, P.FB_b], writes=[score_b])
                    sp_, sp_b = m_ps.next()
                    for sub in range(4):
                        kb.op("dve", lambda e, sub=sub: e.max(out=m8[:, sub, 0:8], in_=score[:, sub, :]), reads=[score_b, m8_b], writes=[m8_b])
                        kb.op("dve", lambda e, sub=sub: e.match_replace(out=wk[:, sub, :], in_to_replace=m8[:, sub, 0:8],
                                                                        in_values=score[:, sub, :], imm_value=-1e30),
                              reads=[score_b, m8_b, wk_b], writes=[wk_b])
                        kb.op("dve", lambda e, sub=sub: e.max(out=m8[:, sub, 8:16], in_=wk[:, sub, :]), reads=[wk_b, m8_b], writes=[m8_b])
                        kb.op("dve", lambda e, sub=sub: e.tensor_scalar(out=m8[:, sub, 15:16], in0=m8[:, sub, 15:16], scalar1=-1e29,
                                                                        scalar2=None, op0=ALU.max), reads=[m8_b], writes=[m8_b])
                        kb.op("dve", lambda e, sub=sub: e.tensor_scalar(out=selm[:, sub, 0:32], in0=score[:, sub, :], scalar1=m8[:, sub, 15:16],
                                                                        scalar2=1.0, op0=ALU.is_ge, op1=ALU.subtract),
                              reads=[score_b, m8_b, selm_b], writes=[selm_b])
                        kb.op("pe", lambda e, sub=sub: e.matmul(sp_[:, sub * 128:(sub + 1) * 128], lhsT=selm[:, sub, :], rhs=P.I30k[:],
                                                               start=True, stop=True),
                              reads=[selm_b, P.I30k_b], writes=[sp_b])
                    kb.op("act", lambda e: e.copy(out=selbT[:], in_=sp_[:]), reads=[sp_b], writes=[selbT_b])
                    if P.dbg_nsa is not None and g == 0:
                        kb.dma("sp", P.dbg_nsa["imp"][:, qt * 4:qt * 4 + 4, :], imp[:], reads=[imp_b])
                        kb.dma("sp", P.dbg_nsa["selbT"][:, t0:t0 + 512], selbT[0:32, :], reads=[selbT_b])

                def emit_scores(br, j, ki, kt):
                    h = g * 4 + j
                    dlt = t0 - kt * 128
                    ps, ps_b = sc_ps.next()
                    kT_, kT_b_ = (ksT, ksT_b) if br == 0 else (kwT, kwT_b)
                    const_bias = (br == 0 and dlt >= 256)
                    only_qk = (br == 0 and const_bias and not need_sel)
                    kb.op("pe", lambda e: e.matmul(ps[:], lhsT=kT_[:, kt * 128:(kt + 1) * 128], rhs=qT[:, h, t0:t0 + 512],
                                                   start=True, stop=only_qk),
                          reads=[kT_b_, qT_b], writes=[ps_b], sig=only_qk)
                    if br == 0 and not need_sel:
                        if not const_bias:
                            kb.op("pe", lambda e: e.matmul(ps[:], lhsT=P.ident_b[:], rhs=Ms[j][0][:, dlt + OFFS:dlt + OFFS + 512],
                                                           start=False, stop=True),
                                  reads=[P.ident_b_b, Ms[j][1]], writes=[ps_b])
                    elif br == 0:
                        kb.op("pe", lambda e: e.matmul(ps[:], lhsT=P.expall[:, kt * 128:(kt + 1) * 128], rhs=selbT[:],
                                                       start=False, stop=const_bias),
                              reads=[P.expall_b, selbT_b], writes=[ps_b], sig=const_bias)
                        if not const_bias:
                            kb.op("pe", lambda e: e.matmul(ps[:], lhsT=P.ident_b[:], rhs=Ms[j][0][:, dlt + OFFS:dlt + OFFS + 512],
                                                           start=False, stop=True),
                                  reads=[P.ident_b_b, Ms[j][1]], writes=[ps_b])
                    else:
                        kb.op("pe", lambda e: e.matmul(ps[:], lhsT=P.ident_b[:], rhs=Mw[j][0][:, dlt + OFFW:dlt + OFFW + 512],
                                                       start=False, stop=True),
                              reads=[P.ident_b_b, Mw[j][1]], writes=[ps_b])
                    eb, eb_b = Ebf.next()
                    if const_bias:
                        kb.op("act", lambda e: e.activation(out=eb[:], in_=ps[:], func=AF.Exp, scale=SCALE, bias=P.t31[:, h:h + 1]),
                              reads=[ps_b, P.t31_b], writes=[eb_b])
                    else:
                        kb.op("act", lambda e: e.activation(out=eb[:], in_=ps[:], func=AF.Exp, scale=SCALE),
                              reads=[ps_b], writes=[eb_b])
                    return eb, eb_b

                def emit_pv(br, j, ki, kt, nk, eb, eb_b):
                    h = g * 4 + j
                    v_, v_b_ = (vs_aug, vs_aug_b) if br == 0 else (vw_aug, vw_aug_b)
                    for sub in range(4):
                        op_, op_b = o_ps[sub]
                        kb.op("pe", lambda e, op_=op_, sub=sub: e.matmul(op_[:, 0:129], lhsT=eb[:, sub * 128:(sub + 1) * 128],
                                                                        rhs=v_[:, kt, 0:129], start=(ki == 0), stop=(ki == nk - 1)),
                              reads=[eb_b, v_b_], writes=[op_b], sig=(sub == 3))
                    if ki == nk - 1:
                        for sub in range(4):
                            op_, op_b = o_ps[sub]
                            st = qt * 4 + sub
                            sm, sm_b = small.next()
                            kb.op("dve", lambda e, sm=sm, op_=op_: e.reciprocal(out=sm[:, 1:2], in_=op_[:, 128:129]),
                                  reads=[op_b], writes=[sm_b])
                            kb.op("dve", lambda e, sm=sm, st=st: e.tensor_tensor(
                                out=sm[:, 2:3], in0=sm[:, 1:2], in1=gates[:, st, h * 3 + 1 + br:h * 3 + 2 + br], op=ALU.mult),
                                reads=[sm_b, gates_b], writes=[sm_b])
                            kb.op("dve", lambda e, sm=sm, op_=op_, sub=sub: e.scalar_tensor_tensor(
                                out=acc[:, sub, j * 128:(j + 1) * 128], in0=op_[:, 0:128], scalar=sm[:, 2:3],
                                in1=acc[:, sub, j * 128:(j + 1) * 128], op0=ALU.mult, op1=ALU.add),
                                reads=[op_b, sm_b, acc_b], writes=[acc_b])

                tiles = []
                for br in range(2):
                    for j in range(4):
                        if br == 0:
                            kts = list(range(0, (t0 + 511) // 128 + 1))
                        else:
                            kts = list(range(max(0, t0 // 128 - 4), t0 // 128 + 4))
                        for ki, kt in enumerate(kts):
                            tiles.append((br, j, ki, kt, len(kts)))
                pend = None
                for (br, j, ki, kt, nk) in tiles:
                    eb, eb_b = emit_scores(br, j, ki, kt)
                    if pend is not None:
                        emit_pv(*pend)
                    pend = (br, j, ki, kt, nk, eb, eb_b)
                emit_pv(*pend)
                for j in range(4):
                    h = g * 4 + j
                    tp, tp_b = m_ps.next()
                    for sub in range(4):
                        kb.op("pe", lambda e, tp=tp, sub=sub, j=j: e.transpose(out=tp[:, sub * 128:(sub + 1) * 128],
                                                                             in_=acc[:, sub, j * 128:(j + 1) * 128],
                                                                             identity=P.ident_f[:]),
                              reads=[acc_b, P.ident_f_b], writes=[tp_b], sig=(sub == 3))
                    og, og_b = ostg.next()
                    kb.op("act", lambda e, tp=tp, og=og: e.copy(out=og[:], in_=tp[:]), reads=[tp_b], writes=[og_b])
                    kb.dma("sp", P.oaT_d[h, :, t0:t0 + 512], og[:], reads=[og_b])
        kb.barrier()


def gdn_setup(P, kb, es):
    P.U_f, P.U_b = kb.sb(es, "U_f", [128, 128], F32)
    P.ones_f, P.ones_b = kb.sb(es, "ones_f", [128, 128], F32)
    P.mpos, P.mpos_b = kb.sb(es, "mpos", [128, 128], F32)
    P.mneg, P.mneg_b = kb.sb(es, "mneg", [128, 128], F32)
    kb.op("pool", lambda e: e.memset(P.ones_f[:], 1.0), writes=[P.ones_b])
    kb.op("pool", lambda e: e.memset(P.U_f[:], 1.0), writes=[P.U_b])
    kb.op("pool", lambda e: e.affine_select(out=P.U_f[:], in_=P.U_f[:], pattern=[[1, 128]], compare_op=ALU.is_ge, fill=0.0,
                                            base=0, channel_multiplier=-1), reads=[P.U_b], writes=[P.U_b])
    kb.op("pool", lambda e: e.memset(P.mpos[:], 0.0), writes=[P.mpos_b])
    kb.op("pool", lambda e: e.affine_select(out=P.mpos[:], in_=P.mpos[:], pattern=[[-1, 128]], compare_op=ALU.is_gt, fill=1e4,
                                            base=0, channel_multiplier=1), reads=[P.mpos_b], writes=[P.mpos_b])
    kb.op("pool", lambda e: e.memset(P.mneg[:], 0.0), writes=[P.mneg_b])
    kb.op("pool", lambda e: e.affine_select(out=P.mneg[:], in_=P.mneg[:], pattern=[[1, 128]], compare_op=ALU.is_ge, fill=-1e4,
                                            base=0, channel_multiplier=-1), reads=[P.mneg_b], writes=[P.mneg_b])


GDN_K = 4
GDN_TEST_HEADS = 1


def gdn_phase(P, kb, l):
    nc = P.nc
    W = P.W
    NCH = S // 128
    with ExitStack() as es:
        ab, ab_b = kb.sb(es, "ab", [128, NCH, 16], F32)
        kb.dma("sp", ab[:], P.ab_d.rearrange("(n p) c -> p n c", p=128), writes=[ab_b])
        dtb, dtb_b = kb.sb(es, "dtb", [128, 8], F32)
        nA, nA_b = kb.sb(es, "nA", [128, 8], F32)
        nw, nw_b = kb.sb(es, "nw", [128, 128], F32)
        mhalf, mhalf_b = kb.sb(es, "mhalf", [128, 1], F32)
        kb.op("pool", lambda e: e.memset(mhalf[:], -0.5), writes=[mhalf_b])
        kb.dma("sp", dtb[:], bass.AP(tensor=W["dt_bias"].tensor, offset=W["dt_bias"][l].offset, ap=[[0, 128], [1, 8]]), writes=[dtb_b])
        kb.dma("sp", nA[:], bass.AP(tensor=W["a_log"].tensor, offset=W["a_log"][l].offset, ap=[[0, 128], [1, 8]]), writes=[nA_b])
        kb.dma("sp", nw[:], bass.AP(tensor=W["gdn_norm_w"].tensor, offset=W["gdn_norm_w"][l].offset, ap=[[0, 128], [1, 128]]),
               writes=[nw_b])
        kb.op("act", lambda e: e.activation(out=nA[:], in_=nA[:], func=AF.Exp), reads=[nA_b], writes=[nA_b])
        kb.op("dve", lambda e: e.tensor_scalar(out=nA[:], in0=nA[:], scalar1=-1.0, scalar2=None, op0=ALU.mult), reads=[nA_b], writes=[nA_b])
        graw, graw_b = kb.sb(es, "graw", [128, NCH, 8], F32)
        beta, beta_b = kb.sb(es, "beta", [128, NCH, 8], F32)
        nbeta, nbeta_b = kb.sb(es, "nbeta", [128, NCH, 8], F32)
        gcol, gcol_b = kb.sb(es, "gcol", [128, NCH, 8], F32)
        eg, eg_b = kb.sb(es, "eg", [128, NCH, 8], F32)
        bg, bg_b = kb.sb(es, "bg", [128, NCH, 8], F32)
        for n in range(NCH):
            kb.op("dve", lambda e, n=n: e.tensor_tensor(out=graw[:, n, :], in0=ab[:, n, 0:8], in1=dtb[:], op=ALU.add),
                  reads=[ab_b, dtb_b, graw_b], writes=[graw_b])
        kb.op("act", lambda e: e.activation(out=graw[:], in_=graw[:], func=AF.Exp), reads=[graw_b], writes=[graw_b])
        kb.op("act", lambda e: e.activation(out=graw[:], in_=graw[:], func=AF.Ln, bias=P.one_t[:, 0:1]), reads=[graw_b, P.one_b], writes=[graw_b])
        for n in range(NCH):
            kb.op("dve", lambda e, n=n: e.tensor_tensor(out=graw[:, n, :], in0=graw[:, n, :], in1=nA[:], op=ALU.mult),
                  reads=[graw_b, nA_b], writes=[graw_b])
        kb.op("act", lambda e: e.activation(out=beta[:], in_=ab[:, :, 8:16], func=AF.Sigmoid), reads=[ab_b], writes=[beta_b])
        kb.op("dve", lambda e: e.tensor_scalar(out=nbeta[:], in0=beta[:], scalar1=-1.0, scalar2=None, op0=ALU.mult),
              reads=[beta_b], writes=[nbeta_b])
        bank = [kb.ps(es, f"gbank{i}", [128, 4, 128], F32) for i in range(6)]
        Q = Rot([(bank[i][0][:, 0, :], bank[i][1]) for i in range(6)])
        big = Rot([kb.ps(es, f"gbig{i}", [128, 512], F32) for i in range(2)])
        for n in range(NCH):
            pq, pq_b = Q.next()
            kb.op("pe", lambda e, n=n, pq=pq: e.matmul(pq[:, 0:8], lhsT=P.U_f[:], rhs=graw[:, n, :], start=True, stop=True),
                  reads=[P.U_b, graw_b], writes=[pq_b])
            kb.op("act", lambda e, n=n, pq=pq: e.copy(out=gcol[:, n, :], in_=pq[:, 0:8]), reads=[pq_b, gcol_b], writes=[gcol_b])
        kb.op("act", lambda e: e.activation(out=eg[:], in_=gcol[:], func=AF.Exp), reads=[gcol_b], writes=[eg_b])
        kb.op("dve", lambda e: e.tensor_tensor(out=bg[:], in0=eg[:], in1=beta[:], op=ALU.mult), reads=[eg_b, beta_b], writes=[bg_b])
        cw, cw_b = kb.sb(es, "cw", [128, 24, 4], F32)
        with ExitStack() as es_c:
            cwraw, cwraw_b = kb.sb(es_c, "cwraw", [128, 3072], F32)
            kb.op("pool", lambda e: e.memset(cwraw[:], 0.0), writes=[cwraw_b])
            kb.dma("sp", cwraw[0:4, :], W["conv_w"][l], reads=[cwraw_b], writes=[cwraw_b])
            for c in range(24):
                pq, pq_b = Q.next()
                kb.op("pe", lambda e, pq=pq, c=c: e.transpose(out=pq, in_=cwraw[:, c * 128:(c + 1) * 128], identity=P.ident_f[:]),
                      reads=[cwraw_b, P.ident_f_b], writes=[pq_b])
                kb.op("act", lambda e, pq=pq, c=c: e.copy(out=cw[:, c, :], in_=pq[:, 0:4]), reads=[pq_b, cw_b], writes=[cw_b])
            kb.barrier()
        xr1 = kb.sb(es, "xr0", [128, S], F32)
        xr = [xr1, xr1, xr1]
        sqt, sqt_b = kb.sb(es, "sqt", [128, S], F32)
        rn, rn_b = kb.sb(es, "rn", [128, 512], F32)

        def slot_bufs(k):
            B = {}
            B["xc"] = [kb.sb(es, f"xc{k}_{i}", [128, S], F32) for i in range(3)]
            B["tok"] = [kb.sb(es, f"tok{k}_{i}", [128, 128], F32) for i in range(3)]
            for nm, shp in [("vbk", [128, 256]), ("kd", [128, 128]), ("gd", [128, 128]), ("e1", [128, 128]),
                            ("e2", [128, 128]), ("qkt", [128, 128]), ("u", [128, 128]), ("nwk", [128, 128]), ("zt", [128, 128]),
                            ("xm", [128, 128]), ("yy", [128, 128]), ("junk", [128, 128])]:
                B[nm] = kb.sb(es, f"{nm}{k}", shp, F32)
            for nm in ["Pm", "PTm", "Rm", "St", "ztok"]:
                B[nm] = Rot([kb.sb(es, f"{nm}{k}_{i}", [128, 128], F32) for i in range(2)])
            B["sm"] = Rot([kb.sb(es, f"gsm{k}_{i}", [128, 8], F32) for i in range(4)])
            B["og"] = Rot([kb.sb(es, f"gost{k}_{i}", [128, 512], BF16) for i in range(2)])
            return B

        def head_gen(h, B):
            xc = B["xc"]
            tok = B["tok"]
            vbk, vbk_b = B["vbk"]; kd, kd_b = B["kd"]; gd, gd_b = B["gd"]
            e1, e1_b = B["e1"]; e2, e2_b = B["e2"]; qkt, qkt_b = B["qkt"]; u_, u_b = B["u"]; nwk, nwk_b = B["nwk"]
            zt, zt_b = B["zt"]; xm, xm_b = B["xm"]; yy, yy_b = B["yy"]; junk, junk_b = B["junk"]
            Pm, PTm, Rm, St, ztok, sm, ogr = B["Pm"], B["PTm"], B["Rm"], B["St"], B["ztok"], B["sm"], B["og"]
            for i in range(3):
                c = i * 8 + h
                kb.dma("sp", xr[i][0][:], P.gqkvT_d[c], writes=[xr[i][1]])
                x_, x_b = xr[i]
                y_, y_b = xc[i]
                kb.op("dve", lambda e, x_=x_, y_=y_, c=c: e.tensor_scalar(out=y_[:], in0=x_[:], scalar1=cw[:, c, 3:4], scalar2=None,
                                                                         op0=ALU.mult), reads=[x_b, cw_b], writes=[y_b])
                for sft in range(1, 4):
                    kb.op("dve", lambda e, x_=x_, y_=y_, c=c, sft=sft: e.scalar_tensor_tensor(
                        out=y_[:, sft:S], in0=x_[:, 0:S - sft], scalar=cw[:, c, 3 - sft:4 - sft], in1=y_[:, sft:S],
                        op0=ALU.mult, op1=ALU.add), reads=[x_b, cw_b, y_b], writes=[y_b])
                kb.op("act", lambda e, y_=y_: e.activation(out=y_[:], in_=y_[:], func=AF.Silu), reads=[y_b], writes=[y_b])
                if i < 2:
                    kb.op("act", lambda e, y_=y_: e.activation(out=sqt[:], in_=y_[:], func=AF.Square), reads=[y_b], writes=[sqt_b])
                    for t0 in range(0, S, 512):
                        bp, bp_b = big.next()
                        kb.op("pe", lambda e, bp=bp, t0=t0: e.matmul(bp[:], lhsT=P.ones_f[:], rhs=sqt[:, t0:t0 + 512], start=True, stop=True),
                              reads=[P.ones_b, sqt_b], writes=[bp_b])
                        kb.op("act", lambda e, bp=bp: e.activation(out=rn[:], in_=bp[:], func=AF.Sqrt, bias=P.eps_t[:, 0:1]),
                              reads=[bp_b, P.eps_b], writes=[rn_b])
                        kb.op("dve", lambda e: e.reciprocal(out=rn[:], in_=rn[:]), reads=[rn_b], writes=[rn_b])
                        if i == 0:
                            kb.op("dve", lambda e, y_=y_, t0=t0: e.scalar_tensor_tensor(
                                out=y_[:, t0:t0 + 512], in0=rn[:], scalar=SCALE, in1=y_[:, t0:t0 + 512], op0=ALU.mult, op1=ALU.mult),
                                reads=[rn_b, y_b], writes=[y_b])
                        else:
                            kb.op("dve", lambda e, y_=y_, t0=t0: e.tensor_tensor(out=y_[:, t0:t0 + 512], in0=rn[:], in1=y_[:, t0:t0 + 512],
                                                                               op=ALU.mult), reads=[rn_b, y_b], writes=[y_b])
            yield
            qTn, qTn_b = xc[0]
            kTn, kTn_b = xc[1]
            S_, S_b = St.next()
            kb.op("pool", lambda e, S_=S_: e.memset(S_[:], 0.0), writes=[S_b])
            og, og_b = None, None
            for n in range(NCH):
                cs = slice(n * 128, (n + 1) * 128)
                (q_tok, q_tok_b), (k_tok, k_tok_b), (v_tok, v_tok_b) = tok
                for i in range(3):
                    pq, pq_b = Q.next()
                    kb.op("pe", lambda e, pq=pq, i=i: e.transpose(out=pq, in_=xc[i][0][:, cs], identity=P.ident_f[:]),
                          reads=[xc[i][1], P.ident_f_b], writes=[pq_b])
                    if i == 0:
                        kb.op("act", lambda e, pq=pq: e.activation(out=tok[0][0][:], in_=pq, func=AF.Copy, scale=eg[:, n, h:h + 1]),
                              reads=[pq_b, eg_b], writes=[tok[0][1]])
                    if i == 1:
                        kb.op("act", lambda e, pq=pq: e.copy(out=tok[1][0][:], in_=pq), reads=[pq_b], writes=[tok[1][1]])
                    if i == 1:
                        kb.op("act", lambda e, pq=pq: e.activation(out=vbk[:, 128:256], in_=pq, func=AF.Copy, scale=bg[:, n, h:h + 1]),
                              reads=[pq_b, bg_b, vbk_b], writes=[vbk_b])
                    if i == 2:
                        kb.op("act", lambda e, pq=pq: e.activation(out=vbk[:, 0:128], in_=pq, func=AF.Copy, scale=beta[:, n, h:h + 1]),
                              reads=[pq_b, beta_b, vbk_b], writes=[vbk_b])
                yield
                kb.op("act", lambda e: e.activation(out=gd[:], in_=P.ident_f[:], func=AF.Copy, scale=gcol[:, n, h:h + 1]),
                      reads=[P.ident_f_b, gcol_b], writes=[gd_b])
                pg, pg_b = Q.next()
                kb.op("pe", lambda e, pg=pg: e.matmul(pg, lhsT=P.ones_f[:], rhs=gd[:], start=True, stop=True),
                      reads=[P.ones_b, gd_b], writes=[pg_b])
                kb.op("dve", lambda e, pg=pg: e.scalar_tensor_tensor(out=e1[:], in0=pg, scalar=gcol[:, n, h:h + 1], in1=P.mpos[:],
                                                                    op0=ALU.subtract, op1=ALU.max),
                      reads=[pg_b, gcol_b, P.mpos_b], writes=[e1_b])
                kb.op("dve", lambda e, pg=pg: e.scalar_tensor_tensor(out=e2[:], in0=pg, scalar=gcol[:, n, h:h + 1], in1=P.mneg[:],
                                                                    op0=ALU.subtract, op1=ALU.min),
                      reads=[pg_b, gcol_b, P.mneg_b], writes=[e2_b])
                s1, s1_b = sm.next()
                kb.op("dve", lambda e, pg=pg, s1=s1: e.tensor_copy(out=s1[:, 0:1], in_=pg[:, 127:128]), reads=[pg_b], writes=[s1_b])
                kb.op("act", lambda e: e.activation(out=e1[:], in_=e1[:], func=AF.Exp, scale=-1.0), reads=[e1_b], writes=[e1_b])
                kb.op("act", lambda e: e.activation(out=e2[:], in_=e2[:], func=AF.Exp), reads=[e2_b], writes=[e2_b])
                kb.op("act", lambda e, s1=s1: e.activation(out=s1[:, 1:2], in_=gcol[:, n, h:h + 1], func=AF.Exp, scale=-1.0, bias=s1[:, 0:1]),
                      reads=[s1_b, gcol_b], writes=[s1_b])
                kb.op("act", lambda e, s1=s1: e.activation(out=s1[:, 2:3], in_=s1[:, 0:1], func=AF.Exp), reads=[s1_b], writes=[s1_b])
                kb.op("act", lambda e, s1=s1: e.activation(out=kd[:], in_=k_tok[:], func=AF.Copy, scale=s1[:, 1:2]),
                      reads=[k_tok_b, s1_b], writes=[kd_b])
                yield
                pk, pk_b = Q.next()
                kb.op("pe", lambda e, pk=pk: e.matmul(pk, lhsT=kTn[:, cs], rhs=kTn[:, cs], start=True, stop=True),
                      reads=[kTn_b], writes=[pk_b])
                P0, P0_b = Pm.next()
                kb.op("dve", lambda e, pk=pk, P0=P0: e.scalar_tensor_tensor(out=P0[:], in0=pk, scalar=nbeta[:, n, h:h + 1], in1=e1[:],
                                                                           op0=ALU.mult, op1=ALU.mult),
                      reads=[pk_b, nbeta_b, e1_b], writes=[P0_b])
                pk2, pk2_b = Q.next()
                kb.op("pe", lambda e, pk2=pk2: e.matmul(pk2, lhsT=kTn[:, cs], rhs=qTn[:, cs], start=True, stop=True),
                      reads=[kTn_b, qTn_b], writes=[pk2_b])
                kb.op("dve", lambda e, pk2=pk2: e.tensor_tensor(out=qkt[:], in0=pk2, in1=e2[:], op=ALU.mult),
                      reads=[pk2_b, e2_b], writes=[qkt_b])
                pt_, pt_b = Q.next()
                kb.op("pe", lambda e, pt_=pt_, P0=P0: e.transpose(out=pt_, in_=P0[:], identity=P.ident_f[:]),
                      reads=[P0_b, P.ident_f_b], writes=[pt_b])
                PT0, PT0_b = PTm.next()
                R0, R0_b = Rm.next()
                kb.op("dve", lambda e, pt_=pt_, PT0=PT0: e.tensor_copy(out=PT0[:], in_=pt_), reads=[pt_b], writes=[PT0_b])
                kb.op("dve", lambda e, pt_=pt_, R0=R0: e.tensor_tensor(out=R0[:], in0=pt_, in1=P.ident_f[:], op=ALU.add),
                      reads=[pt_b, P.ident_f_b], writes=[R0_b])
                yield
                Pc, Pc_b, PTc, PTc_b, Rc, Rc_b = P0, P0_b, PT0, PT0_b, R0, R0_b
                for lvl in range(1, 7):
                    pa_, pa_b = Q.next()
                    kb.op("pe", lambda e, pa_=pa_, Pc=Pc, PTc=PTc: e.matmul(pa_, lhsT=PTc[:], rhs=Pc[:], start=True, stop=True),
                          reads=[Pc_b, PTc_b], writes=[pa_b])
                    Pn, Pn_b = Pm.next()
                    kb.op("act", lambda e, pa_=pa_, Pn=Pn: e.copy(out=Pn[:], in_=pa_), reads=[pa_b], writes=[Pn_b])
                    if lvl < 6:
                        pb_, pb_b = Q.next()
                        kb.op("pe", lambda e, pb_=pb_, Pc=Pc, PTc=PTc: e.matmul(pb_, lhsT=Pc[:], rhs=PTc[:], start=True, stop=True),
                              reads=[Pc_b, PTc_b], writes=[pb_b])
                        PTn, PTn_b = PTm.next()
                        kb.op("dve", lambda e, pb_=pb_, PTn=PTn: e.tensor_copy(out=PTn[:], in_=pb_), reads=[pb_b], writes=[PTn_b])
                    yield
                    pr_, pr_b = Q.next()
                    kb.op("pe", lambda e, pr_=pr_, Pn=Pn, Rc=Rc: e.matmul(pr_, lhsT=Pn[:], rhs=Rc[:], start=True, stop=True),
                          reads=[Pn_b, Rc_b], writes=[pr_b])
                    Rn, Rn_b = Rm.next()
                    kb.op("dve", lambda e, pr_=pr_, Rn=Rn, Rc=Rc: e.tensor_tensor(out=Rn[:], in0=pr_, in1=Rc[:], op=ALU.add),
                          reads=[pr_b, Rc_b], writes=[Rn_b])
                    Rc, Rc_b = Rn, Rn_b
                    if lvl < 6:
                        Pc, Pc_b, PTc, PTc_b = Pn, Pn_b, PTn, PTn_b
                    yield
                TT, TT_b = Rc, Rc_b
                bp, bp_b = big.next()
                kb.op("pe", lambda e, bp=bp, TT=TT: e.matmul(bp[:, 0:256], lhsT=TT[:], rhs=vbk[:], start=True, stop=True),
                      reads=[TT_b, vbk_b], writes=[bp_b])
                kb.op("act", lambda e, bp=bp: e.copy(out=u_[:], in_=bp[:, 0:128]), reads=[bp_b], writes=[u_b])
                kb.op("act", lambda e, bp=bp: e.activation(out=nwk[:], in_=bp[:, 128:256], func=AF.Copy, scale=-1.0),
                      reads=[bp_b], writes=[nwk_b])
                yield
                pz, pz_b = Q.next()
                kb.op("pe", lambda e, pz=pz: e.matmul(pz, lhsT=q_tok[:], rhs=P.ident_f[:], start=True, stop=False),
                      reads=[q_tok_b, P.ident_f_b], writes=[pz_b])
                kb.op("pe", lambda e, pz=pz: e.matmul(pz, lhsT=nwk[:], rhs=qkt[:], start=False, stop=True),
                      reads=[nwk_b, qkt_b], writes=[pz_b])
                kb.op("act", lambda e, pz=pz: e.copy(out=zt[:], in_=pz), reads=[pz_b], writes=[zt_b])
                px, px_b = Q.next()
                kb.op("pe", lambda e, px=px: e.matmul(px, lhsT=nwk[:], rhs=kd[:], start=True, stop=True),
                      reads=[nwk_b, kd_b], writes=[px_b])
                kb.op("dve", lambda e, px=px, s1=s1: e.scalar_tensor_tensor(out=xm[:], in0=P.ident_f[:], scalar=s1[:, 2:3], in1=px,
                                                                           op0=ALU.mult, op1=ALU.add),
                      reads=[px_b, s1_b, P.ident_f_b], writes=[xm_b])
                yield
                po, po_b = Q.next()
                kb.op("pe", lambda e, po=po: e.matmul(po, lhsT=qkt[:], rhs=u_[:], start=True, stop=False),
                      reads=[qkt_b, u_b], writes=[po_b])
                kb.op("pe", lambda e, po=po, S_=S_: e.matmul(po, lhsT=zt[:], rhs=S_[:], start=False, stop=True),
                      reads=[zt_b, S_b], writes=[po_b])
                pS, pS_b = Q.next()
                kb.op("pe", lambda e, pS=pS: e.matmul(pS, lhsT=kd[:], rhs=u_[:], start=True, stop=False),
                      reads=[kd_b, u_b], writes=[pS_b])
                kb.op("pe", lambda e, pS=pS, S_=S_: e.matmul(pS, lhsT=xm[:], rhs=S_[:], start=False, stop=True),
                      reads=[xm_b, S_b], writes=[pS_b])
                Sn, Sn_b = St.next()
                kb.op("act", lambda e, pS=pS, Sn=Sn: e.copy(out=Sn[:], in_=pS), reads=[pS_b], writes=[Sn_b])
                S_, S_b = Sn, Sn_b
                zk, zk_b = ztok.next()
                kb.dma("act", zk[:], P.z_d[n * 128:(n + 1) * 128, h * 128:(h + 1) * 128], writes=[zk_b])
                s2, s2_b = sm.next()
                kb.op("act", lambda e, po=po, s2=s2: e.activation(out=junk[:], in_=po, func=AF.Square, accum_out=s2[:, 0:1]),
                      reads=[po_b], writes=[junk_b, s2_b])
                kb.op("act", lambda e, s2=s2: e.activation(out=s2[:, 1:2], in_=s2[:, 0:1], func=AF.Sqrt, scale=1.0 / 128, bias=P.eps_t[:, 0:1]),
                      reads=[s2_b, P.eps_b], writes=[s2_b])
                kb.op("dve", lambda e, s2=s2: e.reciprocal(out=s2[:, 2:3], in_=s2[:, 1:2]), reads=[s2_b], writes=[s2_b])
                kb.op("dve", lambda e, po=po, s2=s2: e.scalar_tensor_tensor(out=yy[:], in0=po, scalar=s2[:, 2:3], in1=zk[:],
                                                                           op0=ALU.mult, op1=ALU.mult),
                      reads=[po_b, s2_b, zk_b], writes=[yy_b])
                yield
                pT, pT_b = Q.next()
                kb.op("pe", lambda e, pT=pT: e.transpose(out=pT, in_=yy[:], identity=P.ident_f[:]),
                      reads=[yy_b, P.ident_f_b], writes=[pT_b])
                if n % 4 == 0:
                    og, og_b = ogr.next()
                kb.op("act", lambda e, pT=pT, og=og, n=n: e.copy(out=og[:, (n % 4) * 128:(n % 4 + 1) * 128], in_=pT),
                      reads=[pT_b, og_b], writes=[og_b])
                if n % 4 == 3:
                    kb.dma("sp", P.obT_d[h, :, (n - 3) * 128:(n + 1) * 128], og[:], reads=[og_b])
                yield

        slots = [slot_bufs(k) for k in range(GDN_K)]
        heads = list(range(P.gdn_heads))
        for r0 in range(0, len(heads), GDN_K):
            gens = [head_gen(h, slots[k]) for k, h in enumerate(heads[r0:r0 + GDN_K])]
            while gens:
                for g_ in list(gens):
                    try:
                        next(g_)
                    except StopIteration:
                        gens.remove(g_)
        kb.barrier()


def merge_phase(P, kb, l):
    W = P.W
    with ExitStack() as es:
        oaT, oaT_b = kb.sb(es, "oaT", [128, 8, S], BF16)
        obT, obT_b = kb.sb(es, "obT", [128, 8, S], BF16)
        for c in range(8):
            kb.dma("sp", oaT[:, c, :], P.oaT_d[c], writes=[oaT_b])
            kb.dma("sp", obT[:, c, :], P.obT_d[c], writes=[obT_b])
        pa_p = Rot([kb.sb(es, f"pap{i}", [128, 8, 512], BF16) for i in range(2)])
        pb_p = Rot([kb.sb(es, f"pbp{i}", [128, 8, 512], BF16) for i in range(2)])
        psA = Rot([kb.ps(es, f"psA{i}", [128, 512], F32) for i in range(3)])
        psB = Rot([kb.ps(es, f"psB{i}", [128, 512], F32) for i in range(3)])
        gA = Rot([kb.sb(es, f"gA{i}", [128, 512], F32) for i in range(2)])
        gB = Rot([kb.sb(es, f"gB{i}", [128, 512], F32) for i in range(2)])
        t1 = Rot([kb.sb(es, f"t1{i}", [128, 512], F32) for i in range(2)])
        t2 = Rot([kb.sb(es, f"t2{i}", [128, 512], F32) for i in range(2)])
        uo = Rot([kb.sb(es, f"uo{i}", [128, 512], BF16) for i in range(3)])
        Wa = W["w_pa"][l].rearrange("(kc p) n -> p kc n", p=128)
        Wb = W["w_pb"][l].rearrange("(kc p) n -> p kc n", p=128)
        for p0 in range(0, D, 512):
            wa, wa_b = pa_p.next()
            wb, wb_b = pb_p.next()
            kb.dma("pool", wa[:], Wa[:, :, p0:p0 + 512], writes=[wa_b])
            kb.dma("pool", wb[:], Wb[:, :, p0:p0 + 512], writes=[wb_b])
            for cb in range(0, 512, 128):
                fb = (p0 + cb) // 128
                for t0 in range(0, S, 512):
                    pA, pA_b = psA.next()
                    pB, pB_b = psB.next()
                    for kc in range(8):
                        kb.op("pe", lambda e, kc=kc, pA=pA: e.matmul(pA[:], lhsT=wa[:, kc, cb:cb + 128], rhs=oaT[:, kc, t0:t0 + 512],
                                                                    start=(kc == 0), stop=(kc == 7)),
                              reads=[wa_b, oaT_b], writes=[pA_b], sig=(kc == 7))
                    for kc in range(8):
                        kb.op("pe", lambda e, kc=kc, pB=pB: e.matmul(pB[:], lhsT=wb[:, kc, cb:cb + 128], rhs=obT[:, kc, t0:t0 + 512],
                                                                    start=(kc == 0), stop=(kc == 7)),
                              reads=[wb_b, obT_b], writes=[pB_b], sig=(kc == 7))
                    ga, ga_b = gA.next()
                    gb_, gb_b = gB.next()
                    kb.dma("act", ga[:], P.mergeT_d[fb, :, t0:t0 + 512], writes=[ga_b])
                    kb.dma("act", gb_[:], P.mergeT_d[16 + fb, :, t0:t0 + 512], writes=[gb_b])
                    a1, a1_b = t1.next()
                    a2, a2_b = t2.next()
                    kb.op("dve", lambda e, pA=pA, ga=ga, a1=a1: e.tensor_tensor(out=a1[:], in0=pA[:], in1=ga[:], op=ALU.mult),
                          reads=[pA_b, ga_b], writes=[a1_b])
                    kb.op("dve", lambda e, pB=pB, gb_=gb_, a2=a2: e.tensor_tensor(out=a2[:], in0=pB[:], in1=gb_[:], op=ALU.mult),
                          reads=[pB_b, gb_b], writes=[a2_b])
                    uu, uu_b = uo.next()
                    kb.op("dve", lambda e, a1=a1, a2=a2, uu=uu: e.tensor_tensor(out=uu[:], in0=a1[:], in1=a2[:], op=ALU.add),
                          reads=[a1_b, a2_b], writes=[uu_b])
                    kb.dma("sp", P.uT_d[fb, :, t0:t0 + 512], uu[:], reads=[uu_b])
        kb.barrier()


def load_T(P, kb, dst, dst_b, src_d, nchunks, t0, ntok):
    for c in range(nchunks):
        kb.dma("sp", dst[:, c, 0:ntok], src_d[c, :, t0:t0 + ntok], writes=[dst_b])


def residual_epi(P, kb, es, src_ap, dst_ap, tok_off):
    rt = Rot([kb.sb(es, f"rt{i}", [128, 512], F32) for i in range(3)])
    ro = Rot([kb.sb(es, f"ro{i}", [128, 512], F32) for i in range(3)])

    def epi(ps, ps_b, j0, pw, t0, tw):
        r_, r_b = rt.next()
        o_, o_b = ro.next()
        r0 = tok_off + t0
        kb.dma("act", r_[:, 0:pw], src_ap[r0:r0 + 128, j0:j0 + pw], writes=[r_b])
        kb.op("dve", lambda e: e.tensor_tensor(out=o_[:, 0:pw], in0=ps[:, 0:pw], in1=r_[:, 0:pw], op=ALU.add),
              reads=[ps_b, r_b], writes=[o_b])
        kb.dma("sp", dst_ap[r0:r0 + 128, j0:j0 + pw], o_[:, 0:pw], reads=[o_b])
    return epi


def wo_phase(P, kb, l, x_src, x_dst):
    with ExitStack() as es:
        uT, uT_b = kb.sb(es, "uT", [128, 16, S], BF16)
        load_T(P, kb, uT, uT_b, P.uT_d, 16, 0, S)
        epi = residual_epi(P, kb, es, x_src, x_dst, 0)
        P.dense(kb, uT, uT_b, 16, S, P.W["w_o"][l], [(0, D, "a", epi)])


def mlp_phase(P, kb, l, x1, x2):
    W = P.W
    TT_ = 1024
    NQ = 4
    HQ = DFF // NQ
    with ExitStack() as es:
        panels = Rot([kb.sb(es, f"mwp{i}", [128, 16, 512], BF16) for i in range(3)])
        pss = Rot([kb.ps(es, f"mlps{i}", [128, 512], F32) for i in range(4)])
        actT, actT_b = kb.sb(es, "actT", [128, 16, TT_], BF16)
        h2T, h2T_b = kb.sb(es, "h2T", [128, 16, TT_], BF16)
        nres = P.norm_res(kb, es, W["ln2_w"][l], npts=3)
        rl = Rot([kb.sb(es, f"rl{i}", [128, 512], F32) for i in range(3)])
        rt = Rot([kb.sb(es, f"rt{i}", [128, 512], F32) for i in range(3)])
        ro = Rot([kb.sb(es, f"ro{i}", [128, 512], F32) for i in range(3)])
        xb = {}

        def epi_up(ps, ps_b, j0, cw, t0, tw):
            r_, r_b = rl.next()
            kb.op("act", lambda e: e.activation(out=r_[0:cw, 0:tw], in_=ps[0:cw, 0:tw], func=AF.Relu),
                  reads=[ps_b], writes=[r_b])
            kb.op("dve", lambda e: e.tensor_tensor(out=actT[0:cw, j0 // 128, t0:t0 + tw], in0=r_[0:cw, 0:tw],
                                                   in1=r_[0:cw, 0:tw], op=ALU.mult),
                  reads=[r_b, actT_b], writes=[actT_b])

        for tt in range(S // TT_):
            tok0 = tt * TT_
            P.norm_T(kb, x1, W["ln2_w"][l], h2T, h2T_b, TT_, tok0=tok0, res=nres)
            for qd in range(NQ):
                src = x1 if qd == 0 else x2

                def epi_dn(ps, ps_b, j0, pw, t0, tw, src=src):
                    r_, r_b = rt.next()
                    o_, o_b = ro.next()
                    r0 = tok0 + t0
                    key = (r0, j0)
                    if key not in xb:
                        xb[key] = Buf(f"x2_{r0}_{j0}")
                    kb.dma("act", r_[:, 0:pw], src[r0:r0 + 128, j0:j0 + pw], reads=[xb[key]], writes=[r_b])
                    kb.op("dve", lambda e: e.tensor_tensor(out=o_[:, 0:pw], in0=ps[:, 0:pw], in1=r_[:, 0:pw], op=ALU.add),
                          reads=[ps_b, r_b], writes=[o_b])
                    kb.dma("sp", x2[r0:r0 + 128, j0:j0 + pw], o_[:, 0:pw], reads=[o_b], writes=[xb[key]])

                P.dense_core(kb, h2T, h2T_b, 16, TT_, W["w_up"][l][:, qd * HQ:(qd + 1) * HQ], [(0, HQ, "b", epi_up)], panels, pss)
                P.dense_core(kb, actT, actT_b, 16, TT_, W["w_down"][l][qd * HQ:(qd + 1) * HQ, :], [(0, D, "a", epi_dn)], panels, pss)
        kb.barrier()


def final_norm(P, kb, x_src, lnw_ap, out_ap):
    with ExitStack() as es:
        lnw, lnw_b = kb.sb(es, "flnw", [128, D], F32)
        kb.dma("sp", lnw[:], bass.AP(tensor=lnw_ap.tensor, offset=lnw_ap.offset, ap=[[0, 128], [1, D]]), writes=[lnw_b])
        xts = Rot([kb.sb(es, f"fxt{i}", [128, D], F32) for i in range(2)])
        ots = Rot([kb.sb(es, f"fot{i}", [128, D], F32) for i in range(2)])
        junk, junk_b = kb.sb(es, "fjunk", [128, D], BF16)
        sts = Rot([kb.sb(es, f"fst{i}", [128, 4], F32) for i in range(2)])
        for tt in range(S // 128):
            xt, xt_b = xts.next()
            ot, ot_b = ots.next()
            st, st_b = sts.next()
            kb.dma("sp", xt[:], x_src[tt * 128:(tt + 1) * 128, :], writes=[xt_b])
            kb.op("act", lambda e: e.activation(out=junk[:], in_=xt[:], func=AF.Square, accum_out=st[:, 0:1]),
                  reads=[xt_b], writes=[junk_b, st_b])
            kb.op("act", lambda e: e.activation(out=st[:, 1:2], in_=st[:, 0:1], func=AF.Sqrt, scale=1.0 / D, bias=P.eps_t[:, 0:1]),
                  reads=[st_b, P.eps_b], writes=[st_b])
            kb.op("dve", lambda e: e.reciprocal(out=st[:, 2:3], in_=st[:, 1:2]), reads=[st_b], writes=[st_b])
            kb.op("dve", lambda e: e.scalar_tensor_tensor(out=ot[:], in0=xt[:], scalar=st[:, 2:3], in1=lnw[:], op0=ALU.mult, op1=ALU.mult),
                  reads=[xt_b, st_b, lnw_b], writes=[ot_b])
            kb.dma("sp", out_ap[tt * 128:(tt + 1) * 128, :], ot[:], reads=[ot_b])
        kb.barrier()
```
